# Optimizing a Trainium2 kernel written in Bass

```python
import math
import jax, jax.numpy as jnp
from jax import lax
import numpy as np

D_MODEL = 2048
BATCH = 4
SEQ = 4096
DEPTH = 2

PLE_DIM = 256
SG_HEADS = 4
SG_HEAD_DIM = 128
SG_WIDTH = SG_HEADS * SG_HEAD_DIM
SG_CHUNK = 128
SC_GROUPS = 4
SC_GROUP_DIM = 128
SC_WIDTH = SC_GROUPS * SC_GROUP_DIM
SC_KERNEL = 3
GDN_HEADS = 8
GDN_HEAD_DIM = 128
GDN_WIDTH = GDN_HEADS * GDN_HEAD_DIM
GDN_CONV = 4
GDN_CHUNK = 64

MIX_WIDTH = SG_WIDTH + SC_WIDTH + GDN_WIDTH
D_FF = 4 * D_MODEL
EPS = 1e-6

IN_SIZES = (2 * SG_WIDTH, 3 * SC_WIDTH, 3 * GDN_WIDTH, GDN_WIDTH, GDN_HEADS, GDN_HEADS)
IN_COLS = int(sum(IN_SIZES))
IN_SPLITS = [int(s) for s in np.cumsum(IN_SIZES)[:-1]]

kernel_name = "hybrid_sgu_shortconv_gdn_block"


def rmsnorm(x, g):
    x32 = x.astype(jnp.float32)
    y = x32 * lax.rsqrt(jnp.mean(x32 * x32, axis=-1, keepdims=True) + EPS)
    return (y * g).astype(x.dtype)


def group_rmsnorm(x, g, n_groups):
    shp = x.shape
    xg = x.reshape(shp[:-1] + (n_groups, shp[-1] // n_groups))
    y = rmsnorm(xg, g.reshape(n_groups, shp[-1] // n_groups))
    return y.reshape(shp)


def layernorm(x, g, b):
    x32 = x.astype(jnp.float32)
    mu = jnp.mean(x32, axis=-1, keepdims=True)
    var = jnp.mean(jnp.square(x32 - mu), axis=-1, keepdims=True)
    return ((x32 - mu) * lax.rsqrt(var + EPS) * g + b).astype(x.dtype)


def causal_dwconv(x, w):
    K = w.shape[0]
    S = x.shape[1]
    xp = jnp.pad(x, ((0, 0), (K - 1, 0), (0, 0)))
    y = xp[:, 0:S] * w[0]
    for j in range(1, K):
        y = y + xp[:, j:j + S] * w[j]
    return y


def spatial_gating(u, v, ln_g, ln_b, w_s, b_s):
    Bsz, S, _ = v.shape
    nc = S // SG_CHUNK
    vh = v.reshape(Bsz, nc, SG_CHUNK, SG_HEADS, SG_HEAD_DIM)
    vh = layernorm(vh, ln_g.reshape(SG_HEADS, SG_HEAD_DIM), ln_b.reshape(SG_HEADS, SG_HEAD_DIM))
    w_causal = jnp.tril(w_s)
    f = jnp.einsum('hts,bnshd->bnthd', w_causal, vh) + jnp.transpose(b_s)[None, None, :, :, None]
    return u * f.reshape(Bsz, S, SG_WIDTH)


def gated_delta_chunked(q, k, v, g, beta):
    Bsz, S, H, Dk = q.shape
    Dv = v.shape[-1]
    L = GDN_CHUNK
    nc = S // L

    def chunk4(t):
        return t.reshape(Bsz, nc, L, H, t.shape[-1]).transpose(0, 3, 1, 2, 4)

    def chunk3(t):
        return t.reshape(Bsz, nc, L, H).transpose(0, 3, 1, 2)

    q = chunk4(q.astype(jnp.float32)) * (Dk ** -0.5)
    k = chunk4(k.astype(jnp.float32))
    v = chunk4(v.astype(jnp.float32))
    gc = jnp.cumsum(chunk3(g.astype(jnp.float32)), axis=-1)
    beta = chunk3(beta.astype(jnp.float32))

    incl = jnp.tril(jnp.ones((L, L), dtype=bool))
    strict = jnp.tril(jnp.ones((L, L), dtype=bool), -1)
    diff = gc[..., :, None] - gc[..., None, :]
    decay_incl = jnp.where(incl, jnp.exp(jnp.where(incl, diff, 0.0)), 0.0)
    decay_strict = jnp.where(strict, decay_incl, 0.0)

    kb = k * beta[..., None]
    vb = v * beta[..., None]
    A = jnp.einsum('bhnid,bhnjd->bhnij', kb, k) * decay_strict
    eye = jnp.eye(L, dtype=jnp.float32)
    rhs = jnp.concatenate([vb, kb * jnp.exp(gc)[..., None]], axis=-1)
    sol = lax.linalg.triangular_solve(A + eye, rhs, left_side=True, lower=True,
                                      transpose_a=False, conjugate_a=False, unit_diagonal=True)
    value, kcd = sol[..., :Dv], sol[..., Dv:]

    intra = jnp.einsum('bhnid,bhnjd->bhnij', q, k) * decay_incl
    q_exp = q * jnp.exp(gc)[..., None]
    k_tail = k * jnp.exp(gc[..., -1:] - gc)[..., None]
    g_last = jnp.exp(gc[..., -1])

    def step(state, inp):
        qe, kc, val, intra_c, kt, gl = inp
        v_new = val - jnp.einsum('bhld,bhdv->bhlv', kc, state)
        o = jnp.einsum('bhld,bhdv->bhlv', qe, state) + jnp.einsum('bhij,bhjv->bhiv', intra_c, v_new)
        state = state * gl[..., None, None] + jnp.einsum('bhld,bhlv->bhdv', kt, v_new)
        return state, o

    xs = (jnp.moveaxis(q_exp, 2, 0), jnp.moveaxis(kcd, 2, 0), jnp.moveaxis(value, 2, 0),
          jnp.moveaxis(intra, 2, 0), jnp.moveaxis(k_tail, 2, 0), jnp.moveaxis(g_last, 2, 0))
    s0 = jnp.zeros((Bsz, H, Dk, Dv), jnp.float32)
    _, o = lax.scan(step, s0, xs)
    return o.transpose(1, 0, 3, 2, 4).reshape(Bsz, S, H, Dv)


def l2norm(x):
    x32 = x.astype(jnp.float32)
    return x32 * lax.rsqrt(jnp.sum(x32 * x32, axis=-1, keepdims=True) + EPS)


def hybrid_layer(h, p_i, norm_mix, w_in, sg_ln_g, sg_ln_b, sg_w, sg_b, sc_conv, gdn_conv,
                 gdn_a_log, gdn_dt_bias, gdn_norm, out_norm_a, out_norm_b, w_o, norm_ffn,
                 w_ff1, w_ff2, norm_ple, w_ple_gate, w_ple_proj):
    Bsz, S, _ = h.shape
    xn = rmsnorm(h, norm_mix)
    proj = xn @ w_in
    sg_uv, sc_bcx, gdn_qkv, gdn_z, gdn_a, gdn_b = jnp.split(proj, IN_SPLITS, axis=-1)

    u, v = jnp.split(jax.nn.gelu(sg_uv), 2, axis=-1)
    ya = spatial_gating(u, v, sg_ln_g, sg_ln_b, sg_w, sg_b)
    ya = group_rmsnorm(ya, out_norm_a, SG_HEADS)

    gb, gc_, xin = jnp.split(sc_bcx, 3, axis=-1)
    yb = gb * causal_dwconv(gc_ * xin, sc_conv)
    yb = group_rmsnorm(yb, out_norm_b, SC_GROUPS)

    qkv = jax.nn.silu(causal_dwconv(gdn_qkv, gdn_conv))
    q, k, vv = jnp.split(qkv, 3, axis=-1)
    q = l2norm(q.reshape(Bsz, S, GDN_HEADS, GDN_HEAD_DIM))
    k = l2norm(k.reshape(Bsz, S, GDN_HEADS, GDN_HEAD_DIM))
    vv = vv.reshape(Bsz, S, GDN_HEADS, GDN_HEAD_DIM)
    g = -jnp.exp(gdn_a_log.astype(jnp.float32)) * jax.nn.softplus(
        gdn_a.astype(jnp.float32) + gdn_dt_bias.astype(jnp.float32))
    beta = jax.nn.sigmoid(gdn_b.astype(jnp.float32))
    o = gated_delta_chunked(q, k, vv, g, beta)
    z = gdn_z.reshape(Bsz, S, GDN_HEADS, GDN_HEAD_DIM).astype(jnp.float32)
    yc = (rmsnorm(o, gdn_norm) * jax.nn.silu(z)).reshape(Bsz, S, GDN_WIDTH).astype(h.dtype)

    h = h + jnp.concatenate([ya, yb, yc], axis=-1) @ w_o

    hn = rmsnorm(h, norm_ffn)
    h = h + jnp.square(jax.nn.relu(hn @ w_ff1)) @ w_ff2

    hn = rmsnorm(h, norm_ple)
    h = h + (p_i @ w_ple_proj) * jax.nn.sigmoid(hn @ w_ple_gate)
    return h


def setup_inputs(seed: int = 0) -> dict:
    key = jax.random.key(seed)
    ks = jax.random.split(key, 24)

    def nrm(k, shape, scale):
        return jax.random.normal(k, shape, jnp.float32) * scale

    def gain(k, w):
        return 1.0 + nrm(k, (DEPTH, w), 0.02)

    dt = jnp.exp(jax.random.uniform(ks[11], (DEPTH, GDN_HEADS), jnp.float32,
                                    math.log(1e-3), math.log(1e-1)))
    return {
        "x": nrm(ks[0], (BATCH, SEQ, D_MODEL), 1.0),
        "p": nrm(ks[1], (DEPTH, BATCH, SEQ, PLE_DIM), 1.0),
        "norm_mix": gain(ks[2], D_MODEL),
        "w_in": nrm(ks[3], (DEPTH, D_MODEL, IN_COLS), D_MODEL ** -0.5),
        "sg_ln_g": gain(ks[4], SG_WIDTH),
        "sg_ln_b": nrm(ks[5], (DEPTH, SG_WIDTH), 0.02),
        "sg_w": nrm(ks[6], (DEPTH, SG_HEADS, SG_CHUNK, SG_CHUNK), 0.5 * SG_CHUNK ** -0.5),
        "sg_b": 1.0 + nrm(ks[7], (DEPTH, SG_HEADS, SG_CHUNK), 0.1),
        "sc_conv": nrm(ks[8], (DEPTH, SC_KERNEL, SC_WIDTH), SC_KERNEL ** -0.5),
        "gdn_conv": nrm(ks[9], (DEPTH, GDN_CONV, 3 * GDN_WIDTH), GDN_CONV ** -0.5),
        "gdn_a_log": jnp.log(jax.random.uniform(ks[10], (DEPTH, GDN_HEADS), jnp.float32, 1.0, 16.0)),
        "gdn_dt_bias": dt + jnp.log(-jnp.expm1(-dt)),
        "gdn_norm": gain(ks[12], GDN_HEAD_DIM),
        "out_norm_a": gain(ks[13], SG_WIDTH),
        "out_norm_b": gain(ks[14], SC_WIDTH),
        "w_o": nrm(ks[15], (DEPTH, MIX_WIDTH, D_MODEL), MIX_WIDTH ** -0.5),
        "norm_ffn": gain(ks[16], D_MODEL),
        "w_ff1": nrm(ks[17], (DEPTH, D_MODEL, D_FF), D_MODEL ** -0.5),
        "w_ff2": nrm(ks[18], (DEPTH, D_FF, D_MODEL), D_FF ** -0.5),
        "norm_ple": gain(ks[19], D_MODEL),
        "w_ple_gate": nrm(ks[20], (DEPTH, D_MODEL, D_MODEL), D_MODEL ** -0.5),
        "w_ple_proj": nrm(ks[21], (DEPTH, PLE_DIM, D_MODEL), PLE_DIM ** -0.5),
        "norm_final": 1.0 + nrm(ks[22], (D_MODEL,), 0.02),
    }


def reference(x, p, norm_mix, w_in, sg_ln_g, sg_ln_b, sg_w, sg_b, sc_conv, gdn_conv,
              gdn_a_log, gdn_dt_bias, gdn_norm, out_norm_a, out_norm_b, w_o, norm_ffn,
              w_ff1, w_ff2, norm_ple, w_ple_gate, w_ple_proj, norm_final):
    h = x
    for i in range(DEPTH):
        h = hybrid_layer(h, p[i], norm_mix[i], w_in[i], sg_ln_g[i], sg_ln_b[i], sg_w[i], sg_b[i],
                         sc_conv[i], gdn_conv[i], gdn_a_log[i], gdn_dt_bias[i], gdn_norm[i],
                         out_norm_a[i], out_norm_b[i], w_o[i], norm_ffn[i], w_ff1[i], w_ff2[i],
                         norm_ple[i], w_ple_gate[i], w_ple_proj[i])
    return rmsnorm(h, norm_final)
```

```python
import numpy as np
from contextlib import ExitStack
import concourse.bass as bass
import concourse.mybir as mybir
from concourse.bass_utils import run_bass_kernel_spmd

F32 = mybir.dt.float32
BF16 = mybir.dt.bfloat16
AF = mybir.ActivationFunctionType
ALU = mybir.AluOpType
AX = mybir.AxisListType

D = 2048
SEQ = 4096
TT = 512
EPS = 1e-6
IN_COLS = 6672


class TL:
    def __init__(self, sem, unit, name):
        self.sem, self.unit, self.count, self.name = sem, unit, 0, name


class Buf:
    def __init__(self, name):
        self.name, self.w, self.r = name, None, {}


class Prog:
    ENGS = ("pe", "act", "dve", "pool", "sp")

    def __init__(self, nc, stack):
        self.nc, self.stack = nc, stack
        self.ops = {e: [] for e in self.ENGS}
        self.tl = {e: TL(stack.enter_context(nc.semaphore("tl_" + e)), 1, e) for e in self.ENGS}
        self.seen = {e: {} for e in self.ENGS}
        self.nops = 0

    def dma_tl(self, name):
        return TL(self.stack.enter_context(self.nc.semaphore("d_" + name)), 16, name)

    def _deps(self, eng, reads, writes):
        deps = {}
        me = self.tl[eng]

        def add(t):
            if t is None:
                return
            tl, c = t
            if tl is me and eng == "pe":
                return
            if deps.get(tl, 0) < c:
                deps[tl] = c

        for b in reads:
            add(b.w)
        for b in writes:
            add(b.w)
            for tl, c in b.r.items():
                add((tl, c))
        waits = []
        seen = self.seen[eng]
        for tl, c in deps.items():
            if seen.get(tl, 0) < c:
                seen[tl] = c
                waits.append((tl.sem, c * tl.unit))
        return waits

    def op(self, eng, fn, reads=(), writes=(), dma=None):
        waits = self._deps(eng, reads, writes)
        tl = dma if dma is not None else self.tl[eng]
        tl.count += 1
        cnt = tl.count
        sem, unit = tl.sem, tl.unit

        def run(e, fn=fn, waits=waits, sem=sem, unit=unit):
            for s, v in waits:
                e.wait_ge(s, v)
            fn(e).then_inc(sem, unit)

        self.ops[eng].append(run)
        self.nops += 1
        for b in reads:
            b.r[tl] = cnt
        for b in writes:
            b.w = (tl, cnt)
            b.r = {}

    def wait_all(self, eng, bufs):
        waits = self._deps(eng, (), bufs)

        def run(e, waits=waits):
            for s, v in waits:
                e.wait_ge(s, v)

        self.ops[eng].append(run)

    def emit(self):
        with self.nc.Block() as block:
            @block.tensor
            def _(e):
                for f in self.ops["pe"]:
                    f(e)

            @block.scalar
            def _(e):
                for f in self.ops["act"]:
                    f(e)

            @block.vector
            def _(e):
                for f in self.ops["dve"]:
                    f(e)

            @block.gpsimd
            def _(e):
                for f in self.ops["pool"]:
                    f(e)

            @block.sync
            def _(e):
                for f in self.ops["sp"]:
                    f(e)


class Ring:
    def __init__(self, aps, name):
        self.aps = aps
        self.bufs = [Buf("%s%d" % (name, i)) for i in range(len(aps))]
        self.i = 0

    def next(self):
        i = self.i
        self.i = (i + 1) % len(self.aps)
        return self.aps[i], self.bufs[i]


def sp_layout(NL):
    off = {}
    o = 0

    def add(name, n):
        nonlocal o
        off[name] = o
        o += n

    for l in range(NL):
        add("g_mix%d" % l, 16)
        add("g_ffn%d" % l, 16)
        add("g_ple%d" % l, 16)
        add("g_a%d" % l, 4)
        add("g_b%d" % l, 4)
        add("g_gdn%d" % l, 1)
        add("scw%d" % l, 12)
        add("gcw%d" % l, 96)
        add("alog%d" % l, 8)
        add("dtb%d" % l, 8)
        add("lng%d" % l, 512)
        add("lnb%d" % l, 512)
    add("g_fin", 16)
    return off, o


def build(NT=8, NL=2, dumps=()):
    nc = bass.Bass("TRN2", target_bir_lowering=False)
    SPO, NSP = sp_layout(NL)
    dt_in = lambda name, shape: nc.dram_tensor(name, shape, F32, kind="ExternalInput").ap()
    x_d = dt_in("x", [SEQ, D])
    p_d = dt_in("p", [2, SEQ, 256])
    win_d = dt_in("w_in", [2, D, IN_COLS])
    wab_d = dt_in("w_ab", [2, D, 16])
    wo_d = dt_in("w_o", [2, D, D])
    w1_d = dt_in("w_ff1", [2, D, 4 * D])
    w2_d = dt_in("w_ff2", [2, 4 * D, D])
    wpg_d = dt_in("w_ple_gate", [2, D, D])
    wpp_d = dt_in("w_ple_proj", [2, 256, D])
    sp_d = dt_in("sp", [128, NSP])
    sgw_d = dt_in("sgwT", [128, 8, 128])
    sgb_d = dt_in("sgb", [1, 1024])
    out_d = nc.dram_tensor("out", [SEQ, D], F32, kind="ExternalOutput").ap()

    with ExitStack() as st:
        P = Prog(nc, st)
        sbt = lambda name, shape, dt: st.enter_context(nc.sbuf_tensor(name, shape, dt))
        pst_ = lambda name, shape, dt: st.enter_context(nc.psum_tensor(name, shape, dt))

        hT = sbt("hT", [128, 16, TT], F32)
        BhT = [Buf("hT%d" % i) for i in range(16)]
        xnT = sbt("xnT", [128, 16, TT], BF16)
        BxnT = [Buf("xnT%d" % i) for i in range(16)]
        yT = sbt("yT", [128, 16, TT], BF16)
        ByT = [Buf("yT%d" % i) for i in range(16)]
        stage = yT[:].rearrange("p a b -> p (a b)").bitcast(F32).rearrange("p (j c) -> p j c", j=2)
        NWB = 2
        wbuf = sbt("wbuf", [128, NWB, 8192], BF16)
        Bw = [Buf("w%d" % i) for i in range(NWB)]
        Dw = [P.dma_tl("w%d" % i) for i in range(NWB)]
        spk = sbt("spk", [128, NSP], F32)
        Bsp = Buf("spk")
        cst = sbt("cst", [128, 6, 128], F32)
        Bc = Buf("cst")
        cbias = sbt("cbias", [128, 4], F32)
        identb = sbt("identb", [128, 128], BF16)
        onesb = sbt("onesb", [128, 128], BF16)
        Bcb = Buf("cstb")
        wcT = sbt("wcT", [128, 8, 128], BF16)
        bsrow = sbt("bsrow", [1, 1024], BF16)
        Bsg = Buf("sg")
        wab = sbt("wab", [128, 2, 16, 16], BF16)
        Bwab = Buf("wab")
        gsc = sbt("gsc", [128, 2, 8], F32)
        Bgsc = Buf("gsc")
        nexpa = sbt("nexpa", [128, 2, 8], F32)
        Sst = sbt("Sst", [128, 2, 8, 128], F32)
        Sbf = sbt("Sbf", [128, 2, 8, 128], BF16)
        BS = [[Buf("S%d_%d" % (l, hh)) for hh in range(2)] for l in range(2)]
        BSb = [[Buf("Sb%d_%d" % (l, hh)) for hh in range(2)] for l in range(2)]
        ghalo = sbt("ghalo", [128, 2, 24, 3], F32)
        Bgh = [[Buf("gh%d_%d" % (l, c)) for c in range(24)] for l in range(2)]
        shalo = sbt("shalo", [128, 2, 4, 2], F32)
        Bsh = [[Buf("sh%d_%d" % (l, c)) for c in range(4)] for l in range(2)]
        scrf_t = sbt("scrf", [128, 6, TT], F32)
        SF = Ring([scrf_t[:, i, :] for i in range(6)], "scrf")
        sgst = scrf_t[:, 0:2, :].rearrange("p a (b c) -> p (a b) c", c=128)
        Bsgst = SF.bufs[0]
        SF.bufs[1] = SF.bufs[0]
        bsst = scrf_t[0:1, 2:4, :].rearrange("p a b -> p (a b)")
        Bbsst = SF.bufs[2]
        SF.bufs[3] = SF.bufs[2]
        scrb_t = sbt("scrb", [128, 4, TT], BF16)
        SB = Ring([scrb_t[:, i, :] for i in range(4)], "scrb")
        pre_t = sbt("pre", [128, 2, TT + 4], F32)
        PRE = Ring([pre_t[:, i, :] for i in range(2)], "pre")
        qT = sbt("qT", [128, 4, TT], BF16); BqT = [Buf("qT%d" % i) for i in range(4)]
        kT = sbt("kT", [128, 4, TT], BF16); BkT = [Buf("kT%d" % i) for i in range(4)]
        qT2 = sbt("qT2", [128, 4, TT], BF16); BqT2 = [Buf("qT2%d" % i) for i in range(4)]
        kT2 = sbt("kT2", [128, 4, TT], BF16); BkT2 = [Buf("kT2%d" % i) for i in range(4)]
        vT2 = sbt("vT2", [128, 4, TT], BF16); BvT2 = [Buf("vT2%d" % i) for i in range(4)]
        zsT2 = sbt("zsT2", [128, 4, TT], BF16); BzsT2 = [Buf("zsT2%d" % i) for i in range(4)]
        uT = qT; BuT = BqT
        vn = kT; Bvn = BkT
        gbT = uT; BgbT = BuT
        gcT = vn; BgcT = Bvn
        vT = sbt("vT", [128, 4, TT], BF16); BvT = [Buf("vT%d" % i) for i in range(4)]
        zsT = sbt("zsT", [128, 4, TT], BF16); BzsT = [Buf("zsT%d" % i) for i in range(4)]
        Dp = [P.dma_tl("p%d" % i) for i in range(2)]
        pT = sbt("pT", [128, 2, TT], BF16); BpT = Buf("pT")
        sm = sbt("sm", [128, 12, 4, 8], F32)
        Bsm = [Buf("sm%d" % i) for i in range(12)]
        G_, BETA, NEGB, EGC, EGL, ETAIL, BEGA, T1, T2, GC, GL, T3 = range(12)
        smps = sbt("smps", [128, 4, 16], F32); Bsmps = Buf("smps")
        lnst = sbt("lnst", [128, 8, 4], F32); Blnst = Buf("lnst")
        mk = lambda name, dt: (sbt(name, [128, 4, 128], dt), Buf(name))
        Em, BEm = mk("Em", F32)
        qeT, BqeT = mk("qeT", BF16)
        kbe, Bkbe = mk("kbe", BF16)
        ktl, Bktl = mk("ktl", BF16)
        vb, Bvb = mk("vb", BF16)
        Wm, BWm = mk("Wm", F32)
        Rbf, BRbf = mk("Rbf", BF16)
        intra, Bintra = mk("intra", BF16)
        intraT, BintraT = mk("intraT", BF16)
        Pm = [mk("Pm%d" % i, F32) for i in range(2)]
        PTm = [mk("PTm%d" % i, F32) for i in range(2)]
        Rm = [mk("Rm%d" % i, F32) for i in range(2)]
        nkcd, Bnkcd = mk("nkcd", BF16)
        vnew, Bvnew = mk("vnew", BF16)
        psum = [pst_("ps%d" % i, [128, 512], F32) for i in range(8)]
        Bps = [Buf("ps%d" % i) for i in range(8)]
        psb = [psum[i][:].bitcast(BF16) for i in range(8)]
        psi = [0]

        def PS():
            i = psi[0]
            psi[0] = (i + 1) % 8
            return i

        Dx = [P.dma_tl("x%d" % j) for j in range(2)]
        Do = [P.dma_tl("o%d" % j) for j in range(2)]
        Dsetup = P.dma_tl("setup")

        dump_d = {}

        def DUMP(name, ap, shape, dt, bufs):
            if name not in dumps or name in dump_d:
                return
            d = nc.dram_tensor("dbg_" + name, shape, dt, kind="ExternalOutput").ap()
            dump_d[name] = d
            P.op("sp", lambda e: e.dma_start(out=d, in_=ap), reads=bufs, dma=P.dma_tl("dbg_" + name))

        def ACT(out, in_, func, reads, writes, **kw):
            P.op("act", lambda e: e.activation(out=out, in_=in_, func=func, **kw), reads, writes)

        def TTo(out, in0, in1, op, reads, writes, eng="dve"):
            P.op(eng, lambda e: e.tensor_tensor(out=out, in0=in0, in1=in1, op=op), reads, writes)

        def TS(out, in0, s1, s2, op0, op1, reads, writes, eng="dve"):
            P.op(eng, lambda e: e.tensor_scalar(out=out, in0=in0, scalar1=s1, scalar2=s2, op0=op0, op1=op1), reads, writes)

        def TS1(out, in_, s, op, reads, writes, eng="dve"):
            P.op(eng, lambda e: e.tensor_single_scalar(out=out, in_=in_, scalar=s, op=op), reads, writes)

        def STT(out, in0, s, in1, op0, op1, reads, writes, eng="dve"):
            P.op(eng, lambda e: e.scalar_tensor_tensor(out=out, in0=in0, scalar=s, in1=in1, op0=op0, op1=op1), reads, writes)

        def CP(out, in_, reads, writes, eng="dve"):
            if eng == "act":
                P.op("act", lambda e: e.copy(out=out, in_=in_), reads, writes)
            else:
                P.op(eng, lambda e: e.tensor_copy(out=out, in_=in_), reads, writes)

        def MM(groups, reads, writes):
            def f(e):
                ins = None
                for out, pairs in groups:
                    n = len(pairs)
                    for i, (l, r) in enumerate(pairs):
                        ins = e.matmul(out, lhsT=l, rhs=r, start=(i == 0), stop=(i == n - 1))
                return ins
            P.op("pe", f, reads, writes)

        def TR(items, reads, writes):
            def f(e):
                ins = None
                for o, i, idn in items:
                    ins = e.transpose(o, i, idn)
                return ins
            P.op("pe", f, reads, writes)

        def bc(ap, shape):
            return ap.to_broadcast(shape)

        identf = cst[:, 0, :]
        onesf = cst[:, 1, :]
        negonesf = cst[:, 2, :]
        Uf = cst[:, 3, :]
        maskneg = cst[:, 4, :]
        SLm = cst[:, 5, :]

        P.op("sp", lambda e: e.dma_start(out=spk[:], in_=sp_d), writes=[Bsp], dma=Dsetup)
        P.op("sp", lambda e: e.dma_start(out=sgst[:], in_=sgw_d), writes=[Bsgst], dma=P.dma_tl("sgst"))
        P.op("sp", lambda e: e.dma_start(out=bsst[:], in_=sgb_d), writes=[Bbsst], dma=P.dma_tl("bsst"))
        P.op("pool", lambda e: e.dma_start(out=wab[:].rearrange("p l k c -> p (l k) c"),
                                           in_=wab_d.rearrange("l (k p) c -> p (l k) c", p=128)),
             writes=[Bwab], dma=P.dma_tl("wab"))
        def cmem(i, v):
            P.op("pool", lambda e: e.memset(cst[:, i, :], v), writes=[Bc])

        def csel(i, pat, op, fill, cm):
            P.op("pool", lambda e: e.affine_select(out=cst[:, i, :], in_=cst[:, i, :], pattern=[[pat, 128]], compare_op=op,
                                                   fill=fill, base=0, channel_multiplier=cm), reads=[Bc], writes=[Bc])
        cmem(0, 1.0); csel(0, -1, ALU.is_equal, 0.0, 1)
        cmem(1, 1.0)
        cmem(2, -1.0)
        cmem(3, 1.0); csel(3, 1, ALU.is_ge, 0.0, -1)
        cmem(4, 0.0); csel(4, -1, ALU.is_ge, -30000.0, 1)
        cmem(5, 1.0); csel(5, -1, ALU.is_gt, 0.0, 1)
        P.op("dve", lambda e: e.memset(cbias[:, 0:1], EPS), writes=[Bcb])
        P.op("dve", lambda e: e.memset(cbias[:, 1:2], 128.0 * EPS), writes=[Bcb])
        P.op("dve", lambda e: e.memset(cbias[:, 2:3], 1.0), writes=[Bcb])
        CP(identb[:], identf, [Bc], [Bcb])
        CP(onesb[:], onesf, [Bc], [Bcb])
        TTo(wcT[:], sgst, bc(cst[:, 3:4, :], [128, 8, 128]), ALU.mult, [Bsgst, Bc], [Bsg])
        CP(bsrow[:], bsst, [Bbsst], [Bsg])
        SF.bufs[1] = Buf("scrf1")
        SF.bufs[3] = Buf("scrf3")
        P.op("dve", lambda e: e.memset(Sst[:], 0.0), writes=[b for r in BS for b in r])
        P.op("dve", lambda e: e.memset(Sbf[:], 0.0), writes=[b for r in BSb for b in r])
        P.op("dve", lambda e: e.memset(ghalo[:], 0.0), writes=[b for r in Bgh for b in r])
        P.op("dve", lambda e: e.memset(shalo[:], 0.0), writes=[b for r in Bsh for b in r])
        for l in range(NL):
            TS1(gsc[:, l, :], spk[:, SPO["g_a%d" % l]:SPO["g_a%d" % l] + 8], float(np.sqrt(128.0)), ALU.mult, [Bsp], [Bgsc])
            ACT(nexpa[:, l, :], spk[:, SPO["alog%d" % l]:SPO["alog%d" % l] + 8], AF.Exp, [Bsp], [Bgsc])
            TS1(nexpa[:, l, :], nexpa[:, l, :], -1.0, ALU.mult, [Bgsc], [Bgsc])

        sched = []

        def blk(W, l, r0, c0):
            return W[l, r0:r0 + D, c0:c0 + 512].rearrange("(k p) c -> p k c", p=128)

        v16 = lambda s: wbuf[:, s, :].rearrange("p (k c) -> p k c", k=16)
        v2 = lambda s: wbuf[:, s, 0:4096].rearrange("p (k c) -> p k c", k=2)
        INB = {"u": 0, "v": 1, "gb": 2, "gc": 3, "xin": 4, "q0": 5, "q1": 6, "k0": 7, "k1": 8, "v0": 9, "v1": 10, "z0": 11, "z1": 12}
        for t in range(NT):
            for l in range(NL):
                for key in ("q0", "k0", "v0", "z0", "q1", "k1", "v1", "z1", "u", "v", "gb", "gc", "xin"):
                    sched.append((key, blk(win_d, l, 0, INB[key] * 512), v16))
                for cb in range(4):
                    sched.append(("wo%d" % cb, blk(wo_d, l, 0, cb * 512), v16))
                for q in range(4):
                    for j in range(4):
                        sched.append(("w1_%d_%d" % (q, j), blk(w1_d, l, 0, q * 2048 + j * 512), v16))
                    for cb in range(4):
                        sched.append(("w2_%d_%d" % (q, cb), blk(w2_d, l, q * 2048, cb * 512), v16))
                for cb in range(4):
                    sched.append(("pg%d" % cb, blk(wpg_d, l, 0, cb * 512), v16))
                sched.append(("pp", wpp_d[l].rearrange("(k p) c -> p k c", p=128), v2))
        wstate = {"issued": 0, "next": 0}
        NBT = len(sched) // NT
        wcache = nc.dram_tensor("wcache", [NBT, 128, 8192], BF16).ap()
        Bcache = [Buf("wc%d" % i) for i in range(NBT)]
        Dst = [P.dma_tl("wst%d" % i) for i in range(NWB)]

        def WNEXT(key):
            i = wstate["next"]
            assert sched[i][0] == key, (sched[i][0], key)
            while wstate["issued"] < min(len(sched), i + NWB):
                j = wstate["issued"]
                s = j % NWB
                cid = j % NBT
                _, src, vf = sched[j]
                if j < NBT or NT == 1:
                    P.op("pool", lambda e, s=s, src=src, vf=vf: e.dma_start(out=vf(s), in_=src), writes=[Bw[s]], dma=Dw[s])
                    if NT > 1:
                        P.op("sp", lambda e, s=s, cid=cid: e.dma_start(out=wcache[cid], in_=wbuf[:, s, :]),
                             reads=[Bw[s]], writes=[Bcache[cid]], dma=Dst[s])
                else:
                    P.op("pool", lambda e, s=s, cid=cid: e.dma_start(out=wbuf[:, s, :], in_=wcache[cid]),
                         reads=[Bcache[cid]], writes=[Bw[s]], dma=Dw[s])
                wstate["issued"] += 1
            wstate["next"] += 1
            s = i % NWB
            return sched[i][2](s), Bw[s]

        pending = []

        def defer(th):
            pending.append(th)

        def tick(n=1):
            for _ in range(n):
                if pending:
                    pending.pop(0)()

        def flush():
            while pending:
                pending.pop(0)()

        def proj_fm(key, src, Bsrc, nk=16):
            wv, bw = WNEXT(key)
            banks = []
            for m in range(4):
                b = PS()
                MM([(psum[b][:], [(wv[:, kc, m * 128:(m + 1) * 128], src[:, kc, :]) for kc in range(nk)])],
                   [bw] + Bsrc, [Bps[b]])
                banks.append(b)
                tick()
            return banks

        def rstd_from_ss(b, scale, n=TT):
            r, Br = SF.next()
            ACT(r[:, 0:n], psum[b][:, 0:n], AF.Ln, [Bps[b], Bcb], [Br], scale=scale, bias=cbias[:, 0:1])
            ACT(r[:, 0:n], r[:, 0:n], AF.Exp, [Br], [Br], scale=-0.5)
            return r, Br

        def rmsnorm_to(dst, Bdst, gcol, src_bufs_extra=()):
            b = PS()
            for kc in range(16):
                s, Bs_ = SB.next()
                ACT(s, hT[:, kc, :], AF.Square, [BhT[kc]], [Bs_])
                P.op("pe", lambda e, s=s, kc=kc, b=b: e.matmul(psum[b][:], lhsT=onesb[:], rhs=s, start=(kc == 0), stop=(kc == 15)),
                     [Bs_, Bcb], [Bps[b]])
            r, Br = rstd_from_ss(b, 1.0 / D)
            DUMP("rstd", r, [128, TT], F32, [Br])
            for kc in range(16):
                STT(dst[:, kc, :], hT[:, kc, :], spk[:, gcol + kc:gcol + kc + 1], r, ALU.mult, ALU.mult,
                    [BhT[kc], Bsp, Br], [Bdst[kc]])

        def group_norm_out(raw, Braw, gain_ap, Bgain, dst, Bdst_list, extra_in1=None, Bextra=()):
            s, Bs_ = SB.next()
            ACT(s, raw, AF.Square, [Braw], [Bs_])

            def partB():
                b = PS()
                MM([(psum[b][:], [(onesb[:], s)])], [Bs_, Bcb], [Bps[b]])
                r, Br = SF.next()
                ACT(r, psum[b][:], AF.Ln, [Bps[b], Bcb], [Br], scale=1.0, bias=cbias[:, 1:2])
                ACT(r, r, AF.Exp, [Br], [Br], scale=-0.5)
                STT(dst, raw, gain_ap, r, ALU.mult, ALU.mult, [Braw, Bgain, Br], Bdst_list)
            defer(partB)

        def load_x(t):
            for tc in range(4):
                j = tc % 2
                r0 = t * TT + tc * 128
                P.op("sp", lambda e, j=j, r0=r0: e.dma_start(out=stage[:, j, :], in_=x_d[r0:r0 + 128, :]),
                     writes=ByT[8 * j:8 * j + 8], dma=Dx[j])
                for g in range(4):
                    b = PS()
                    TR([(psum[b][:, k * 128:(k + 1) * 128], stage[:, j, (4 * g + k) * 128:(4 * g + k + 1) * 128], identf)
                        for k in range(4)], ByT[8 * j:8 * j + 8] + [Bc], [Bps[b]])
                    CP(hT[:, 4 * g:4 * g + 4, tc * 128:(tc + 1) * 128], psum[b][:].rearrange("p (k c) -> p k c", k=4),
                       [Bps[b]], BhT[4 * g:4 * g + 4], eng=("act" if g % 2 == 0 else "dve"))

        def store_out(t):
            b = PS()
            for kc in range(16):
                s, Bs_ = SB.next()
                ACT(s, hT[:, kc, :], AF.Square, [BhT[kc]], [Bs_])
                P.op("pe", lambda e, s=s, kc=kc, b=b: e.matmul(psum[b][:], lhsT=onesb[:], rhs=s, start=(kc == 0), stop=(kc == 15)),
                     [Bs_, Bcb], [Bps[b]])
            r, Br = rstd_from_ss(b, 1.0 / D)
            gcol = SPO["g_fin"]
            for kc in range(16):
                STT(hT[:, kc, :], hT[:, kc, :], spk[:, gcol + kc:gcol + kc + 1], r, ALU.mult, ALU.mult,
                    [BhT[kc], Bsp, Br], [BhT[kc]])
            for tc in range(4):
                j = tc % 2
                r0 = t * TT + tc * 128
                for g in range(4):
                    b = PS()
                    TR([(psum[b][:, k * 128:(k + 1) * 128], hT[:, 4 * g + k, tc * 128:(tc + 1) * 128], identf)
                        for k in range(4)], BhT[4 * g:4 * g + 4] + [Bc], [Bps[b]])
                    CP(stage[:, j, g * 512:(g + 1) * 512], psum[b][:], [Bps[b]], ByT[8 * j + 2 * g:8 * j + 2 * g + 2],
                       eng=("act" if g % 2 == 0 else "dve"))
                P.op("sp", lambda e, j=j, r0=r0: e.dma_start(out=out_d[r0:r0 + 128, :], in_=stage[:, j, :]),
                     reads=ByT[8 * j:8 * j + 8], dma=Do[j])

        def sgu(l):
            for h, b in proj_units("u"):
                ACT(uT[:, h, :], psum[b][:], AF.Gelu, [Bps[b]], [BuT[h]])
                yield
            wv, bw = WNEXT("v")
            lng = spk[:, SPO["lng%d" % l]:SPO["lng%d" % l] + 512]
            lnb = spk[:, SPO["lnb%d" % l]:SPO["lnb%d" % l] + 512]
            for tc in range(4):
                b = PS()
                MM([(psum[b][:], [(xnT[:, kc, tc * 128:(tc + 1) * 128], wv[:, kc, :]) for kc in range(16)])],
                   [bw] + BxnT, [Bps[b]])
                tick()
                vf, Bvf = SF.next()
                ACT(vf, psum[b][:], AF.Gelu, [Bps[b]], [Bvf])
                sq, Bsq = SF.next()
                ACT(sq, vf, AF.Square, [Bvf], [Bsq])
                v3 = vf.rearrange("p (h d) -> p h d", h=4)
                P.op("dve", lambda e, v3=v3: e.tensor_reduce(out=lnst[:, 0, :], in_=v3, axis=AX.X, op=ALU.add), [Bvf], [Blnst])
                P.op("dve", lambda e, sq=sq: e.tensor_reduce(out=lnst[:, 1, :], in_=sq.rearrange("p (h d) -> p h d", h=4), axis=AX.X, op=ALU.add), [Bsq], [Blnst])
                TS1(lnst[:, 2, :], lnst[:, 0, :], 1.0 / 128, ALU.mult, [Blnst], [Blnst])
                TTo(lnst[:, 3, :], lnst[:, 2, :], lnst[:, 2, :], ALU.mult, [Blnst], [Blnst])
                STT(lnst[:, 4, :], lnst[:, 1, :], 1.0 / 128, lnst[:, 3, :], ALU.mult, ALU.subtract, [Blnst], [Blnst])
                ACT(lnst[:, 5, :], lnst[:, 4, :], AF.Ln, [Blnst, Bcb], [Blnst], scale=1.0, bias=cbias[:, 0:1])
                ACT(lnst[:, 5, :], lnst[:, 5, :], AF.Exp, [Blnst], [Blnst], scale=-0.5)
                STT(lnst[:, 6, :], lnst[:, 2, :], -1.0, lnst[:, 5, :], ALU.mult, ALU.mult, [Blnst], [Blnst])
                for h in range(4):
                    TS(v3[:, h, :], v3[:, h, :], lnst[:, 5, h:h + 1], lnst[:, 6, h:h + 1], ALU.mult, ALU.add, [Bvf, Blnst], [Bvf])
                TTo(vf, vf, lng, ALU.mult, [Bvf, Bsp], [Bvf])
                TTo(vn[:, tc, :], vf, lnb, ALU.add, [Bvf, Bsp], [Bvn[tc]])
                yield
            for h in range(4):
                b = PS()
                groups = []
                for tc in range(4):
                    groups.append((psum[b][:, tc * 128:(tc + 1) * 128],
                                   [(vn[:, tc, h * 128:(h + 1) * 128], wcT[:, l * 4 + h, :]),
                                    (onesb[0:1, :], bsrow[0:1, (l * 4 + h) * 128:(l * 4 + h + 1) * 128])]))
                MM(groups, Bvn + [Bsg, Bcb], [Bps[b]])
                raw, Braw = SF.next()
                TTo(raw, psum[b][:], uT[:, h, :], ALU.mult, [Bps[b], BuT[h]], [Braw])
                group_norm_out(raw, Braw, gsc[:, l, h:h + 1], Bgsc, yT[:, h, :], [ByT[h]])
                yield

        def sconv(l):
            for g, b in proj_units("gb"):
                ACT(gbT[:, g, :], psum[b][:], AF.Copy, [Bps[b]], [BgbT[g]])
                yield
            for g, b in proj_units("gc"):
                ACT(gcT[:, g, :], psum[b][:], AF.Copy, [Bps[b]], [BgcT[g]])
                yield
            cw = SPO["scw%d" % l]
            for g, b in proj_units("xin"):
                z, Bz = PRE.next()
                CP(z[:, 0:2], shalo[:, l, g, :], [Bsh[l][g]], [Bz])
                TTo(z[:, 2:2 + TT], psum[b][:], gcT[:, g, :], ALU.mult, [Bps[b], BgcT[g]], [Bz])
                CP(shalo[:, l, g, :], z[:, TT:TT + 2], [Bz], [Bsh[l][g]])
                acc, Bacc = SF.next()
                TS1(acc, z[:, 0:TT], spk[:, cw + g * 3:cw + g * 3 + 1], ALU.mult, [Bz, Bsp], [Bacc])
                for j in (1, 2):
                    STT(acc, z[:, j:j + TT], spk[:, cw + g * 3 + j:cw + g * 3 + j + 1], acc, ALU.mult, ALU.add, [Bz, Bsp, Bacc], [Bacc])
                TTo(acc, acc, gbT[:, g, :], ALU.mult, [Bacc, BgbT[g]], [Bacc])
                group_norm_out(acc, Bacc, gsc[:, l, 4 + g:5 + g], Bgsc, yT[:, 4 + g, :], [ByT[4 + g]])
                yield

        def conv_silu(l, b, ci, dst, Bdst, norm):
            pr, Bpr = PRE.next()
            CP(pr[:, 0:3], ghalo[:, l, ci, :], [Bgh[l][ci]], [Bpr])
            ACT(pr[:, 3:3 + TT], psum[b][:], AF.Copy, [Bps[b]], [Bpr])
            CP(ghalo[:, l, ci, :], pr[:, TT:TT + 3], [Bpr], [Bgh[l][ci]])
            cw = SPO["gcw%d" % l] + ci * 4
            acc, Bacc = SF.next()
            TS1(acc, pr[:, 0:TT], spk[:, cw:cw + 1], ALU.mult, [Bpr, Bsp], [Bacc])
            for j in (1, 2, 3):
                STT(acc, pr[:, j:j + TT], spk[:, cw + j:cw + j + 1], acc, ALU.mult, ALU.add, [Bpr, Bsp, Bacc], [Bacc])
            if not norm:
                ACT(dst, acc, AF.Silu, [Bacc], [Bdst])
                return
            ACT(acc, acc, AF.Silu, [Bacc], [Bacc])
            s, Bs_ = SB.next()
            ACT(s, acc, AF.Square, [Bacc], [Bs_])

            def partB():
                bb = PS()
                MM([(psum[bb][:], [(onesb[:], s)])], [Bs_, Bcb], [Bps[bb]])
                r, Br = rstd_from_ss(bb, 1.0)
                TTo(dst, acc, r, ALU.mult, [Bacc, Br], [Bdst])
            defer(partB)

        def gdn_scalars(l):
            b = PS()
            groups = []
            for tc in range(4):
                groups.append((psum[b][:, tc * 16:(tc + 1) * 16],
                               [(xnT[:, kc, tc * 128:(tc + 1) * 128], wab[:, l, kc, :]) for kc in range(16)]))
            MM(groups, BxnT + [Bwab], [Bps[b]])
            ab = psum[b][:, 0:64].rearrange("p (c k) -> p c k", c=4)
            dtb = spk[:, SPO["dtb%d" % l]:SPO["dtb%d" % l] + 8]
            TTo(sm[:, T1], ab[:, :, 0:8], bc(dtb.unsqueeze(1), [128, 4, 8]), ALU.add, [Bps[b], Bsp], [Bsm[T1]])
            ACT(sm[:, T1], sm[:, T1], AF.Exp, [Bsm[T1]], [Bsm[T1]])
            ACT(sm[:, T1], sm[:, T1], AF.Ln, [Bsm[T1], Bcb], [Bsm[T1]], scale=1.0, bias=cbias[:, 2:3])
            TTo(sm[:, G_], sm[:, T1], bc(nexpa[:, l, :].unsqueeze(1), [128, 4, 8]), ALU.mult, [Bsm[T1], Bgsc], [Bsm[G_]])
            ACT(sm[:, T2], ab[:, :, 8:16], AF.Exp, [Bps[b], Bsm[T1]], [Bsm[T2]], scale=-1.0)
            TS1(sm[:, T2], sm[:, T2], 1.0, ALU.add, [Bsm[T2]], [Bsm[T2]])
            P.op("dve", lambda e: e.reciprocal(out=sm[:, BETA], in_=sm[:, T2]), [Bsm[T2]], [Bsm[BETA]])
            TS1(sm[:, NEGB], sm[:, BETA], -1.0, ALU.mult, [Bsm[BETA]], [Bsm[NEGB]])
            b2 = PS()
            groups = []
            for tc in range(4):
                groups.append((psum[b2][:, tc * 16:tc * 16 + 8], [(Uf, sm[:, G_, tc, :])]))
                groups.append((psum[b2][:, tc * 16 + 8:tc * 16 + 16], [(onesf, sm[:, G_, tc, :])]))
            MM(groups, [Bsm[G_], Bc], [Bps[b2]])
            CP(smps[:], psum[b2][:, 0:64].rearrange("p (c k) -> p c k", c=4), [Bps[b2]], [Bsmps])
            ACT(sm[:, EGC], smps[:, :, 0:8], AF.Exp, [Bsmps], [Bsm[EGC]])
            ACT(sm[:, EGL], smps[:, :, 8:16], AF.Exp, [Bsmps], [Bsm[EGL]])
            TTo(sm[:, T3], smps[:, :, 8:16], smps[:, :, 0:8], ALU.subtract, [Bsmps], [Bsm[T3]])
            ACT(sm[:, ETAIL], sm[:, T3], AF.Exp, [Bsm[T3]], [Bsm[ETAIL]])
            TTo(sm[:, BEGA], sm[:, BETA], sm[:, EGC], ALU.mult, [Bsm[BETA], Bsm[EGC]], [Bsm[BEGA]])

        def proj_units(key):
            wv, bw = WNEXT(key)
            for m in range(4):
                b = PS()
                MM([(psum[b][:], [(wv[:, kc, m * 128:(m + 1) * 128], xnT[:, kc, :]) for kc in range(16)])],
                   [bw] + BxnT, [Bps[b]])
                tick()
                yield m, b

        def gdn_proj(l, hh, QS):
            qT, BqT, kT, BkT, vT, BvT, zsT, BzsT = QS
            h0 = 4 * hh
            for key, dst, Bdst, cbase, norm in (("q%d" % hh, qT, BqT, 0, True), ("k%d" % hh, kT, BkT, 8, True),
                                                ("v%d" % hh, vT, BvT, 16, False)):
                for hi, b in proj_units(key):
                    conv_silu(l, b, cbase + h0 + hi, dst[:, hi, :], Bdst[hi], norm)
                    yield
            for hi, b in proj_units("z%d" % hh):
                ACT(zsT[:, hi, :], psum[b][:], AF.Silu, [Bps[b]], [BzsT[hi]])
                yield

        def gdn_chunks(l, hh, QS, filler):
            qT, BqT, kT, BkT, vT, BvT, zsT, BzsT = QS
            h0 = 4 * hh

            def fill():
                flush()
                if filler is not None:
                    next(filler, None)
            flush()
            hs = slice(h0, h0 + 4)
            v4 = lambda b: psum[b][:].rearrange("p (h c) -> p h c", h=4)
            v4b = lambda b: psb[b][:, 0:512].rearrange("p (h c) -> p h c", h=4)
            for c in range(4):
                cs = slice(c * 128, (c + 1) * 128)
                sc = lambda idx: bc(sm[:, idx, c, hs].unsqueeze(2), [128, 4, 128])
                r4 = lambda ap: ap.rearrange("p (h c) -> p h c", h=4)
                GU_, BGU = SF.next(); GU = r4(GU_)
                EROW_, BEROW = SF.next(); EROW = r4(EROW_)
                tmpf_, Btmpf = SF.next(); tmpf = r4(tmpf_)
                ESL_, BESL = SF.next(); ESL = r4(ESL_)
                TTo(GU, bc(cst[:, 3:4, :], [128, 4, 128]), sc(G_), ALU.mult, [Bc, Bsm[G_]], [BGU])
                b = PS()
                MM([(psum[b][:, hi * 128:(hi + 1) * 128],
                     [(GU[:, hi, :], onesf), (negonesf, GU[:, hi, :]), (identf, maskneg)]) for hi in range(4)],
                   [BGU, Bc], [Bps[b]])
                ACT(Em[:], v4(b), AF.Exp, [Bps[b]], [BEm])
                tick()
                b = PS()
                MM([(psum[b][:, hi * 128:(hi + 1) * 128], [(onesf, GU[:, hi, :])]) for hi in range(4)], [BGU, Bc], [Bps[b]])
                ACT(EROW, v4(b), AF.Exp, [Bps[b]], [BEROW])
                TTo(ESL, Em[:], bc(cst[:, 5:6, :], [128, 4, 128]), ALU.mult, [BEm, Bc], [BESL])
                STT(qeT[:], qT[:, :, cs], float(128.0 ** -0.5), EROW, ALU.mult, ALU.mult, BqT + [BEROW], [BqeT])
                b = PS()
                TR([(psb[b][:, hi * 128:(hi + 1) * 128], kT[:, hi, cs], identb[:]) for hi in range(4)] +
                   [(psb[b][:, 512 + hi * 128:512 + (hi + 1) * 128], vT[:, hi, cs], identb[:]) for hi in range(4)],
                   BkT + BvT + [Bcb], [Bps[b]])
                kk = psb[b][:, 0:512].rearrange("p (h c) -> p h c", h=4)
                vv = psb[b][:, 512:1024].rearrange("p (h c) -> p h c", h=4)
                TTo(kbe[:], kk, sc(BEGA), ALU.mult, [Bps[b], Bsm[BEGA]], [Bkbe])
                TTo(ktl[:], kk, sc(ETAIL), ALU.mult, [Bps[b], Bsm[ETAIL]], [Bktl])
                TTo(vb[:], vv, sc(BETA), ALU.mult, [Bps[b], Bsm[BETA]], [Bvb])
                b = PS()
                MM([(psum[b][:, hi * 128:(hi + 1) * 128], [(kT[:, hi, cs], kT[:, hi, cs])]) for hi in range(4)], BkT, [Bps[b]])
                TTo(tmpf, v4(b), sc(NEGB), ALU.mult, [Bps[b], Bsm[NEGB]], [Btmpf])
                TTo(Wm[:], tmpf, ESL, ALU.mult, [Btmpf, BESL], [BWm])
                b = PS()
                MM([(psum[b][:, hi * 128:(hi + 1) * 128], [(qT[:, hi, cs], kT[:, hi, cs])]) for hi in range(4)], BqT + BkT, [Bps[b]])
                STT(intra[:], v4(b), float(128.0 ** -0.5), Em[:], ALU.mult, ALU.mult, [Bps[b], BEm], [Bintra])
                b = PS()
                TR([(psum[b][:, hi * 128:(hi + 1) * 128], Wm[:, hi, :], identf) for hi in range(4)], [BWm, Bc], [Bps[b]])
                b2 = PS()
                TR([(psb[b2][:, hi * 128:(hi + 1) * 128], intra[:, hi, :], identb[:]) for hi in range(4)], [Bintra, Bcb], [Bps[b2]])
                P0, BP0 = Pm[0]
                ACT(P0[:], v4(b), AF.Copy, [Bps[b]], [BP0])
                ACT(intraT[:], psb[b2][:, 0:512].rearrange("p (h c) -> p h c", h=4), AF.Copy, [Bps[b2]], [BintraT])
                R0, BR0 = Rm[0]
                TTo(R0[:], P0[:], bc(cst[:, 0:1, :], [128, 4, 128]), ALU.add, [BP0, Bc], [BR0])
                Pc, BPc = P0, BP0
                PTc, BPTc = Wm, BWm
                Rc, BRc = R0, BR0
                for lev in range(1, 7):
                    PTn, BPTn = PTm[lev % 2]
                    b = PS()
                    MM([(psum[b][:, hi * 128:(hi + 1) * 128], [(Pc[:, hi, :], PTc[:, hi, :])]) for hi in range(4)], [BPc, BPTc], [Bps[b]])
                    ACT(PTn[:], v4(b), AF.Copy, [Bps[b]], [BPTn])
                    if lev < 6:
                        Pn, BPn = Pm[lev % 2]
                        b2 = PS()
                        MM([(psum[b2][:, hi * 128:(hi + 1) * 128], [(PTc[:, hi, :], Pc[:, hi, :])]) for hi in range(4)], [BPc, BPTc], [Bps[b2]])
                        CP(Pn[:], v4(b2), [Bps[b2]], [BPn])
                    Rn, BRn = Rm[lev % 2]
                    b3 = PS()
                    MM([(psum[b3][:, hi * 128:(hi + 1) * 128], [(PTn[:, hi, :], Rc[:, hi, :])]) for hi in range(4)], [BPTn, BRc], [Bps[b3]])
                    TTo(Rn[:], v4(b3), Rc[:], ALU.add, [Bps[b3], BRc], [BRn])
                    PTc, BPTc = PTn, BPTn
                    if lev < 6:
                        Pc, BPc = Pn, BPn
                        fill()
                    Rc, BRc = Rn, BRn
                flush()
                ACT(Rbf[:], Rc[:], AF.Copy, [BRc], [BRbf])
                b = PS()
                MM([(psum[b][:, hi * 128:(hi + 1) * 128], [(kbe[:, hi, :], Rbf[:, hi, :])]) for hi in range(4)], [Bkbe, BRbf], [Bps[b]])
                ACT(nkcd[:], v4(b), AF.Copy, [Bps[b]], [Bnkcd], scale=-1.0)
                b = PS()
                MM([(psum[b][:, hi * 128:(hi + 1) * 128],
                     [(Rbf[:, hi, :], vb[:, hi, :]), (nkcd[:, hi, :], Sbf[:, l, h0 + hi, :])]) for hi in range(4)],
                   [BRbf, Bvb, Bnkcd, BSb[l][hh]], [Bps[b]])
                ACT(vnew[:], v4(b), AF.Copy, [Bps[b]], [Bvnew])
                bo = PS()
                MM([(psum[bo][:, hi * 128:(hi + 1) * 128],
                     [(Sbf[:, l, h0 + hi, :], qeT[:, hi, :]), (vnew[:, hi, :], intraT[:, hi, :])]) for hi in range(4)],
                   [BSb[l][hh], BqeT, Bvnew, BintraT], [Bps[bo]])
                bs = PS()
                MM([(psum[bs][:, hi * 128:(hi + 1) * 128], [(ktl[:, hi, :], vnew[:, hi, :])]) for hi in range(4)],
                   [Bktl, Bvnew], [Bps[bs]])
                TTo(Sst[:, l, hs, :], Sst[:, l, hs, :], sc(EGL), ALU.mult, [BS[l][hh], Bsm[EGL]], [BS[l][hh]])
                TTo(Sst[:, l, hs, :], Sst[:, l, hs, :], v4(bs), ALU.add, [BS[l][hh], Bps[bs]], [BS[l][hh]])
                ACT(Sbf[:, l, hs, :], Sst[:, l, hs, :], AF.Copy, [BS[l][hh]], [BSb[l][hh]])
                osb, Bosb = SF.next()
                ACT(osb, psum[bo][:], AF.Copy, [Bps[bo]], [Bosb])
                s, Bs_ = SB.next()
                ACT(s, osb, AF.Square, [Bosb], [Bs_])

                def partB(osb=osb, Bosb=Bosb, s=s, Bs_=Bs_, cs=cs):
                    bb = PS()
                    MM([(psum[bb][:], [(onesb[:], s)])], [Bs_, Bcb], [Bps[bb]])
                    r, Br = SF.next()
                    ACT(r, psum[bb][:], AF.Ln, [Bps[bb], Bcb], [Br], scale=1.0 / 128, bias=cbias[:, 0:1])
                    ACT(r, r, AF.Exp, [Br], [Br], scale=-0.5)
                    TTo(osb, osb, r, ALU.mult, [Bosb, Br], [Bosb])
                    gg = SPO["g_gdn%d" % l]
                    STT(yT[:, 8 + h0:12 + h0, cs], osb.rearrange("p (h c) -> p h c", h=4), spk[:, gg:gg + 1], zsT[:, :, cs],
                        ALU.mult, ALU.mult, [Bosb, Bsp] + BzsT, ByT[8 + h0:12 + h0])
                defer(partB)
            flush()
            if filler is not None:
                for _ in filler:
                    pass
            flush()

        def out_proj(l):
            flush()
            for cb in range(4):
                banks = proj_fm("wo%d" % cb, yT, ByT)
                for m in range(4):
                    kc = cb * 4 + m
                    TTo(hT[:, kc, :], hT[:, kc, :], psum[banks[m]][:], ALU.add, [BhT[kc], Bps[banks[m]]], [BhT[kc]])

        def ffn(l):
            rmsnorm_to(xnT, BxnT, SPO["g_ffn%d" % l])
            for q in range(4):
                for j in range(4):
                    banks = proj_fm("w1_%d_%d" % (q, j), xnT, BxnT)
                    for m in range(4):
                        kc = j * 4 + m
                        t_, Bt_ = SF.next()
                        ACT(t_, psum[banks[m]][:], AF.Relu, [Bps[banks[m]]], [Bt_])
                        TTo(yT[:, kc, :], t_, psum[banks[m]][:], ALU.mult, [Bt_, Bps[banks[m]]], [ByT[kc]])
                for cb in range(4):
                    banks = proj_fm("w2_%d_%d" % (q, cb), yT, ByT)
                    for m in range(4):
                        kc = cb * 4 + m
                        TTo(hT[:, kc, :], hT[:, kc, :], psum[banks[m]][:], ALU.add, [BhT[kc], Bps[banks[m]]], [BhT[kc]])

        def ple(t, l):
            rmsnorm_to(xnT, BxnT, SPO["g_ple%d" % l])
            r0 = t * TT
            b = PS()
            b2 = PS()
            for half in range(2):
                pg_, Bpg = SF.next()
                pg = pg_.rearrange("p (c f) -> p c f", c=2)
                rr = r0 + half * 256
                P.op("sp", lambda e, pg=pg, rr=rr: e.dma_start(out=pg, in_=p_d[l, rr:rr + 256, :].rearrange("(c p) f -> p c f", p=128)),
                     writes=[Bpg], dma=Dp[half])
                TR([(psum[b][:, (2 * half + c) * 128:(2 * half + c + 1) * 128], pg[:, c, 0:128], identf) for c in range(2)], [Bpg, Bc], [Bps[b]])
                TR([(psum[b2][:, (2 * half + c) * 128:(2 * half + c + 1) * 128], pg[:, c, 128:256], identf) for c in range(2)], [Bpg, Bc], [Bps[b2]])
            ACT(pT[:, 0, :], psum[b][:], AF.Copy, [Bps[b]], [BpT])
            ACT(pT[:, 1, :], psum[b2][:], AF.Copy, [Bps[b2]], [BpT])
            gates = []
            for cb in range(4):
                banks = proj_fm("pg%d" % cb, xnT, BxnT)
                for m in range(4):
                    g_, Bg_ = (yT[:, cb * 4 + m, :], ByT[cb * 4 + m])
                    ACT(g_, psum[banks[m]][:], AF.Sigmoid, [Bps[banks[m]]], [Bg_])
            wv, bw = WNEXT("pp")
            for kc in range(16):
                b = PS()
                MM([(psum[b][:], [(wv[:, k, kc * 128:(kc + 1) * 128], pT[:, k, :]) for k in range(2)])], [bw, BpT], [Bps[b]])
                t_, Bt_ = SF.next()
                TTo(t_, psum[b][:], yT[:, kc, :], ALU.mult, [Bps[b], ByT[kc]], [Bt_])
                TTo(hT[:, kc, :], hT[:, kc, :], t_, ALU.add, [BhT[kc], Bt_], [BhT[kc]])

        for t in range(NT):
            load_x(t)
            DUMP("h0", hT[:], [128, 16, TT], F32, BhT)
            for l in range(NL):
                rmsnorm_to(xnT, BxnT, SPO["g_mix%d" % l])
                DUMP("xn@%d" % l, xnT[:], [128, 16, TT], BF16, BxnT)
                gdn_scalars(l)
                DUMP("sm@%d" % l, sm[:], [128, 12, 4, 8], F32, Bsm)
                QA = (qT, BqT, kT, BkT, vT, BvT, zsT, BzsT)
                QB = (qT2, BqT2, kT2, BkT2, vT2, BvT2, zsT2, BzsT2)
                for _ in gdn_proj(l, 0, QA):
                    pass
                gdn_chunks(l, 0, QA, gdn_proj(l, 1, QB))

                def sgu_sc(l=l):
                    yield from sgu(l)
                    yield from sconv(l)
                gdn_chunks(l, 1, QB, sgu_sc())
                DUMP("y@%d" % l, yT[:], [128, 16, TT], BF16, ByT)
                out_proj(l)
                DUMP("h1@%d" % l, hT[:], [128, 16, TT], F32, BhT)
                ffn(l)
                DUMP("h2@%d" % l, hT[:], [128, 16, TT], F32, BhT)
                ple(t, l)
                DUMP("h3@%d" % l, hT[:], [128, 16, TT], F32, BhT)
            store_out(t)
        assert wstate["next"] == len(sched), (wstate, len(sched))
        P.wait_all("sp", ByT)
        print("ops recorded:", P.nops)
        P.emit()
    return nc


def host_pack(inp, NL=2):
    SPO, NSP = sp_layout(NL)
    sp = np.zeros((128, NSP), np.float32)
    fm = lambda v: np.ascontiguousarray(v.reshape(-1, 128).T)
    for l in range(NL):
        sp[:, SPO["g_mix%d" % l]:][:, :16] = fm(inp["norm_mix"][l])
        sp[:, SPO["g_ffn%d" % l]:][:, :16] = fm(inp["norm_ffn"][l])
        sp[:, SPO["g_ple%d" % l]:][:, :16] = fm(inp["norm_ple"][l])
        sp[:, SPO["g_a%d" % l]:][:, :4] = fm(inp["out_norm_a"][l])
        sp[:, SPO["g_b%d" % l]:][:, :4] = fm(inp["out_norm_b"][l])
        sp[:, SPO["g_gdn%d" % l]:][:, :1] = fm(inp["gdn_norm"][l])
        sp[:, SPO["scw%d" % l]:][:, :12] = inp["sc_conv"][l].reshape(3, 4, 128).transpose(2, 1, 0).reshape(128, 12)
        sp[:, SPO["gcw%d" % l]:][:, :96] = inp["gdn_conv"][l].reshape(4, 24, 128).transpose(2, 1, 0).reshape(128, 96)
        sp[:, SPO["alog%d" % l]:][:, :8] = np.broadcast_to(inp["gdn_a_log"][l], (128, 8))
        sp[:, SPO["dtb%d" % l]:][:, :8] = np.broadcast_to(inp["gdn_dt_bias"][l], (128, 8))
        sp[:, SPO["lng%d" % l]:][:, :512] = np.broadcast_to(inp["sg_ln_g"][l], (128, 512))
        sp[:, SPO["lnb%d" % l]:][:, :512] = np.broadcast_to(inp["sg_ln_b"][l], (128, 512))
    sp[:, SPO["g_fin"]:][:, :16] = fm(inp["norm_final"])
    sgwT = np.ascontiguousarray(inp["sg_w"].transpose(3, 0, 1, 2).reshape(128, 8, 128))
    sgb = np.ascontiguousarray(inp["sg_b"].reshape(1, 1024))
    wab = np.ascontiguousarray(inp["w_in"][:, :, 6656:6672])
    return sp, sgwT, sgb, wab


CORES = [0, 2, 4, 6]


def kernel(**inp):
    inp = {k: np.asarray(v) for k, v in inp.items()}
    sp, sgwT, sgb, wab = host_pack(inp)
    nc = build()
    common = {"w_in": inp["w_in"], "w_ab": wab, "w_o": inp["w_o"], "w_ff1": inp["w_ff1"], "w_ff2": inp["w_ff2"],
              "w_ple_gate": inp["w_ple_gate"], "w_ple_proj": inp["w_ple_proj"], "sp": sp, "sgwT": sgwT, "sgb": sgb}
    in_maps = []
    for b in range(4):
        m = dict(common)
        m["x"] = np.ascontiguousarray(inp["x"][b])
        m["p"] = np.ascontiguousarray(inp["p"][:, b])
        in_maps.append(m)
    res = run_bass_kernel_spmd(nc, in_maps, core_ids=CORES)
    return np.stack([np.asarray(res.results[b]["out"]) for b in range(4)], axis=0).astype(np.float32)
```

```python
import numpy as np
from contextlib import ExitStack
import concourse.bass as bass
import concourse.mybir as mybir
from concourse.bass_utils import run_bass_kernel_spmd

F32 = mybir.dt.float32
BF16 = mybir.dt.bfloat16
AF = mybir.ActivationFunctionType
ALU = mybir.AluOpType
AX = mybir.AxisListType

D = 2048
SEQ = 4096
TT = 512
EPS = 1e-6
IN_COLS = 6672


class TL:
    def __init__(self, sem, unit, name):
        self.sem, self.unit, self.count, self.name = sem, unit, 0, name


class Buf:
    def __init__(self, name):
        self.name, self.w, self.r = name, None, {}


class Prog:
    ENGS = ("pe", "act", "dve", "pool", "sp")

    def __init__(self, nc, stack):
        self.nc, self.stack = nc, stack
        self.ops = {e: [] for e in self.ENGS}
        self.tl = {e: TL(stack.enter_context(nc.semaphore("tl_" + e)), 1, e) for e in self.ENGS}
        self.seen = {e: {} for e in self.ENGS}
        self.nops = 0

    def dma_tl(self, name):
        return TL(self.stack.enter_context(self.nc.semaphore("d_" + name)), 16, name)

    def _deps(self, eng, reads, writes):
        deps = {}
        me = self.tl[eng]

        def add(t):
            if t is None:
                return
            tl, c = t
            if tl is me and eng == "pe":
                return
            if deps.get(tl, 0) < c:
                deps[tl] = c

        for b in reads:
            add(b.w)
        for b in writes:
            add(b.w)
            for tl, c in b.r.items():
                add((tl, c))
        waits = []
        seen = self.seen[eng]
        for tl, c in deps.items():
            if seen.get(tl, 0) < c:
                seen[tl] = c
                waits.append((tl.sem, c * tl.unit))
        return waits

    def op(self, eng, fn, reads=(), writes=(), dma=None):
        waits = self._deps(eng, reads, writes)
        tl = dma if dma is not None else self.tl[eng]
        tl.count += 1
        cnt = tl.count
        sem, unit = tl.sem, tl.unit

        def run(e, fn=fn, waits=waits, sem=sem, unit=unit):
            for s, v in waits:
                e.wait_ge(s, v)
            fn(e).then_inc(sem, unit)

        self.ops[eng].append(run)
        self.nops += 1
        for b in reads:
            b.r[tl] = cnt
        for b in writes:
            b.w = (tl, cnt)
            b.r = {}

    def wait_all(self, eng, bufs):
        waits = self._deps(eng, (), bufs)

        def run(e, waits=waits):
            for s, v in waits:
                e.wait_ge(s, v)

        self.ops[eng].append(run)

    def emit(self):
        with self.nc.Block() as block:
            @block.tensor
            def _(e):
                for f in self.ops["pe"]:
                    f(e)

            @block.scalar
            def _(e):
                for f in self.ops["act"]:
                    f(e)

            @block.vector
            def _(e):
                for f in self.ops["dve"]:
                    f(e)

            @block.gpsimd
            def _(e):
                for f in self.ops["pool"]:
                    f(e)

            @block.sync
            def _(e):
                for f in self.ops["sp"]:
                    f(e)


class Ring:
    def __init__(self, aps, name):
        self.aps = aps
        self.bufs = [Buf("%s%d" % (name, i)) for i in range(len(aps))]
        self.i = 0

    def next(self):
        i = self.i
        self.i = (i + 1) % len(self.aps)
        return self.aps[i], self.bufs[i]


def sp_layout(NL):
    off = {}
    o = 0

    def add(name, n):
        nonlocal o
        off[name] = o
        o += n

    for l in range(NL):
        add("g_mix%d" % l, 16)
        add("g_ffn%d" % l, 16)
        add("g_ple%d" % l, 16)
        add("g_a%d" % l, 4)
        add("g_b%d" % l, 4)
        add("g_gdn%d" % l, 1)
        add("scw%d" % l, 12)
        add("gcw%d" % l, 96)
        add("alog%d" % l, 8)
        add("dtb%d" % l, 8)
        add("lng%d" % l, 512)
        add("lnb%d" % l, 512)
    add("g_fin", 16)
    return off, o


def build(NT=8, NL=2, dumps=()):
    nc = bass.Bass("TRN2", target_bir_lowering=False)
    SPO, NSP = sp_layout(NL)
    dt_in = lambda name, shape: nc.dram_tensor(name, shape, F32, kind="ExternalInput").ap()
    x_d = dt_in("x", [SEQ, D])
    p_d = dt_in("p", [2, SEQ, 256])
    win_d = dt_in("w_in", [2, D, IN_COLS])
    wab_d = dt_in("w_ab", [2, D, 16])
    wo_d = dt_in("w_o", [2, D, D])
    w1_d = dt_in("w_ff1", [2, D, 4 * D])
    w2_d = dt_in("w_ff2", [2, 4 * D, D])
    wpg_d = dt_in("w_ple_gate", [2, D, D])
    wpp_d = dt_in("w_ple_proj", [2, 256, D])
    sp_d = dt_in("sp", [128, NSP])
    sgw_d = dt_in("sgwT", [128, 8, 128])
    sgb_d = dt_in("sgb", [1, 1024])
    out_d = nc.dram_tensor("out", [SEQ, D], F32, kind="ExternalOutput").ap()

    with ExitStack() as st:
        P = Prog(nc, st)
        sbt = lambda name, shape, dt: st.enter_context(nc.sbuf_tensor(name, shape, dt))
        pst_ = lambda name, shape, dt: st.enter_context(nc.psum_tensor(name, shape, dt))

        hT = sbt("hT", [128, 16, TT], F32)
        BhT = [Buf("hT%d" % i) for i in range(16)]
        xnT = sbt("xnT", [128, 16, TT], BF16)
        BxnT = [Buf("xnT%d" % i) for i in range(16)]
        yT = sbt("yT", [128, 16, TT], BF16)
        ByT = [Buf("yT%d" % i) for i in range(16)]
        stage = yT[:].rearrange("p a b -> p (a b)").bitcast(F32).rearrange("p (j c) -> p j c", j=2)
        NWB = 2
        wbuf = sbt("wbuf", [128, NWB, 8192], BF16)
        Bw = [Buf("w%d" % i) for i in range(NWB)]
        Dw = [P.dma_tl("w%d" % i) for i in range(NWB)]
        spk = sbt("spk", [128, NSP], F32)
        Bsp = Buf("spk")
        cst = sbt("cst", [128, 6, 128], F32)
        Bc = Buf("cst")
        cbias = sbt("cbias", [128, 4], F32)
        identb = sbt("identb", [128, 128], BF16)
        onesb = sbt("onesb", [128, 128], BF16)
        Bcb = Buf("cstb")
        wcT = sbt("wcT", [128, 8, 128], BF16)
        bsrow = sbt("bsrow", [1, 1024], BF16)
        Bsg = Buf("sg")
        wab = sbt("wab", [128, 2, 16, 16], BF16)
        Bwab = Buf("wab")
        gsc = sbt("gsc", [128, 2, 8], F32)
        Bgsc = Buf("gsc")
        nexpa = sbt("nexpa", [128, 2, 8], F32)
        Sst = sbt("Sst", [128, 2, 8, 128], F32)
        Sbf = sbt("Sbf", [128, 2, 8, 128], BF16)
        BS = [[Buf("S%d_%d" % (l, hh)) for hh in range(2)] for l in range(2)]
        BSb = [[Buf("Sb%d_%d" % (l, hh)) for hh in range(2)] for l in range(2)]
        ghalo = sbt("ghalo", [128, 2, 24, 3], F32)
        Bgh = [[Buf("gh%d_%d" % (l, c)) for c in range(24)] for l in range(2)]
        shalo = sbt("shalo", [128, 2, 4, 2], F32)
        Bsh = [[Buf("sh%d_%d" % (l, c)) for c in range(4)] for l in range(2)]
        scrf_t = sbt("scrf", [128, 6, TT], F32)
        SF = Ring([scrf_t[:, i, :] for i in range(6)], "scrf")
        sgst = scrf_t[:, 0:2, :].rearrange("p a (b c) -> p (a b) c", c=128)
        Bsgst = SF.bufs[0]
        SF.bufs[1] = SF.bufs[0]
        bsst = scrf_t[0:1, 2:4, :].rearrange("p a b -> p (a b)")
        Bbsst = SF.bufs[2]
        SF.bufs[3] = SF.bufs[2]
        scrb_t = sbt("scrb", [128, 4, TT], BF16)
        SB = Ring([scrb_t[:, i, :] for i in range(4)], "scrb")
        pre_t = sbt("pre", [128, 2, TT + 4], F32)
        PRE = Ring([pre_t[:, i, :] for i in range(2)], "pre")
        qT = sbt("qT", [128, 4, TT], BF16); BqT = [Buf("qT%d" % i) for i in range(4)]
        kT = sbt("kT", [128, 4, TT], BF16); BkT = [Buf("kT%d" % i) for i in range(4)]
        qT2 = sbt("qT2", [128, 4, TT], BF16); BqT2 = [Buf("qT2%d" % i) for i in range(4)]
        kT2 = sbt("kT2", [128, 4, TT], BF16); BkT2 = [Buf("kT2%d" % i) for i in range(4)]
        vT2 = sbt("vT2", [128, 4, TT], BF16); BvT2 = [Buf("vT2%d" % i) for i in range(4)]
        zsT2 = sbt("zsT2", [128, 4, TT], BF16); BzsT2 = [Buf("zsT2%d" % i) for i in range(4)]
        uT = qT; BuT = BqT
        vn = kT; Bvn = BkT
        gbT = uT; BgbT = BuT
        gcT = vn; BgcT = Bvn
        vT = sbt("vT", [128, 4, TT], BF16); BvT = [Buf("vT%d" % i) for i in range(4)]
        zsT = sbt("zsT", [128, 4, TT], BF16); BzsT = [Buf("zsT%d" % i) for i in range(4)]
        Dp = [P.dma_tl("p%d" % i) for i in range(2)]
        pT = sbt("pT", [128, 2, TT], BF16); BpT = Buf("pT")
        sm = sbt("sm", [128, 12, 4, 8], F32)
        Bsm = [Buf("sm%d" % i) for i in range(12)]
        G_, BETA, NEGB, EGC, EGL, ETAIL, BEGA, T1, T2, GC, GL, T3 = range(12)
        smps = sbt("smps", [128, 4, 16], F32); Bsmps = Buf("smps")
        lnst = sbt("lnst", [128, 8, 4], F32); Blnst = Buf("lnst")
        mk = lambda name, dt: (sbt(name, [128, 4, 128], dt), Buf(name))
        Em, BEm = mk("Em", F32)
        qeT, BqeT = mk("qeT", BF16)
        kbe, Bkbe = mk("kbe", BF16)
        ktl, Bktl = mk("ktl", BF16)
        vb, Bvb = mk("vb", BF16)
        Wm, BWm = mk("Wm", F32)
        Rbf, BRbf = mk("Rbf", BF16)
        intra, Bintra = mk("intra", BF16)
        intraT, BintraT = mk("intraT", BF16)
        Pm = [mk("Pm%d" % i, F32) for i in range(2)]
        PTm = [mk("PTm%d" % i, F32) for i in range(2)]
        Rm = [mk("Rm%d" % i, F32) for i in range(2)]
        nkcd, Bnkcd = mk("nkcd", BF16)
        vnew, Bvnew = mk("vnew", BF16)
        psum = [pst_("ps%d" % i, [128, 512], F32) for i in range(8)]
        Bps = [Buf("ps%d" % i) for i in range(8)]
        psb = [psum[i][:].bitcast(BF16) for i in range(8)]
        psi = [0]

        def PS():
            i = psi[0]
            psi[0] = (i + 1) % 8
            return i

        Dx = [P.dma_tl("x%d" % j) for j in range(2)]
        Do = [P.dma_tl("o%d" % j) for j in range(2)]
        Dsetup = P.dma_tl("setup")

        dump_d = {}

        def DUMP(name, ap, shape, dt, bufs):
            if name not in dumps or name in dump_d:
                return
            d = nc.dram_tensor("dbg_" + name, shape, dt, kind="ExternalOutput").ap()
            dump_d[name] = d
            P.op("sp", lambda e: e.dma_start(out=d, in_=ap), reads=bufs, dma=P.dma_tl("dbg_" + name))

        def ACT(out, in_, func, reads, writes, **kw):
            P.op("act", lambda e: e.activation(out=out, in_=in_, func=func, **kw), reads, writes)

        def TTo(out, in0, in1, op, reads, writes, eng="dve"):
            P.op(eng, lambda e: e.tensor_tensor(out=out, in0=in0, in1=in1, op=op), reads, writes)

        def TS(out, in0, s1, s2, op0, op1, reads, writes, eng="dve"):
            P.op(eng, lambda e: e.tensor_scalar(out=out, in0=in0, scalar1=s1, scalar2=s2, op0=op0, op1=op1), reads, writes)

        def TS1(out, in_, s, op, reads, writes, eng="dve"):
            P.op(eng, lambda e: e.tensor_single_scalar(out=out, in_=in_, scalar=s, op=op), reads, writes)

        def STT(out, in0, s, in1, op0, op1, reads, writes, eng="dve"):
            P.op(eng, lambda e: e.scalar_tensor_tensor(out=out, in0=in0, scalar=s, in1=in1, op0=op0, op1=op1), reads, writes)

        def CP(out, in_, reads, writes, eng="dve"):
            if eng == "act":
                P.op("act", lambda e: e.copy(out=out, in_=in_), reads, writes)
            else:
                P.op(eng, lambda e: e.tensor_copy(out=out, in_=in_), reads, writes)

        def MM(groups, reads, writes):
            def f(e):
                ins = None
                for out, pairs in groups:
                    n = len(pairs)
                    for i, (l, r) in enumerate(pairs):
                        ins = e.matmul(out, lhsT=l, rhs=r, start=(i == 0), stop=(i == n - 1))
                return ins
            P.op("pe", f, reads, writes)

        def TR(items, reads, writes):
            def f(e):
                ins = None
                for o, i, idn in items:
                    ins = e.transpose(o, i, idn)
                return ins
            P.op("pe", f, reads, writes)

        def bc(ap, shape):
            return ap.to_broadcast(shape)

        identf = cst[:, 0, :]
        onesf = cst[:, 1, :]
        negonesf = cst[:, 2, :]
        Uf = cst[:, 3, :]
        maskneg = cst[:, 4, :]
        SLm = cst[:, 5, :]

        P.op("sp", lambda e: e.dma_start(out=spk[:], in_=sp_d), writes=[Bsp], dma=Dsetup)
        P.op("sp", lambda e: e.dma_start(out=sgst[:], in_=sgw_d), writes=[Bsgst], dma=P.dma_tl("sgst"))
        P.op("sp", lambda e: e.dma_start(out=bsst[:], in_=sgb_d), writes=[Bbsst], dma=P.dma_tl("bsst"))
        P.op("pool", lambda e: e.dma_start(out=wab[:].rearrange("p l k c -> p (l k) c"),
                                           in_=wab_d.rearrange("l (k p) c -> p (l k) c", p=128)),
             writes=[Bwab], dma=P.dma_tl("wab"))
        def cmem(i, v):
            P.op("pool", lambda e: e.memset(cst[:, i, :], v), writes=[Bc])

        def csel(i, pat, op, fill, cm):
            P.op("pool", lambda e: e.affine_select(out=cst[:, i, :], in_=cst[:, i, :], pattern=[[pat, 128]], compare_op=op,
                                                   fill=fill, base=0, channel_multiplier=cm), reads=[Bc], writes=[Bc])
        cmem(0, 1.0); csel(0, -1, ALU.is_equal, 0.0, 1)
        cmem(1, 1.0)
        cmem(2, -1.0)
        cmem(3, 1.0); csel(3, 1, ALU.is_ge, 0.0, -1)
        cmem(4, 0.0); csel(4, -1, ALU.is_ge, -30000.0, 1)
        cmem(5, 1.0); csel(5, -1, ALU.is_gt, 0.0, 1)
        P.op("dve", lambda e: e.memset(cbias[:, 0:1], EPS), writes=[Bcb])
        P.op("dve", lambda e: e.memset(cbias[:, 1:2], 128.0 * EPS), writes=[Bcb])
        P.op("dve", lambda e: e.memset(cbias[:, 2:3], 1.0), writes=[Bcb])
        CP(identb[:], identf, [Bc], [Bcb])
        CP(onesb[:], onesf, [Bc], [Bcb])
        TTo(wcT[:], sgst, bc(cst[:, 3:4, :], [128, 8, 128]), ALU.mult, [Bsgst, Bc], [Bsg])
        CP(bsrow[:], bsst, [Bbsst], [Bsg])
        SF.bufs[1] = Buf("scrf1")
        SF.bufs[3] = Buf("scrf3")
        P.op("dve", lambda e: e.memset(Sst[:], 0.0), writes=[b for r in BS for b in r])
        P.op("dve", lambda e: e.memset(Sbf[:], 0.0), writes=[b for r in BSb for b in r])
        P.op("dve", lambda e: e.memset(ghalo[:], 0.0), writes=[b for r in Bgh for b in r])
        P.op("dve", lambda e: e.memset(shalo[:], 0.0), writes=[b for r in Bsh for b in r])
        for l in range(NL):
            TS1(gsc[:, l, :], spk[:, SPO["g_a%d" % l]:SPO["g_a%d" % l] + 8], float(np.sqrt(128.0)), ALU.mult, [Bsp], [Bgsc])
            ACT(nexpa[:, l, :], spk[:, SPO["alog%d" % l]:SPO["alog%d" % l] + 8], AF.Exp, [Bsp], [Bgsc])
            TS1(nexpa[:, l, :], nexpa[:, l, :], -1.0, ALU.mult, [Bgsc], [Bgsc])

        sched = []

        def blk(W, l, r0, c0):
            return W[l, r0:r0 + D, c0:c0 + 512].rearrange("(k p) c -> p k c", p=128)

        v16 = lambda s: wbuf[:, s, :].rearrange("p (k c) -> p k c", k=16)
        v2 = lambda s: wbuf[:, s, 0:4096].rearrange("p (k c) -> p k c", k=2)
        INB = {"u": 0, "v": 1, "gb": 2, "gc": 3, "xin": 4, "q0": 5, "q1": 6, "k0": 7, "k1": 8, "v0": 9, "v1": 10, "z0": 11, "z1": 12}
        for t in range(NT):
            for l in range(NL):
                for key in ("q0", "k0", "v0", "z0", "q1", "k1", "v1", "z1", "u", "v", "gb", "gc", "xin"):
                    sched.append((key, blk(win_d, l, 0, INB[key] * 512), v16))
                for cb in range(4):
                    sched.append(("wo%d" % cb, blk(wo_d, l, 0, cb * 512), v16))
                for q in range(4):
                    for j in range(4):
                        sched.append(("w1_%d_%d" % (q, j), blk(w1_d, l, 0, q * 2048 + j * 512), v16))
                    for cb in range(4):
                        sched.append(("w2_%d_%d" % (q, cb), blk(w2_d, l, q * 2048, cb * 512), v16))
                for cb in range(4):
                    sched.append(("pg%d" % cb, blk(wpg_d, l, 0, cb * 512), v16))
                sched.append(("pp", wpp_d[l].rearrange("(k p) c -> p k c", p=128), v2))
        wstate = {"issued": 0, "next": 0}
        NBT = len(sched) // NT
        wcache = nc.dram_tensor("wcache", [NBT, 128, 8192], BF16).ap()
        Bcache = [Buf("wc%d" % i) for i in range(NBT)]
        Dst = [P.dma_tl("wst%d" % i) for i in range(NWB)]

        def WNEXT(key):
            i = wstate["next"]
            assert sched[i][0] == key, (sched[i][0], key)
            while wstate["issued"] < min(len(sched), i + NWB):
                j = wstate["issued"]
                s = j % NWB
                cid = j % NBT
                _, src, vf = sched[j]
                if j < NBT or NT == 1:
                    P.op("pool", lambda e, s=s, src=src, vf=vf: e.dma_start(out=vf(s), in_=src), writes=[Bw[s]], dma=Dw[s])
                    if NT > 1:
                        P.op("sp", lambda e, s=s, cid=cid: e.dma_start(out=wcache[cid], in_=wbuf[:, s, :]),
                             reads=[Bw[s]], writes=[Bcache[cid]], dma=Dst[s])
                else:
                    P.op("pool", lambda e, s=s, cid=cid: e.dma_start(out=wbuf[:, s, :], in_=wcache[cid]),
                         reads=[Bcache[cid]], writes=[Bw[s]], dma=Dw[s])
                wstate["issued"] += 1
            wstate["next"] += 1
            s = i % NWB
            return sched[i][2](s), Bw[s]

        pending = []

        def defer(th):
            pending.append(th)

        def tick(n=1):
            for _ in range(n):
                if pending:
                    pending.pop(0)()

        def flush():
            while pending:
                pending.pop(0)()

        def proj_fm(key, src, Bsrc, nk=16):
            wv, bw = WNEXT(key)
            banks = []
            for m in range(4):
                b = PS()
                MM([(psum[b][:], [(wv[:, kc, m * 128:(m + 1) * 128], src[:, kc, :]) for kc in range(nk)])],
                   [bw] + Bsrc, [Bps[b]])
                banks.append(b)
                tick()
            return banks

        def rstd_from_ss(b, scale, n=TT):
            r, Br = SF.next()
            ACT(r[:, 0:n], psum[b][:, 0:n], AF.Ln, [Bps[b], Bcb], [Br], scale=scale, bias=cbias[:, 0:1])
            ACT(r[:, 0:n], r[:, 0:n], AF.Exp, [Br], [Br], scale=-0.5)
            return r, Br

        def rmsnorm_to(dst, Bdst, gcol, src_bufs_extra=()):
            b = PS()
            for kc in range(16):
                s, Bs_ = SB.next()
                ACT(s, hT[:, kc, :], AF.Square, [BhT[kc]], [Bs_])
                P.op("pe", lambda e, s=s, kc=kc, b=b: e.matmul(psum[b][:], lhsT=onesb[:], rhs=s, start=(kc == 0), stop=(kc == 15)),
                     [Bs_, Bcb], [Bps[b]])
            r, Br = rstd_from_ss(b, 1.0 / D)
            DUMP("rstd", r, [128, TT], F32, [Br])
            for kc in range(16):
                STT(dst[:, kc, :], hT[:, kc, :], spk[:, gcol + kc:gcol + kc + 1], r, ALU.mult, ALU.mult,
                    [BhT[kc], Bsp, Br], [Bdst[kc]])

        def group_norm_out(raw, Braw, gain_ap, Bgain, dst, Bdst_list, extra_in1=None, Bextra=()):
            s, Bs_ = SB.next()
            ACT(s, raw, AF.Square, [Braw], [Bs_])

            def partB():
                b = PS()
                MM([(psum[b][:], [(onesb[:], s)])], [Bs_, Bcb], [Bps[b]])
                r, Br = SF.next()
                ACT(r, psum[b][:], AF.Ln, [Bps[b], Bcb], [Br], scale=1.0, bias=cbias[:, 1:2])
                ACT(r, r, AF.Exp, [Br], [Br], scale=-0.5)
                STT(dst, raw, gain_ap, r, ALU.mult, ALU.mult, [Braw, Bgain, Br], Bdst_list)
            defer(partB)

        def load_x(t):
            for tc in range(4):
                j = tc % 2
                r0 = t * TT + tc * 128
                P.op("sp", lambda e, j=j, r0=r0: e.dma_start(out=stage[:, j, :], in_=x_d[r0:r0 + 128, :]),
                     writes=ByT[8 * j:8 * j + 8], dma=Dx[j])
                for g in range(4):
                    b = PS()
                    TR([(psum[b][:, k * 128:(k + 1) * 128], stage[:, j, (4 * g + k) * 128:(4 * g + k + 1) * 128], identf)
                        for k in range(4)], ByT[8 * j:8 * j + 8] + [Bc], [Bps[b]])
                    CP(hT[:, 4 * g:4 * g + 4, tc * 128:(tc + 1) * 128], psum[b][:].rearrange("p (k c) -> p k c", k=4),
                       [Bps[b]], BhT[4 * g:4 * g + 4], eng=("act" if g % 2 == 0 else "dve"))

        def store_out(t):
            b = PS()
            for kc in range(16):
                s, Bs_ = SB.next()
                ACT(s, hT[:, kc, :], AF.Square, [BhT[kc]], [Bs_])
                P.op("pe", lambda e, s=s, kc=kc, b=b: e.matmul(psum[b][:], lhsT=onesb[:], rhs=s, start=(kc == 0), stop=(kc == 15)),
                     [Bs_, Bcb], [Bps[b]])
            r, Br = rstd_from_ss(b, 1.0 / D)
            gcol = SPO["g_fin"]
            for kc in range(16):
                STT(hT[:, kc, :], hT[:, kc, :], spk[:, gcol + kc:gcol + kc + 1], r, ALU.mult, ALU.mult,
                    [BhT[kc], Bsp, Br], [BhT[kc]])
            for tc in range(4):
                j = tc % 2
                r0 = t * TT + tc * 128
                for g in range(4):
                    b = PS()
                    TR([(psum[b][:, k * 128:(k + 1) * 128], hT[:, 4 * g + k, tc * 128:(tc + 1) * 128], identf)
                        for k in range(4)], BhT[4 * g:4 * g + 4] + [Bc], [Bps[b]])
                    CP(stage[:, j, g * 512:(g + 1) * 512], psum[b][:], [Bps[b]], ByT[8 * j + 2 * g:8 * j + 2 * g + 2],
                       eng=("act" if g % 2 == 0 else "dve"))
                P.op("sp", lambda e, j=j, r0=r0: e.dma_start(out=out_d[r0:r0 + 128, :], in_=stage[:, j, :]),
                     reads=ByT[8 * j:8 * j + 8], dma=Do[j])

        def sgu(l):
            for h, b in proj_units("u"):
                ACT(uT[:, h, :], psum[b][:], AF.Gelu, [Bps[b]], [BuT[h]])
                yield
            wv, bw = WNEXT("v")
            lng = spk[:, SPO["lng%d" % l]:SPO["lng%d" % l] + 512]
            lnb = spk[:, SPO["lnb%d" % l]:SPO["lnb%d" % l] + 512]
            for tc in range(4):
                b = PS()
                MM([(psum[b][:], [(xnT[:, kc, tc * 128:(tc + 1) * 128], wv[:, kc, :]) for kc in range(16)])],
                   [bw] + BxnT, [Bps[b]])
                tick()
                vf, Bvf = SF.next()
                ACT(vf, psum[b][:], AF.Gelu, [Bps[b]], [Bvf])
                sq, Bsq = SF.next()
                ACT(sq, vf, AF.Square, [Bvf], [Bsq])
                v3 = vf.rearrange("p (h d) -> p h d", h=4)
                P.op("dve", lambda e, v3=v3: e.tensor_reduce(out=lnst[:, 0, :], in_=v3, axis=AX.X, op=ALU.add), [Bvf], [Blnst])
                P.op("dve", lambda e, sq=sq: e.tensor_reduce(out=lnst[:, 1, :], in_=sq.rearrange("p (h d) -> p h d", h=4), axis=AX.X, op=ALU.add), [Bsq], [Blnst])
                TS1(lnst[:, 2, :], lnst[:, 0, :], 1.0 / 128, ALU.mult, [Blnst], [Blnst])
                TTo(lnst[:, 3, :], lnst[:, 2, :], lnst[:, 2, :], ALU.mult, [Blnst], [Blnst])
                STT(lnst[:, 4, :], lnst[:, 1, :], 1.0 / 128, lnst[:, 3, :], ALU.mult, ALU.subtract, [Blnst], [Blnst])
                ACT(lnst[:, 5, :], lnst[:, 4, :], AF.Ln, [Blnst, Bcb], [Blnst], scale=1.0, bias=cbias[:, 0:1])
                ACT(lnst[:, 5, :], lnst[:, 5, :], AF.Exp, [Blnst], [Blnst], scale=-0.5)
                STT(lnst[:, 6, :], lnst[:, 2, :], -1.0, lnst[:, 5, :], ALU.mult, ALU.mult, [Blnst], [Blnst])
                for h in range(4):
                    TS(v3[:, h, :], v3[:, h, :], lnst[:, 5, h:h + 1], lnst[:, 6, h:h + 1], ALU.mult, ALU.add, [Bvf, Blnst], [Bvf])
                TTo(vf, vf, lng, ALU.mult, [Bvf, Bsp], [Bvf])
                TTo(vn[:, tc, :], vf, lnb, ALU.add, [Bvf, Bsp], [Bvn[tc]])
                yield
            for h in range(4):
                b = PS()
                groups = []
                for tc in range(4):
                    groups.append((psum[b][:, tc * 128:(tc + 1) * 128],
                                   [(vn[:, tc, h * 128:(h + 1) * 128], wcT[:, l * 4 + h, :]),
                                    (onesb[0:1, :], bsrow[0:1, (l * 4 + h) * 128:(l * 4 + h + 1) * 128])]))
                MM(groups, Bvn + [Bsg, Bcb], [Bps[b]])
                raw, Braw = SF.next()
                TTo(raw, psum[b][:], uT[:, h, :], ALU.mult, [Bps[b], BuT[h]], [Braw])
                group_norm_out(raw, Braw, gsc[:, l, h:h + 1], Bgsc, yT[:, h, :], [ByT[h]])
                yield

        def sconv(l):
            for g, b in proj_units("gb"):
                ACT(gbT[:, g, :], psum[b][:], AF.Copy, [Bps[b]], [BgbT[g]])
                yield
            for g, b in proj_units("gc"):
                ACT(gcT[:, g, :], psum[b][:], AF.Copy, [Bps[b]], [BgcT[g]])
                yield
            cw = SPO["scw%d" % l]
            for g, b in proj_units("xin"):
                z, Bz = PRE.next()
                CP(z[:, 0:2], shalo[:, l, g, :], [Bsh[l][g]], [Bz])
                TTo(z[:, 2:2 + TT], psum[b][:], gcT[:, g, :], ALU.mult, [Bps[b], BgcT[g]], [Bz])
                CP(shalo[:, l, g, :], z[:, TT:TT + 2], [Bz], [Bsh[l][g]])
                acc, Bacc = SF.next()
                TS1(acc, z[:, 0:TT], spk[:, cw + g * 3:cw + g * 3 + 1], ALU.mult, [Bz, Bsp], [Bacc])
                for j in (1, 2):
                    STT(acc, z[:, j:j + TT], spk[:, cw + g * 3 + j:cw + g * 3 + j + 1], acc, ALU.mult, ALU.add, [Bz, Bsp, Bacc], [Bacc])
                TTo(acc, acc, gbT[:, g, :], ALU.mult, [Bacc, BgbT[g]], [Bacc])
                group_norm_out(acc, Bacc, gsc[:, l, 4 + g:5 + g], Bgsc, yT[:, 4 + g, :], [ByT[4 + g]])
                yield

        def conv_silu(l, b, ci, dst, Bdst, norm):
            pr, Bpr = PRE.next()
            CP(pr[:, 0:3], ghalo[:, l, ci, :], [Bgh[l][ci]], [Bpr])
            ACT(pr[:, 3:3 + TT], psum[b][:], AF.Copy, [Bps[b]], [Bpr])
            CP(ghalo[:, l, ci, :], pr[:, TT:TT + 3], [Bpr], [Bgh[l][ci]])
            cw = SPO["gcw%d" % l] + ci * 4
            acc, Bacc = SF.next()
            TS1(acc, pr[:, 0:TT], spk[:, cw:cw + 1], ALU.mult, [Bpr, Bsp], [Bacc])
            for j in (1, 2, 3):
                STT(acc, pr[:, j:j + TT], spk[:, cw + j:cw + j + 1], acc, ALU.mult, ALU.add, [Bpr, Bsp, Bacc], [Bacc])
            if not norm:
                ACT(dst, acc, AF.Silu, [Bacc], [Bdst])
                return
            ACT(acc, acc, AF.Silu, [Bacc], [Bacc])
            s, Bs_ = SB.next()
            ACT(s, acc, AF.Square, [Bacc], [Bs_])

            def partB():
                bb = PS()
                MM([(psum[bb][:], [(onesb[:], s)])], [Bs_, Bcb], [Bps[bb]])
                r, Br = rstd_from_ss(bb, 1.0)
                TTo(dst, acc, r, ALU.mult, [Bacc, Br], [Bdst])
            defer(partB)

        def gdn_scalars(l):
            b = PS()
            groups = []
            for tc in range(4):
                groups.append((psum[b][:, tc * 16:(tc + 1) * 16],
                               [(xnT[:, kc, tc * 128:(tc + 1) * 128], wab[:, l, kc, :]) for kc in range(16)]))
            MM(groups, BxnT + [Bwab], [Bps[b]])
            ab = psum[b][:, 0:64].rearrange("p (c k) -> p c k", c=4)
            dtb = spk[:, SPO["dtb%d" % l]:SPO["dtb%d" % l] + 8]
            TTo(sm[:, T1], ab[:, :, 0:8], bc(dtb.unsqueeze(1), [128, 4, 8]), ALU.add, [Bps[b], Bsp], [Bsm[T1]])
            ACT(sm[:, T1], sm[:, T1], AF.Exp, [Bsm[T1]], [Bsm[T1]])
            ACT(sm[:, T1], sm[:, T1], AF.Ln, [Bsm[T1], Bcb], [Bsm[T1]], scale=1.0, bias=cbias[:, 2:3])
            TTo(sm[:, G_], sm[:, T1], bc(nexpa[:, l, :].unsqueeze(1), [128, 4, 8]), ALU.mult, [Bsm[T1], Bgsc], [Bsm[G_]])
            ACT(sm[:, T2], ab[:, :, 8:16], AF.Exp, [Bps[b], Bsm[T1]], [Bsm[T2]], scale=-1.0)
            TS1(sm[:, T2], sm[:, T2], 1.0, ALU.add, [Bsm[T2]], [Bsm[T2]])
            P.op("dve", lambda e: e.reciprocal(out=sm[:, BETA], in_=sm[:, T2]), [Bsm[T2]], [Bsm[BETA]])
            TS1(sm[:, NEGB], sm[:, BETA], -1.0, ALU.mult, [Bsm[BETA]], [Bsm[NEGB]])
            b2 = PS()
            groups = []
            for tc in range(4):
                groups.append((psum[b2][:, tc * 16:tc * 16 + 8], [(Uf, sm[:, G_, tc, :])]))
                groups.append((psum[b2][:, tc * 16 + 8:tc * 16 + 16], [(onesf, sm[:, G_, tc, :])]))
            MM(groups, [Bsm[G_], Bc], [Bps[b2]])
            CP(smps[:], psum[b2][:, 0:64].rearrange("p (c k) -> p c k", c=4), [Bps[b2]], [Bsmps])
            ACT(sm[:, EGC], smps[:, :, 0:8], AF.Exp, [Bsmps], [Bsm[EGC]])
            ACT(sm[:, EGL], smps[:, :, 8:16], AF.Exp, [Bsmps], [Bsm[EGL]])
            TTo(sm[:, T3], smps[:, :, 8:16], smps[:, :, 0:8], ALU.subtract, [Bsmps], [Bsm[T3]])
            ACT(sm[:, ETAIL], sm[:, T3], AF.Exp, [Bsm[T3]], [Bsm[ETAIL]])
            TTo(sm[:, BEGA], sm[:, BETA], sm[:, EGC], ALU.mult, [Bsm[BETA], Bsm[EGC]], [Bsm[BEGA]])

        def proj_units(key):
            wv, bw = WNEXT(key)
            for m in range(4):
                b = PS()
                MM([(psum[b][:], [(wv[:, kc, m * 128:(m + 1) * 128], xnT[:, kc, :]) for kc in range(16)])],
                   [bw] + BxnT, [Bps[b]])
                tick()
                yield m, b

        def gdn_proj(l, hh, QS):
            qT, BqT, kT, BkT, vT, BvT, zsT, BzsT = QS
            h0 = 4 * hh
            for key, dst, Bdst, cbase, norm in (("q%d" % hh, qT, BqT, 0, True), ("k%d" % hh, kT, BkT, 8, True),
                                                ("v%d" % hh, vT, BvT, 16, False)):
                for hi, b in proj_units(key):
                    conv_silu(l, b, cbase + h0 + hi, dst[:, hi, :], Bdst[hi], norm)
                    yield
            for hi, b in proj_units("z%d" % hh):
                ACT(zsT[:, hi, :], psum[b][:], AF.Silu, [Bps[b]], [BzsT[hi]])
                yield

        def gdn_chunks(l, hh, QS, filler):
            qT, BqT, kT, BkT, vT, BvT, zsT, BzsT = QS
            h0 = 4 * hh

            def fill():
                flush()
                if filler is not None:
                    next(filler, None)
            flush()
            hs = slice(h0, h0 + 4)
            v4 = lambda b: psum[b][:].rearrange("p (h c) -> p h c", h=4)
            v4b = lambda b: psb[b][:, 0:512].rearrange("p (h c) -> p h c", h=4)
            for c in range(4):
                cs = slice(c * 128, (c + 1) * 128)
                sc = lambda idx: bc(sm[:, idx, c, hs].unsqueeze(2), [128, 4, 128])
                r4 = lambda ap: ap.rearrange("p (h c) -> p h c", h=4)
                GU_, BGU = SF.next(); GU = r4(GU_)
                EROW_, BEROW = SF.next(); EROW = r4(EROW_)
                tmpf_, Btmpf = SF.next(); tmpf = r4(tmpf_)
                ESL_, BESL = SF.next(); ESL = r4(ESL_)
                TTo(GU, bc(cst[:, 3:4, :], [128, 4, 128]), sc(G_), ALU.mult, [Bc, Bsm[G_]], [BGU])
                b = PS()
                MM([(psum[b][:, hi * 128:(hi + 1) * 128],
                     [(GU[:, hi, :], onesf), (negonesf, GU[:, hi, :]), (identf, maskneg)]) for hi in range(4)],
                   [BGU, Bc], [Bps[b]])
                ACT(Em[:], v4(b), AF.Exp, [Bps[b]], [BEm])
                tick()
                b = PS()
                MM([(psum[b][:, hi * 128:(hi + 1) * 128], [(onesf, GU[:, hi, :])]) for hi in range(4)], [BGU, Bc], [Bps[b]])
                ACT(EROW, v4(b), AF.Exp, [Bps[b]], [BEROW])
                TTo(ESL, Em[:], bc(cst[:, 5:6, :], [128, 4, 128]), ALU.mult, [BEm, Bc], [BESL])
                STT(qeT[:], qT[:, :, cs], float(128.0 ** -0.5), EROW, ALU.mult, ALU.mult, BqT + [BEROW], [BqeT])
                b = PS()
                TR([(psb[b][:, hi * 128:(hi + 1) * 128], kT[:, hi, cs], identb[:]) for hi in range(4)] +
                   [(psb[b][:, 512 + hi * 128:512 + (hi + 1) * 128], vT[:, hi, cs], identb[:]) for hi in range(4)],
                   BkT + BvT + [Bcb], [Bps[b]])
                kk = psb[b][:, 0:512].rearrange("p (h c) -> p h c", h=4)
                vv = psb[b][:, 512:1024].rearrange("p (h c) -> p h c", h=4)
                TTo(kbe[:], kk, sc(BEGA), ALU.mult, [Bps[b], Bsm[BEGA]], [Bkbe])
                TTo(ktl[:], kk, sc(ETAIL), ALU.mult, [Bps[b], Bsm[ETAIL]], [Bktl])
                TTo(vb[:], vv, sc(BETA), ALU.mult, [Bps[b], Bsm[BETA]], [Bvb])
                b = PS()
                MM([(psum[b][:, hi * 128:(hi + 1) * 128], [(kT[:, hi, cs], kT[:, hi, cs])]) for hi in range(4)], BkT, [Bps[b]])
                TTo(tmpf, v4(b), sc(NEGB), ALU.mult, [Bps[b], Bsm[NEGB]], [Btmpf])
                TTo(Wm[:], tmpf, ESL, ALU.mult, [Btmpf, BESL], [BWm])
                b = PS()
                MM([(psum[b][:, hi * 128:(hi + 1) * 128], [(qT[:, hi, cs], kT[:, hi, cs])]) for hi in range(4)], BqT + BkT, [Bps[b]])
                STT(intra[:], v4(b), float(128.0 ** -0.5), Em[:], ALU.mult, ALU.mult, [Bps[b], BEm], [Bintra])
                b = PS()
                TR([(psum[b][:, hi * 128:(hi + 1) * 128], Wm[:, hi, :], identf) for hi in range(4)], [BWm, Bc], [Bps[b]])
                b2 = PS()
                TR([(psb[b2][:, hi * 128:(hi + 1) * 128], intra[:, hi, :], identb[:]) for hi in range(4)], [Bintra, Bcb], [Bps[b2]])
                P0, BP0 = Pm[0]
                ACT(P0[:], v4(b), AF.Copy, [Bps[b]], [BP0])
                ACT(intraT[:], psb[b2][:, 0:512].rearrange("p (h c) -> p h c", h=4), AF.Copy, [Bps[b2]], [BintraT])
                R0, BR0 = Rm[0]
                TTo(R0[:], P0[:], bc(cst[:, 0:1, :], [128, 4, 128]), ALU.add, [BP0, Bc], [BR0])
                Pc, BPc = P0, BP0
                PTc, BPTc = Wm, BWm
                Rc, BRc = R0, BR0
                for lev in range(1, 7):
                    PTn, BPTn = PTm[lev % 2]
                    b = PS()
                    MM([(psum[b][:, hi * 128:(hi + 1) * 128], [(Pc[:, hi, :], PTc[:, hi, :])]) for hi in range(4)], [BPc, BPTc], [Bps[b]])
                    ACT(PTn[:], v4(b), AF.Copy, [Bps[b]], [BPTn])
                    if lev < 6:
                        Pn, BPn = Pm[lev % 2]
                        b2 = PS()
                        MM([(psum[b2][:, hi * 128:(hi + 1) * 128], [(PTc[:, hi, :], Pc[:, hi, :])]) for hi in range(4)], [BPc, BPTc], [Bps[b2]])
                        CP(Pn[:], v4(b2), [Bps[b2]], [BPn])
                    if lev < 6:
                        fill()
                    Rn, BRn = Rm[lev % 2]
                    b3 = PS()
                    MM([(psum[b3][:, hi * 128:(hi + 1) * 128], [(PTn[:, hi, :], Rc[:, hi, :])]) for hi in range(4)], [BPTn, BRc], [Bps[b3]])
                    TTo(Rn[:], v4(b3), Rc[:], ALU.add, [Bps[b3], BRc], [BRn])
                    PTc, BPTc = PTn, BPTn
                    if lev < 6:
                        Pc, BPc = Pn, BPn
                    Rc, BRc = Rn, BRn
                flush()
                ACT(Rbf[:], Rc[:], AF.Copy, [BRc], [BRbf])
                b = PS()
                MM([(psum[b][:, hi * 128:(hi + 1) * 128], [(kbe[:, hi, :], Rbf[:, hi, :])]) for hi in range(4)], [Bkbe, BRbf], [Bps[b]])
                ACT(nkcd[:], v4(b), AF.Copy, [Bps[b]], [Bnkcd], scale=-1.0)
                b = PS()
                MM([(psum[b][:, hi * 128:(hi + 1) * 128],
                     [(Rbf[:, hi, :], vb[:, hi, :]), (nkcd[:, hi, :], Sbf[:, l, h0 + hi, :])]) for hi in range(4)],
                   [BRbf, Bvb, Bnkcd, BSb[l][hh]], [Bps[b]])
                ACT(vnew[:], v4(b), AF.Copy, [Bps[b]], [Bvnew])
                bo = PS()
                MM([(psum[bo][:, hi * 128:(hi + 1) * 128],
                     [(Sbf[:, l, h0 + hi, :], qeT[:, hi, :]), (vnew[:, hi, :], intraT[:, hi, :])]) for hi in range(4)],
                   [BSb[l][hh], BqeT, Bvnew, BintraT], [Bps[bo]])
                bs = PS()
                MM([(psum[bs][:, hi * 128:(hi + 1) * 128], [(ktl[:, hi, :], vnew[:, hi, :])]) for hi in range(4)],
                   [Bktl, Bvnew], [Bps[bs]])
                TTo(Sst[:, l, hs, :], Sst[:, l, hs, :], sc(EGL), ALU.mult, [BS[l][hh], Bsm[EGL]], [BS[l][hh]])
                TTo(Sst[:, l, hs, :], Sst[:, l, hs, :], v4(bs), ALU.add, [BS[l][hh], Bps[bs]], [BS[l][hh]])
                ACT(Sbf[:, l, hs, :], Sst[:, l, hs, :], AF.Copy, [BS[l][hh]], [BSb[l][hh]])
                osb, Bosb = SF.next()
                ACT(osb, psum[bo][:], AF.Copy, [Bps[bo]], [Bosb])
                s, Bs_ = SB.next()
                ACT(s, osb, AF.Square, [Bosb], [Bs_])

                def partB(osb=osb, Bosb=Bosb, s=s, Bs_=Bs_, cs=cs):
                    bb = PS()
                    MM([(psum[bb][:], [(onesb[:], s)])], [Bs_, Bcb], [Bps[bb]])
                    r, Br = SF.next()
                    ACT(r, psum[bb][:], AF.Ln, [Bps[bb], Bcb], [Br], scale=1.0 / 128, bias=cbias[:, 0:1])
                    ACT(r, r, AF.Exp, [Br], [Br], scale=-0.5)
                    TTo(osb, osb, r, ALU.mult, [Bosb, Br], [Bosb])
                    gg = SPO["g_gdn%d" % l]
                    STT(yT[:, 8 + h0:12 + h0, cs], osb.rearrange("p (h c) -> p h c", h=4), spk[:, gg:gg + 1], zsT[:, :, cs],
                        ALU.mult, ALU.mult, [Bosb, Bsp] + BzsT, ByT[8 + h0:12 + h0])
                defer(partB)
            flush()
            if filler is not None:
                for _ in filler:
                    pass
            flush()

        def out_proj(l):
            flush()
            for cb in range(4):
                banks = proj_fm("wo%d" % cb, yT, ByT)
                for m in range(4):
                    kc = cb * 4 + m
                    TTo(hT[:, kc, :], hT[:, kc, :], psum[banks[m]][:], ALU.add, [BhT[kc], Bps[banks[m]]], [BhT[kc]])

        def ffn(l):
            rmsnorm_to(xnT, BxnT, SPO["g_ffn%d" % l])
            for q in range(4):
                for j in range(4):
                    banks = proj_fm("w1_%d_%d" % (q, j), xnT, BxnT)
                    for m in range(4):
                        kc = j * 4 + m
                        t_, Bt_ = SF.next()
                        ACT(t_, psum[banks[m]][:], AF.Relu, [Bps[banks[m]]], [Bt_])
                        TTo(yT[:, kc, :], t_, psum[banks[m]][:], ALU.mult, [Bt_, Bps[banks[m]]], [ByT[kc]])
                for cb in range(4):
                    banks = proj_fm("w2_%d_%d" % (q, cb), yT, ByT)
                    for m in range(4):
                        kc = cb * 4 + m
                        TTo(hT[:, kc, :], hT[:, kc, :], psum[banks[m]][:], ALU.add, [BhT[kc], Bps[banks[m]]], [BhT[kc]])

        def ple(t, l):
            rmsnorm_to(xnT, BxnT, SPO["g_ple%d" % l])
            r0 = t * TT
            b = PS()
            b2 = PS()
            for half in range(2):
                pg_, Bpg = SF.next()
                pg = pg_.rearrange("p (c f) -> p c f", c=2)
                rr = r0 + half * 256
                P.op("sp", lambda e, pg=pg, rr=rr: e.dma_start(out=pg, in_=p_d[l, rr:rr + 256, :].rearrange("(c p) f -> p c f", p=128)),
                     writes=[Bpg], dma=Dp[half])
                TR([(psum[b][:, (2 * half + c) * 128:(2 * half + c + 1) * 128], pg[:, c, 0:128], identf) for c in range(2)], [Bpg, Bc], [Bps[b]])
                TR([(psum[b2][:, (2 * half + c) * 128:(2 * half + c + 1) * 128], pg[:, c, 128:256], identf) for c in range(2)], [Bpg, Bc], [Bps[b2]])
            ACT(pT[:, 0, :], psum[b][:], AF.Copy, [Bps[b]], [BpT])
            ACT(pT[:, 1, :], psum[b2][:], AF.Copy, [Bps[b2]], [BpT])
            gates = []
            for cb in range(4):
                banks = proj_fm("pg%d" % cb, xnT, BxnT)
                for m in range(4):
                    g_, Bg_ = (yT[:, cb * 4 + m, :], ByT[cb * 4 + m])
                    ACT(g_, psum[banks[m]][:], AF.Sigmoid, [Bps[banks[m]]], [Bg_])
            wv, bw = WNEXT("pp")
            for kc in range(16):
                b = PS()
                MM([(psum[b][:], [(wv[:, k, kc * 128:(kc + 1) * 128], pT[:, k, :]) for k in range(2)])], [bw, BpT], [Bps[b]])
                t_, Bt_ = SF.next()
                TTo(t_, psum[b][:], yT[:, kc, :], ALU.mult, [Bps[b], ByT[kc]], [Bt_])
                TTo(hT[:, kc, :], hT[:, kc, :], t_, ALU.add, [BhT[kc], Bt_], [BhT[kc]])

        for t in range(NT):
            load_x(t)
            DUMP("h0", hT[:], [128, 16, TT], F32, BhT)
            for l in range(NL):
                rmsnorm_to(xnT, BxnT, SPO["g_mix%d" % l])
                DUMP("xn@%d" % l, xnT[:], [128, 16, TT], BF16, BxnT)
                gdn_scalars(l)
                DUMP("sm@%d" % l, sm[:], [128, 12, 4, 8], F32, Bsm)
                QA = (qT, BqT, kT, BkT, vT, BvT, zsT, BzsT)
                QB = (qT2, BqT2, kT2, BkT2, vT2, BvT2, zsT2, BzsT2)
                for _ in gdn_proj(l, 0, QA):
                    pass
                gdn_chunks(l, 0, QA, gdn_proj(l, 1, QB))

                def sgu_sc(l=l):
                    yield from sgu(l)
                    yield from sconv(l)
                gdn_chunks(l, 1, QB, sgu_sc())
                DUMP("y@%d" % l, yT[:], [128, 16, TT], BF16, ByT)
                out_proj(l)
                DUMP("h1@%d" % l, hT[:], [128, 16, TT], F32, BhT)
                ffn(l)
                DUMP("h2@%d" % l, hT[:], [128, 16, TT], F32, BhT)
                ple(t, l)
                DUMP("h3@%d" % l, hT[:], [128, 16, TT], F32, BhT)
            store_out(t)
        assert wstate["next"] == len(sched), (wstate, len(sched))
        P.wait_all("sp", ByT)
        print("ops recorded:", P.nops)
        P.emit()
    return nc


def host_pack(inp, NL=2):
    SPO, NSP = sp_layout(NL)
    sp = np.zeros((128, NSP), np.float32)
    fm = lambda v: np.ascontiguousarray(v.reshape(-1, 128).T)
    for l in range(NL):
        sp[:, SPO["g_mix%d" % l]:][:, :16] = fm(inp["norm_mix"][l])
        sp[:, SPO["g_ffn%d" % l]:][:, :16] = fm(inp["norm_ffn"][l])
        sp[:, SPO["g_ple%d" % l]:][:, :16] = fm(inp["norm_ple"][l])
        sp[:, SPO["g_a%d" % l]:][:, :4] = fm(inp["out_norm_a"][l])
        sp[:, SPO["g_b%d" % l]:][:, :4] = fm(inp["out_norm_b"][l])
        sp[:, SPO["g_gdn%d" % l]:][:, :1] = fm(inp["gdn_norm"][l])
        sp[:, SPO["scw%d" % l]:][:, :12] = inp["sc_conv"][l].reshape(3, 4, 128).transpose(2, 1, 0).reshape(128, 12)
        sp[:, SPO["gcw%d" % l]:][:, :96] = inp["gdn_conv"][l].reshape(4, 24, 128).transpose(2, 1, 0).reshape(128, 96)
        sp[:, SPO["alog%d" % l]:][:, :8] = np.broadcast_to(inp["gdn_a_log"][l], (128, 8))
        sp[:, SPO["dtb%d" % l]:][:, :8] = np.broadcast_to(inp["gdn_dt_bias"][l], (128, 8))
        sp[:, SPO["lng%d" % l]:][:, :512] = np.broadcast_to(inp["sg_ln_g"][l], (128, 512))
        sp[:, SPO["lnb%d" % l]:][:, :512] = np.broadcast_to(inp["sg_ln_b"][l], (128, 512))
    sp[:, SPO["g_fin"]:][:, :16] = fm(inp["norm_final"])
    sgwT = np.ascontiguousarray(inp["sg_w"].transpose(3, 0, 1, 2).reshape(128, 8, 128))
    sgb = np.ascontiguousarray(inp["sg_b"].reshape(1, 1024))
    wab = np.ascontiguousarray(inp["w_in"][:, :, 6656:6672])
    return sp, sgwT, sgb, wab


CORES = [0, 2, 4, 6]


def kernel(**inp):
    inp = {k: np.asarray(v) for k, v in inp.items()}
    sp, sgwT, sgb, wab = host_pack(inp)
    nc = build()
    common = {"w_in": inp["w_in"], "w_ab": wab, "w_o": inp["w_o"], "w_ff1": inp["w_ff1"], "w_ff2": inp["w_ff2"],
              "w_ple_gate": inp["w_ple_gate"], "w_ple_proj": inp["w_ple_proj"], "sp": sp, "sgwT": sgwT, "sgb": sgb}
    in_maps = []
    for b in range(4):
        m = dict(common)
        m["x"] = np.ascontiguousarray(inp["x"][b])
        m["p"] = np.ascontiguousarray(inp["p"][:, b])
        in_maps.append(m)
    res = run_bass_kernel_spmd(nc, in_maps, core_ids=CORES)
    return np.stack([np.asarray(res.results[b]["out"]) for b in range(4)], axis=0).astype(np.float32)
```

```python
import numpy as np
from contextlib import ExitStack
import concourse.bass as bass
import concourse.mybir as mybir
from concourse.bass_utils import run_bass_kernel_spmd

F32 = mybir.dt.float32
BF16 = mybir.dt.bfloat16
AF = mybir.ActivationFunctionType
ALU = mybir.AluOpType
AX = mybir.AxisListType

D = 2048
SEQ = 4096
TT = 512
EPS = 1e-6
IN_COLS = 6672


class TL:
    def __init__(self, sem, unit, name):
        self.sem, self.unit, self.count, self.name = sem, unit, 0, name


class Buf:
    def __init__(self, name):
        self.name, self.w, self.r = name, None, {}


class Prog:
    ENGS = ("pe", "act", "dve", "pool", "sp")

    def __init__(self, nc, stack):
        self.nc, self.stack = nc, stack
        self.ops = {e: [] for e in self.ENGS}
        self.tl = {e: TL(stack.enter_context(nc.semaphore("tl_" + e)), 1, e) for e in self.ENGS}
        self.seen = {e: {} for e in self.ENGS}
        self.nops = 0

    def dma_tl(self, name):
        return TL(self.stack.enter_context(self.nc.semaphore("d_" + name)), 16, name)

    def _deps(self, eng, reads, writes):
        deps = {}
        me = self.tl[eng]

        def add(t):
            if t is None:
                return
            tl, c = t
            if tl is me and eng == "pe":
                return
            if deps.get(tl, 0) < c:
                deps[tl] = c

        for b in reads:
            add(b.w)
        for b in writes:
            add(b.w)
            for tl, c in b.r.items():
                add((tl, c))
        waits = []
        seen = self.seen[eng]
        for tl, c in deps.items():
            if seen.get(tl, 0) < c:
                seen[tl] = c
                waits.append((tl.sem, c * tl.unit))
        return waits

    def op(self, eng, fn, reads=(), writes=(), dma=None):
        waits = self._deps(eng, reads, writes)
        tl = dma if dma is not None else self.tl[eng]
        tl.count += 1
        cnt = tl.count
        sem, unit = tl.sem, tl.unit

        def run(e, fn=fn, waits=waits, sem=sem, unit=unit):
            for s, v in waits:
                e.wait_ge(s, v)
            fn(e).then_inc(sem, unit)

        self.ops[eng].append(run)
        self.nops += 1
        for b in reads:
            b.r[tl] = cnt
        for b in writes:
            b.w = (tl, cnt)
            b.r = {}

    def wait_all(self, eng, bufs):
        waits = self._deps(eng, (), bufs)

        def run(e, waits=waits):
            for s, v in waits:
                e.wait_ge(s, v)

        self.ops[eng].append(run)

    def emit(self):
        with self.nc.Block() as block:
            @block.tensor
            def _(e):
                for f in self.ops["pe"]:
                    f(e)

            @block.scalar
            def _(e):
                for f in self.ops["act"]:
                    f(e)

            @block.vector
            def _(e):
                for f in self.ops["dve"]:
                    f(e)

            @block.gpsimd
            def _(e):
                for f in self.ops["pool"]:
                    f(e)

            @block.sync
            def _(e):
                for f in self.ops["sp"]:
                    f(e)


class Ring:
    def __init__(self, aps, name):
        self.aps = aps
        self.bufs = [Buf("%s%d" % (name, i)) for i in range(len(aps))]
        self.i = 0

    def next(self):
        i = self.i
        self.i = (i + 1) % len(self.aps)
        return self.aps[i], self.bufs[i]


def sp_layout(NL):
    off = {}
    o = 0

    def add(name, n):
        nonlocal o
        off[name] = o
        o += n

    for l in range(NL):
        add("g_mix%d" % l, 16)
        add("g_ffn%d" % l, 16)
        add("g_ple%d" % l, 16)
        add("g_a%d" % l, 4)
        add("g_b%d" % l, 4)
        add("g_gdn%d" % l, 1)
        add("scw%d" % l, 12)
        add("gcw%d" % l, 96)
        add("alog%d" % l, 8)
        add("dtb%d" % l, 8)
        add("lng%d" % l, 512)
        add("lnb%d" % l, 512)
    add("g_fin", 16)
    return off, o


def build(NT=8, NL=2, dumps=()):
    nc = bass.Bass("TRN2", target_bir_lowering=False)
    SPO, NSP = sp_layout(NL)
    dt_in = lambda name, shape: nc.dram_tensor(name, shape, F32, kind="ExternalInput").ap()
    x_d = dt_in("x", [SEQ, D])
    p_d = dt_in("p", [2, SEQ, 256])
    win_d = dt_in("w_in", [2, D, IN_COLS])
    wab_d = dt_in("w_ab", [2, D, 16])
    wo_d = dt_in("w_o", [2, D, D])
    w1_d = dt_in("w_ff1", [2, D, 4 * D])
    w2_d = dt_in("w_ff2", [2, 4 * D, D])
    wpg_d = dt_in("w_ple_gate", [2, D, D])
    wpp_d = dt_in("w_ple_proj", [2, 256, D])
    sp_d = dt_in("sp", [128, NSP])
    sgw_d = dt_in("sgwT", [128, 8, 128])
    sgb_d = dt_in("sgb", [1, 1024])
    out_d = nc.dram_tensor("out", [SEQ, D], F32, kind="ExternalOutput").ap()

    with ExitStack() as st:
        P = Prog(nc, st)
        sbt = lambda name, shape, dt: st.enter_context(nc.sbuf_tensor(name, shape, dt))
        pst_ = lambda name, shape, dt: st.enter_context(nc.psum_tensor(name, shape, dt))

        hT = sbt("hT", [128, 16, TT], F32)
        BhT = [Buf("hT%d" % i) for i in range(16)]
        xnT = sbt("xnT", [128, 16, TT], BF16)
        BxnT = [Buf("xnT%d" % i) for i in range(16)]
        yT = sbt("yT", [128, 16, TT], BF16)
        ByT = [Buf("yT%d" % i) for i in range(16)]
        stage = yT[:].rearrange("p a b -> p (a b)").bitcast(F32).rearrange("p (j c) -> p j c", j=2)
        NWB = 3
        wbuf = sbt("wbuf", [128, NWB, 8192], BF16)
        Bw = [Buf("w%d" % i) for i in range(NWB)]
        Dw = [P.dma_tl("w%d" % i) for i in range(NWB)]
        spk = sbt("spk", [128, NSP], F32)
        Bsp = Buf("spk")
        cst = sbt("cst", [128, 6, 128], F32)
        Bc = Buf("cst")
        cbias = sbt("cbias", [128, 4], F32)
        identb = sbt("identb", [128, 128], BF16)
        onesb = sbt("onesb", [128, 128], BF16)
        Bcb = Buf("cstb")
        wcT = sbt("wcT", [128, 8, 128], BF16)
        bsrow = sbt("bsrow", [1, 1024], BF16)
        Bsg = Buf("sg")
        wab = sbt("wab", [128, 2, 16, 16], BF16)
        Bwab = Buf("wab")
        gsc = sbt("gsc", [128, 2, 8], F32)
        Bgsc = Buf("gsc")
        nexpa = sbt("nexpa", [128, 2, 8], F32)
        Sst = sbt("Sst", [128, 2, 8, 128], F32)
        Sbf = sbt("Sbf", [128, 2, 8, 128], BF16)
        BS = [[Buf("S%d_%d" % (l, hh)) for hh in range(2)] for l in range(2)]
        BSb = [[Buf("Sb%d_%d" % (l, hh)) for hh in range(2)] for l in range(2)]
        ghalo = sbt("ghalo", [128, 2, 24, 3], F32)
        Bgh = [[Buf("gh%d_%d" % (l, c)) for c in range(24)] for l in range(2)]
        shalo = sbt("shalo", [128, 2, 4, 2], F32)
        Bsh = [[Buf("sh%d_%d" % (l, c)) for c in range(4)] for l in range(2)]
        scrf_t = sbt("scrf", [128, 6, TT], F32)
        SF = Ring([scrf_t[:, i, :] for i in range(6)], "scrf")
        sgst = scrf_t[:, 0:2, :].rearrange("p a (b c) -> p (a b) c", c=128)
        Bsgst = SF.bufs[0]
        SF.bufs[1] = SF.bufs[0]
        bsst = scrf_t[0:1, 2:4, :].rearrange("p a b -> p (a b)")
        Bbsst = SF.bufs[2]
        SF.bufs[3] = SF.bufs[2]
        scrb_t = sbt("scrb", [128, 4, TT], BF16)
        SB = Ring([scrb_t[:, i, :] for i in range(4)], "scrb")
        pre_t = sbt("pre", [128, 2, TT + 4], F32)
        PRE = Ring([pre_t[:, i, :] for i in range(2)], "pre")
        qT = sbt("qT", [128, 4, TT], BF16); BqT = [Buf("qT%d" % i) for i in range(4)]
        kT = sbt("kT", [128, 4, TT], BF16); BkT = [Buf("kT%d" % i) for i in range(4)]
        uT = qT; BuT = BqT
        vn = kT; Bvn = BkT
        gbT = uT; BgbT = BuT
        gcT = vn; BgcT = Bvn
        vT = sbt("vT", [128, 4, TT], BF16); BvT = [Buf("vT%d" % i) for i in range(4)]
        zsT = sbt("zsT", [128, 4, TT], BF16); BzsT = [Buf("zsT%d" % i) for i in range(4)]
        Dp = [P.dma_tl("p%d" % i) for i in range(2)]
        pT = sbt("pT", [128, 2, TT], BF16); BpT = Buf("pT")
        sm = sbt("sm", [128, 12, 4, 8], F32)
        Bsm = [Buf("sm%d" % i) for i in range(12)]
        G_, BETA, NEGB, EGC, EGL, ETAIL, BEGA, T1, T2, GC, GL, T3 = range(12)
        smps = sbt("smps", [128, 4, 16], F32); Bsmps = Buf("smps")
        lnst = sbt("lnst", [128, 8, 4], F32); Blnst = Buf("lnst")
        mk = lambda name, dt: (sbt(name, [128, 4, 128], dt), Buf(name))
        Em, BEm = mk("Em", F32)
        qeT, BqeT = mk("qeT", BF16)
        kbe, Bkbe = mk("kbe", BF16)
        ktl, Bktl = mk("ktl", BF16)
        vb, Bvb = mk("vb", BF16)
        Wm, BWm = mk("Wm", F32)
        Rbf, BRbf = mk("Rbf", BF16)
        intra, Bintra = mk("intra", BF16)
        intraT, BintraT = mk("intraT", BF16)
        Pm = [mk("Pm%d" % i, F32) for i in range(2)]
        PTm = [mk("PTm%d" % i, F32) for i in range(2)]
        Rm = [mk("Rm%d" % i, F32) for i in range(2)]
        nkcd, Bnkcd = mk("nkcd", BF16)
        vnew, Bvnew = mk("vnew", BF16)
        psum = [pst_("ps%d" % i, [128, 512], F32) for i in range(8)]
        Bps = [Buf("ps%d" % i) for i in range(8)]
        psb = [psum[i][:].bitcast(BF16) for i in range(8)]
        psi = [0]

        def PS():
            i = psi[0]
            psi[0] = (i + 1) % 8
            return i

        Dx = [P.dma_tl("x%d" % j) for j in range(2)]
        Do = [P.dma_tl("o%d" % j) for j in range(2)]
        Dsetup = P.dma_tl("setup")

        dump_d = {}

        def DUMP(name, ap, shape, dt, bufs):
            if name not in dumps or name in dump_d:
                return
            d = nc.dram_tensor("dbg_" + name, shape, dt, kind="ExternalOutput").ap()
            dump_d[name] = d
            P.op("sp", lambda e: e.dma_start(out=d, in_=ap), reads=bufs, dma=P.dma_tl("dbg_" + name))

        def ACT(out, in_, func, reads, writes, **kw):
            P.op("act", lambda e: e.activation(out=out, in_=in_, func=func, **kw), reads, writes)

        def TTo(out, in0, in1, op, reads, writes, eng="dve"):
            P.op(eng, lambda e: e.tensor_tensor(out=out, in0=in0, in1=in1, op=op), reads, writes)

        def TS(out, in0, s1, s2, op0, op1, reads, writes, eng="dve"):
            P.op(eng, lambda e: e.tensor_scalar(out=out, in0=in0, scalar1=s1, scalar2=s2, op0=op0, op1=op1), reads, writes)

        def TS1(out, in_, s, op, reads, writes, eng="dve"):
            P.op(eng, lambda e: e.tensor_single_scalar(out=out, in_=in_, scalar=s, op=op), reads, writes)

        def STT(out, in0, s, in1, op0, op1, reads, writes, eng="dve"):
            P.op(eng, lambda e: e.scalar_tensor_tensor(out=out, in0=in0, scalar=s, in1=in1, op0=op0, op1=op1), reads, writes)

        def CP(out, in_, reads, writes, eng="dve"):
            if eng == "act":
                P.op("act", lambda e: e.copy(out=out, in_=in_), reads, writes)
            else:
                P.op(eng, lambda e: e.tensor_copy(out=out, in_=in_), reads, writes)

        def MM(groups, reads, writes):
            def f(e):
                ins = None
                for out, pairs in groups:
                    n = len(pairs)
                    for i, (l, r) in enumerate(pairs):
                        ins = e.matmul(out, lhsT=l, rhs=r, start=(i == 0), stop=(i == n - 1))
                return ins
            P.op("pe", f, reads, writes)

        def TR(items, reads, writes):
            def f(e):
                ins = None
                for o, i, idn in items:
                    ins = e.transpose(o, i, idn)
                return ins
            P.op("pe", f, reads, writes)

        def bc(ap, shape):
            return ap.to_broadcast(shape)

        identf = cst[:, 0, :]
        onesf = cst[:, 1, :]
        negonesf = cst[:, 2, :]
        Uf = cst[:, 3, :]
        maskneg = cst[:, 4, :]
        SLm = cst[:, 5, :]

        P.op("sp", lambda e: e.dma_start(out=spk[:], in_=sp_d), writes=[Bsp], dma=Dsetup)
        P.op("sp", lambda e: e.dma_start(out=sgst[:], in_=sgw_d), writes=[Bsgst], dma=P.dma_tl("sgst"))
        P.op("sp", lambda e: e.dma_start(out=bsst[:], in_=sgb_d), writes=[Bbsst], dma=P.dma_tl("bsst"))
        P.op("pool", lambda e: e.dma_start(out=wab[:].rearrange("p l k c -> p (l k) c"),
                                           in_=wab_d.rearrange("l (k p) c -> p (l k) c", p=128)),
             writes=[Bwab], dma=P.dma_tl("wab"))
        def cmem(i, v):
            P.op("pool", lambda e: e.memset(cst[:, i, :], v), writes=[Bc])

        def csel(i, pat, op, fill, cm):
            P.op("pool", lambda e: e.affine_select(out=cst[:, i, :], in_=cst[:, i, :], pattern=[[pat, 128]], compare_op=op,
                                                   fill=fill, base=0, channel_multiplier=cm), reads=[Bc], writes=[Bc])
        cmem(0, 1.0); csel(0, -1, ALU.is_equal, 0.0, 1)
        cmem(1, 1.0)
        cmem(2, -1.0)
        cmem(3, 1.0); csel(3, 1, ALU.is_ge, 0.0, -1)
        cmem(4, 0.0); csel(4, -1, ALU.is_ge, -30000.0, 1)
        cmem(5, 1.0); csel(5, -1, ALU.is_gt, 0.0, 1)
        P.op("dve", lambda e: e.memset(cbias[:, 0:1], EPS), writes=[Bcb])
        P.op("dve", lambda e: e.memset(cbias[:, 1:2], 128.0 * EPS), writes=[Bcb])
        P.op("dve", lambda e: e.memset(cbias[:, 2:3], 1.0), writes=[Bcb])
        CP(identb[:], identf, [Bc], [Bcb])
        CP(onesb[:], onesf, [Bc], [Bcb])
        TTo(wcT[:], sgst, bc(cst[:, 3:4, :], [128, 8, 128]), ALU.mult, [Bsgst, Bc], [Bsg])
        CP(bsrow[:], bsst, [Bbsst], [Bsg])
        SF.bufs[1] = Buf("scrf1")
        SF.bufs[3] = Buf("scrf3")
        P.op("dve", lambda e: e.memset(Sst[:], 0.0), writes=[b for r in BS for b in r])
        P.op("dve", lambda e: e.memset(Sbf[:], 0.0), writes=[b for r in BSb for b in r])
        P.op("dve", lambda e: e.memset(ghalo[:], 0.0), writes=[b for r in Bgh for b in r])
        P.op("dve", lambda e: e.memset(shalo[:], 0.0), writes=[b for r in Bsh for b in r])
        for l in range(NL):
            TS1(gsc[:, l, :], spk[:, SPO["g_a%d" % l]:SPO["g_a%d" % l] + 8], float(np.sqrt(128.0)), ALU.mult, [Bsp], [Bgsc])
            ACT(nexpa[:, l, :], spk[:, SPO["alog%d" % l]:SPO["alog%d" % l] + 8], AF.Exp, [Bsp], [Bgsc])
            TS1(nexpa[:, l, :], nexpa[:, l, :], -1.0, ALU.mult, [Bgsc], [Bgsc])

        sched = []

        def blk(W, l, r0, c0):
            return W[l, r0:r0 + D, c0:c0 + 512].rearrange("(k p) c -> p k c", p=128)

        v16 = lambda s: wbuf[:, s, :].rearrange("p (k c) -> p k c", k=16)
        v2 = lambda s: wbuf[:, s, 0:4096].rearrange("p (k c) -> p k c", k=2)
        INB = {"u": 0, "v": 1, "gb": 2, "gc": 3, "xin": 4, "q0": 5, "q1": 6, "k0": 7, "k1": 8, "v0": 9, "v1": 10, "z0": 11, "z1": 12}
        for t in range(NT):
            for l in range(NL):
                for key in ("u", "v", "gb", "gc", "xin", "q0", "k0", "v0", "z0", "q1", "k1", "v1", "z1"):
                    sched.append((key, blk(win_d, l, 0, INB[key] * 512), v16))
                for cb in range(4):
                    sched.append(("wo%d" % cb, blk(wo_d, l, 0, cb * 512), v16))
                for q in range(4):
                    for j in range(4):
                        sched.append(("w1_%d_%d" % (q, j), blk(w1_d, l, 0, q * 2048 + j * 512), v16))
                    for cb in range(4):
                        sched.append(("w2_%d_%d" % (q, cb), blk(w2_d, l, q * 2048, cb * 512), v16))
                for cb in range(4):
                    sched.append(("pg%d" % cb, blk(wpg_d, l, 0, cb * 512), v16))
                sched.append(("pp", wpp_d[l].rearrange("(k p) c -> p k c", p=128), v2))
        wstate = {"issued": 0, "next": 0}
        NBT = len(sched) // NT
        wcache = nc.dram_tensor("wcache", [NBT, 128, 8192], BF16).ap()
        Bcache = [Buf("wc%d" % i) for i in range(NBT)]
        Dst = [P.dma_tl("wst%d" % i) for i in range(NWB)]

        def WNEXT(key):
            i = wstate["next"]
            assert sched[i][0] == key, (sched[i][0], key)
            while wstate["issued"] < min(len(sched), i + NWB):
                j = wstate["issued"]
                s = j % NWB
                cid = j % NBT
                _, src, vf = sched[j]
                if j < NBT or NT == 1:
                    P.op("pool", lambda e, s=s, src=src, vf=vf: e.dma_start(out=vf(s), in_=src), writes=[Bw[s]], dma=Dw[s])
                    if NT > 1:
                        P.op("sp", lambda e, s=s, cid=cid: e.dma_start(out=wcache[cid], in_=wbuf[:, s, :]),
                             reads=[Bw[s]], writes=[Bcache[cid]], dma=Dst[s])
                else:
                    P.op("pool", lambda e, s=s, cid=cid: e.dma_start(out=wbuf[:, s, :], in_=wcache[cid]),
                         reads=[Bcache[cid]], writes=[Bw[s]], dma=Dw[s])
                wstate["issued"] += 1
            wstate["next"] += 1
            s = i % NWB
            return sched[i][2](s), Bw[s]

        pending = []

        def defer(th):
            pending.append(th)

        def tick(n=1):
            for _ in range(n):
                if pending:
                    pending.pop(0)()

        def flush():
            while pending:
                pending.pop(0)()

        def proj_fm(key, src, Bsrc, nk=16, pipelined=False):
            wv, bw = WNEXT(key)
            banks = [PS() for m in range(4)]
            if pipelined:
                for kc in range(nk):
                    def f(e, kc=kc):
                        ins = None
                        for m in range(4):
                            ins = e.matmul(psum[banks[m]][:], lhsT=wv[:, kc, m * 128:(m + 1) * 128], rhs=src[:, kc, :],
                                           start=(kc == 0), stop=(kc == nk - 1))
                        return ins
                    P.op("pe", f, [bw, Bsrc[kc]], [Bps[b] for b in banks])
                flush()
                return banks
            for m in range(4):
                b = banks[m]
                MM([(psum[b][:], [(wv[:, kc, m * 128:(m + 1) * 128], src[:, kc, :]) for kc in range(nk)])],
                   [bw] + Bsrc, [Bps[b]])
                if m == 0:
                    flush()
            return banks

        def rstd_from_ss(b, scale, n=TT):
            r, Br = SF.next()
            ACT(r[:, 0:n], psum[b][:, 0:n], AF.Ln, [Bps[b], Bcb], [Br], scale=scale, bias=cbias[:, 0:1])
            ACT(r[:, 0:n], r[:, 0:n], AF.Exp, [Br], [Br], scale=-0.5)
            return r, Br

        def rmsnorm_to(dst, Bdst, gcol, src_bufs_extra=()):
            b = PS()
            for kc in range(16):
                s, Bs_ = SB.next()
                ACT(s, hT[:, kc, :], AF.Square, [BhT[kc]], [Bs_])
                P.op("pe", lambda e, s=s, kc=kc, b=b: e.matmul(psum[b][:], lhsT=onesb[:], rhs=s, start=(kc == 0), stop=(kc == 15)),
                     [Bs_, Bcb], [Bps[b]])
            r, Br = rstd_from_ss(b, 1.0 / D)
            DUMP("rstd", r, [128, TT], F32, [Br])
            for kc in range(16):
                STT(dst[:, kc, :], hT[:, kc, :], spk[:, gcol + kc:gcol + kc + 1], r, ALU.mult, ALU.mult,
                    [BhT[kc], Bsp, Br], [Bdst[kc]])

        def group_norm_out(raw, Braw, gain_ap, Bgain, dst, Bdst_list, extra_in1=None, Bextra=()):
            s, Bs_ = SB.next()
            ACT(s, raw, AF.Square, [Braw], [Bs_])

            def partB():
                b = PS()
                MM([(psum[b][:], [(onesb[:], s)])], [Bs_, Bcb], [Bps[b]])
                r, Br = SF.next()
                ACT(r, psum[b][:], AF.Ln, [Bps[b], Bcb], [Br], scale=1.0, bias=cbias[:, 1:2])
                ACT(r, r, AF.Exp, [Br], [Br], scale=-0.5)
                STT(dst, raw, gain_ap, r, ALU.mult, ALU.mult, [Braw, Bgain, Br], Bdst_list)
            defer(partB)

        def load_x(t):
            for tc in range(4):
                j = tc % 2
                r0 = t * TT + tc * 128
                P.op("sp", lambda e, j=j, r0=r0: e.dma_start(out=stage[:, j, :], in_=x_d[r0:r0 + 128, :]),
                     writes=ByT[8 * j:8 * j + 8], dma=Dx[j])
                for g in range(4):
                    b = PS()
                    TR([(psum[b][:, k * 128:(k + 1) * 128], stage[:, j, (4 * g + k) * 128:(4 * g + k + 1) * 128], identf)
                        for k in range(4)], ByT[8 * j:8 * j + 8] + [Bc], [Bps[b]])
                    CP(hT[:, 4 * g:4 * g + 4, tc * 128:(tc + 1) * 128], psum[b][:].rearrange("p (k c) -> p k c", k=4),
                       [Bps[b]], BhT[4 * g:4 * g + 4], eng=("act" if g % 2 == 0 else "dve"))

        def store_out(t):
            b = PS()
            for kc in range(16):
                s, Bs_ = SB.next()
                ACT(s, hT[:, kc, :], AF.Square, [BhT[kc]], [Bs_])
                P.op("pe", lambda e, s=s, kc=kc, b=b: e.matmul(psum[b][:], lhsT=onesb[:], rhs=s, start=(kc == 0), stop=(kc == 15)),
                     [Bs_, Bcb], [Bps[b]])
            r, Br = rstd_from_ss(b, 1.0 / D)
            gcol = SPO["g_fin"]
            for kc in range(16):
                STT(hT[:, kc, :], hT[:, kc, :], spk[:, gcol + kc:gcol + kc + 1], r, ALU.mult, ALU.mult,
                    [BhT[kc], Bsp, Br], [BhT[kc]])
            for tc in range(4):
                j = tc % 2
                r0 = t * TT + tc * 128
                for g in range(4):
                    b = PS()
                    TR([(psum[b][:, k * 128:(k + 1) * 128], hT[:, 4 * g + k, tc * 128:(tc + 1) * 128], identf)
                        for k in range(4)], BhT[4 * g:4 * g + 4] + [Bc], [Bps[b]])
                    CP(stage[:, j, g * 512:(g + 1) * 512], psum[b][:], [Bps[b]], ByT[8 * j + 2 * g:8 * j + 2 * g + 2],
                       eng=("act" if g % 2 == 0 else "dve"))
                P.op("sp", lambda e, j=j, r0=r0: e.dma_start(out=out_d[r0:r0 + 128, :], in_=stage[:, j, :]),
                     reads=ByT[8 * j:8 * j + 8], dma=Do[j])

        def sgu(l):
            banks = proj_fm("u", xnT, BxnT, pipelined=True)
            for h in range(4):
                ACT(uT[:, h, :], psum[banks[h]][:], AF.Gelu, [Bps[banks[h]]], [BuT[h]])
            wv, bw = WNEXT("v")
            lng = spk[:, SPO["lng%d" % l]:SPO["lng%d" % l] + 512]
            lnb = spk[:, SPO["lnb%d" % l]:SPO["lnb%d" % l] + 512]
            for tc in range(4):
                b = PS()
                MM([(psum[b][:], [(xnT[:, kc, tc * 128:(tc + 1) * 128], wv[:, kc, :]) for kc in range(16)])],
                   [bw] + BxnT, [Bps[b]])
                vf, Bvf = SF.next()
                ACT(vf, psum[b][:], AF.Gelu, [Bps[b]], [Bvf])
                sq, Bsq = SF.next()
                ACT(sq, vf, AF.Square, [Bvf], [Bsq])
                v3 = vf.rearrange("p (h d) -> p h d", h=4)
                P.op("dve", lambda e, v3=v3: e.tensor_reduce(out=lnst[:, 0, :], in_=v3, axis=AX.X, op=ALU.add), [Bvf], [Blnst])
                P.op("dve", lambda e, sq=sq: e.tensor_reduce(out=lnst[:, 1, :], in_=sq.rearrange("p (h d) -> p h d", h=4), axis=AX.X, op=ALU.add), [Bsq], [Blnst])
                TS1(lnst[:, 2, :], lnst[:, 0, :], 1.0 / 128, ALU.mult, [Blnst], [Blnst])
                TTo(lnst[:, 3, :], lnst[:, 2, :], lnst[:, 2, :], ALU.mult, [Blnst], [Blnst])
                STT(lnst[:, 4, :], lnst[:, 1, :], 1.0 / 128, lnst[:, 3, :], ALU.mult, ALU.subtract, [Blnst], [Blnst])
                ACT(lnst[:, 5, :], lnst[:, 4, :], AF.Ln, [Blnst, Bcb], [Blnst], scale=1.0, bias=cbias[:, 0:1])
                ACT(lnst[:, 5, :], lnst[:, 5, :], AF.Exp, [Blnst], [Blnst], scale=-0.5)
                STT(lnst[:, 6, :], lnst[:, 2, :], -1.0, lnst[:, 5, :], ALU.mult, ALU.mult, [Blnst], [Blnst])
                for h in range(4):
                    TS(v3[:, h, :], v3[:, h, :], lnst[:, 5, h:h + 1], lnst[:, 6, h:h + 1], ALU.mult, ALU.add, [Bvf, Blnst], [Bvf])
                TTo(vf, vf, lng, ALU.mult, [Bvf, Bsp], [Bvf])
                TTo(vn[:, tc, :], vf, lnb, ALU.add, [Bvf, Bsp], [Bvn[tc]])
            for h in range(4):
                b = PS()
                groups = []
                for tc in range(4):
                    groups.append((psum[b][:, tc * 128:(tc + 1) * 128],
                                   [(vn[:, tc, h * 128:(h + 1) * 128], wcT[:, l * 4 + h, :]),
                                    (onesb[0:1, :], bsrow[0:1, (l * 4 + h) * 128:(l * 4 + h + 1) * 128])]))
                MM(groups, Bvn + [Bsg, Bcb], [Bps[b]])
                raw, Braw = SF.next()
                TTo(raw, psum[b][:], uT[:, h, :], ALU.mult, [Bps[b], BuT[h]], [Braw])
                group_norm_out(raw, Braw, gsc[:, l, h:h + 1], Bgsc, yT[:, h, :], [ByT[h]])

        def sconv(l):
            banks = proj_fm("gb", xnT, BxnT)
            for g in range(4):
                ACT(gbT[:, g, :], psum[banks[g]][:], AF.Copy, [Bps[banks[g]]], [BgbT[g]])
            banks = proj_fm("gc", xnT, BxnT)
            for g in range(4):
                ACT(gcT[:, g, :], psum[banks[g]][:], AF.Copy, [Bps[banks[g]]], [BgcT[g]])
            banks = proj_fm("xin", xnT, BxnT)
            cw = SPO["scw%d" % l]
            for g in range(4):
                z, Bz = PRE.next()
                CP(z[:, 0:2], shalo[:, l, g, :], [Bsh[l][g]], [Bz])
                TTo(z[:, 2:2 + TT], psum[banks[g]][:], gcT[:, g, :], ALU.mult, [Bps[banks[g]], BgcT[g]], [Bz])
                CP(shalo[:, l, g, :], z[:, TT:TT + 2], [Bz], [Bsh[l][g]])
                acc, Bacc = SF.next()
                TS1(acc, z[:, 0:TT], spk[:, cw + g * 3:cw + g * 3 + 1], ALU.mult, [Bz, Bsp], [Bacc])
                for j in (1, 2):
                    STT(acc, z[:, j:j + TT], spk[:, cw + g * 3 + j:cw + g * 3 + j + 1], acc, ALU.mult, ALU.add, [Bz, Bsp, Bacc], [Bacc])
                TTo(acc, acc, gbT[:, g, :], ALU.mult, [Bacc, BgbT[g]], [Bacc])
                group_norm_out(acc, Bacc, gsc[:, l, 4 + g:5 + g], Bgsc, yT[:, 4 + g, :], [ByT[4 + g]])

        def conv_silu(l, b, ci, dst, Bdst, norm):
            pr, Bpr = PRE.next()
            CP(pr[:, 0:3], ghalo[:, l, ci, :], [Bgh[l][ci]], [Bpr])
            ACT(pr[:, 3:3 + TT], psum[b][:], AF.Copy, [Bps[b]], [Bpr])
            CP(ghalo[:, l, ci, :], pr[:, TT:TT + 3], [Bpr], [Bgh[l][ci]])
            cw = SPO["gcw%d" % l] + ci * 4
            acc, Bacc = SF.next()
            TS1(acc, pr[:, 0:TT], spk[:, cw:cw + 1], ALU.mult, [Bpr, Bsp], [Bacc])
            for j in (1, 2, 3):
                STT(acc, pr[:, j:j + TT], spk[:, cw + j:cw + j + 1], acc, ALU.mult, ALU.add, [Bpr, Bsp, Bacc], [Bacc])
            if not norm:
                ACT(dst, acc, AF.Silu, [Bacc], [Bdst])
                return
            ACT(acc, acc, AF.Silu, [Bacc], [Bacc])
            s, Bs_ = SB.next()
            ACT(s, acc, AF.Square, [Bacc], [Bs_])

            def partB():
                bb = PS()
                MM([(psum[bb][:], [(onesb[:], s)])], [Bs_, Bcb], [Bps[bb]])
                r, Br = rstd_from_ss(bb, 1.0)
                TTo(dst, acc, r, ALU.mult, [Bacc, Br], [Bdst])
            defer(partB)

        def gdn_scalars(l):
            b = PS()
            groups = []
            for tc in range(4):
                groups.append((psum[b][:, tc * 16:(tc + 1) * 16],
                               [(xnT[:, kc, tc * 128:(tc + 1) * 128], wab[:, l, kc, :]) for kc in range(16)]))
            MM(groups, BxnT + [Bwab], [Bps[b]])
            ab = psum[b][:, 0:64].rearrange("p (c k) -> p c k", c=4)
            dtb = spk[:, SPO["dtb%d" % l]:SPO["dtb%d" % l] + 8]
            TTo(sm[:, T1], ab[:, :, 0:8], bc(dtb.unsqueeze(1), [128, 4, 8]), ALU.add, [Bps[b], Bsp], [Bsm[T1]])
            ACT(sm[:, T1], sm[:, T1], AF.Exp, [Bsm[T1]], [Bsm[T1]])
            ACT(sm[:, T1], sm[:, T1], AF.Ln, [Bsm[T1], Bcb], [Bsm[T1]], scale=1.0, bias=cbias[:, 2:3])
            TTo(sm[:, G_], sm[:, T1], bc(nexpa[:, l, :].unsqueeze(1), [128, 4, 8]), ALU.mult, [Bsm[T1], Bgsc], [Bsm[G_]])
            ACT(sm[:, T2], ab[:, :, 8:16], AF.Exp, [Bps[b], Bsm[T1]], [Bsm[T2]], scale=-1.0)
            TS1(sm[:, T2], sm[:, T2], 1.0, ALU.add, [Bsm[T2]], [Bsm[T2]])
            P.op("dve", lambda e: e.reciprocal(out=sm[:, BETA], in_=sm[:, T2]), [Bsm[T2]], [Bsm[BETA]])
            TS1(sm[:, NEGB], sm[:, BETA], -1.0, ALU.mult, [Bsm[BETA]], [Bsm[NEGB]])
            b2 = PS()
            groups = []
            for tc in range(4):
                groups.append((psum[b2][:, tc * 16:tc * 16 + 8], [(Uf, sm[:, G_, tc, :])]))
                groups.append((psum[b2][:, tc * 16 + 8:tc * 16 + 16], [(onesf, sm[:, G_, tc, :])]))
            MM(groups, [Bsm[G_], Bc], [Bps[b2]])
            CP(smps[:], psum[b2][:, 0:64].rearrange("p (c k) -> p c k", c=4), [Bps[b2]], [Bsmps])
            ACT(sm[:, EGC], smps[:, :, 0:8], AF.Exp, [Bsmps], [Bsm[EGC]])
            ACT(sm[:, EGL], smps[:, :, 8:16], AF.Exp, [Bsmps], [Bsm[EGL]])
            TTo(sm[:, T3], smps[:, :, 8:16], smps[:, :, 0:8], ALU.subtract, [Bsmps], [Bsm[T3]])
            ACT(sm[:, ETAIL], sm[:, T3], AF.Exp, [Bsm[T3]], [Bsm[ETAIL]])
            TTo(sm[:, BEGA], sm[:, BETA], sm[:, EGC], ALU.mult, [Bsm[BETA], Bsm[EGC]], [Bsm[BEGA]])

        def gdn_half(l, hh):
            h0 = 4 * hh
            for key, dst, Bdst, cbase, norm in (("q%d" % hh, qT, BqT, 0, True), ("k%d" % hh, kT, BkT, 8, True),
                                                ("v%d" % hh, vT, BvT, 16, False)):
                banks = proj_fm(key, xnT, BxnT)
                for hi in range(4):
                    conv_silu(l, banks[hi], cbase + h0 + hi, dst[:, hi, :], Bdst[hi], norm)
            banks = proj_fm("z%d" % hh, xnT, BxnT)
            for hi in range(4):
                ACT(zsT[:, hi, :], psum[banks[hi]][:], AF.Silu, [Bps[banks[hi]]], [BzsT[hi]])
            flush()
            hs = slice(h0, h0 + 4)
            v4 = lambda b: psum[b][:].rearrange("p (h c) -> p h c", h=4)
            v4b = lambda b: psb[b][:, 0:512].rearrange("p (h c) -> p h c", h=4)
            for c in range(4):
                cs = slice(c * 128, (c + 1) * 128)
                sc = lambda idx: bc(sm[:, idx, c, hs].unsqueeze(2), [128, 4, 128])
                r4 = lambda ap: ap.rearrange("p (h c) -> p h c", h=4)
                GU_, BGU = SF.next(); GU = r4(GU_)
                EROW_, BEROW = SF.next(); EROW = r4(EROW_)
                tmpf_, Btmpf = SF.next(); tmpf = r4(tmpf_)
                ESL_, BESL = SF.next(); ESL = r4(ESL_)
                TTo(GU, bc(cst[:, 3:4, :], [128, 4, 128]), sc(G_), ALU.mult, [Bc, Bsm[G_]], [BGU])
                b = PS()
                MM([(psum[b][:, hi * 128:(hi + 1) * 128],
                     [(GU[:, hi, :], onesf), (negonesf, GU[:, hi, :]), (identf, maskneg)]) for hi in range(4)],
                   [BGU, Bc], [Bps[b]])
                ACT(Em[:], v4(b), AF.Exp, [Bps[b]], [BEm])
                tick()
                b = PS()
                MM([(psum[b][:, hi * 128:(hi + 1) * 128], [(onesf, GU[:, hi, :])]) for hi in range(4)], [BGU, Bc], [Bps[b]])
                ACT(EROW, v4(b), AF.Exp, [Bps[b]], [BEROW])
                TTo(ESL, Em[:], bc(cst[:, 5:6, :], [128, 4, 128]), ALU.mult, [BEm, Bc], [BESL])
                STT(qeT[:], qT[:, :, cs], float(128.0 ** -0.5), EROW, ALU.mult, ALU.mult, BqT + [BEROW], [BqeT])
                b = PS()
                TR([(psb[b][:, hi * 128:(hi + 1) * 128], kT[:, hi, cs], identb[:]) for hi in range(4)] +
                   [(psb[b][:, 512 + hi * 128:512 + (hi + 1) * 128], vT[:, hi, cs], identb[:]) for hi in range(4)],
                   BkT + BvT + [Bcb], [Bps[b]])
                kk = psb[b][:, 0:512].rearrange("p (h c) -> p h c", h=4)
                vv = psb[b][:, 512:1024].rearrange("p (h c) -> p h c", h=4)
                TTo(kbe[:], kk, sc(BEGA), ALU.mult, [Bps[b], Bsm[BEGA]], [Bkbe])
                TTo(ktl[:], kk, sc(ETAIL), ALU.mult, [Bps[b], Bsm[ETAIL]], [Bktl])
                TTo(vb[:], vv, sc(BETA), ALU.mult, [Bps[b], Bsm[BETA]], [Bvb])
                b = PS()
                MM([(psum[b][:, hi * 128:(hi + 1) * 128], [(kT[:, hi, cs], kT[:, hi, cs])]) for hi in range(4)], BkT, [Bps[b]])
                TTo(tmpf, v4(b), sc(NEGB), ALU.mult, [Bps[b], Bsm[NEGB]], [Btmpf])
                TTo(Wm[:], tmpf, ESL, ALU.mult, [Btmpf, BESL], [BWm])
                b = PS()
                MM([(psum[b][:, hi * 128:(hi + 1) * 128], [(qT[:, hi, cs], kT[:, hi, cs])]) for hi in range(4)], BqT + BkT, [Bps[b]])
                STT(intra[:], v4(b), float(128.0 ** -0.5), Em[:], ALU.mult, ALU.mult, [Bps[b], BEm], [Bintra])
                b = PS()
                TR([(psum[b][:, hi * 128:(hi + 1) * 128], Wm[:, hi, :], identf) for hi in range(4)], [BWm, Bc], [Bps[b]])
                b2 = PS()
                TR([(psb[b2][:, hi * 128:(hi + 1) * 128], intra[:, hi, :], identb[:]) for hi in range(4)], [Bintra, Bcb], [Bps[b2]])
                P0, BP0 = Pm[0]
                ACT(P0[:], v4(b), AF.Copy, [Bps[b]], [BP0])
                ACT(intraT[:], psb[b2][:, 0:512].rearrange("p (h c) -> p h c", h=4), AF.Copy, [Bps[b2]], [BintraT])
                R0, BR0 = Rm[0]
                TTo(R0[:], P0[:], bc(cst[:, 0:1, :], [128, 4, 128]), ALU.add, [BP0, Bc], [BR0])
                Pc, BPc = P0, BP0
                PTc, BPTc = Wm, BWm
                Rc, BRc = R0, BR0
                for lev in range(1, 7):
                    PTn, BPTn = PTm[lev % 2]
                    b = PS()
                    MM([(psum[b][:, hi * 128:(hi + 1) * 128], [(Pc[:, hi, :], PTc[:, hi, :])]) for hi in range(4)], [BPc, BPTc], [Bps[b]])
                    ACT(PTn[:], v4(b), AF.Copy, [Bps[b]], [BPTn])
                    if lev < 6:
                        Pn, BPn = Pm[lev % 2]
                        b2 = PS()
                        MM([(psum[b2][:, hi * 128:(hi + 1) * 128], [(PTc[:, hi, :], Pc[:, hi, :])]) for hi in range(4)], [BPc, BPTc], [Bps[b2]])
                        CP(Pn[:], v4(b2), [Bps[b2]], [BPn])
                    Rn, BRn = Rm[lev % 2]
                    b3 = PS()
                    MM([(psum[b3][:, hi * 128:(hi + 1) * 128], [(PTn[:, hi, :], Rc[:, hi, :])]) for hi in range(4)], [BPTn, BRc], [Bps[b3]])
                    TTo(Rn[:], v4(b3), Rc[:], ALU.add, [Bps[b3], BRc], [BRn])
                    PTc, BPTc = PTn, BPTn
                    if lev < 6:
                        Pc, BPc = Pn, BPn
                    Rc, BRc = Rn, BRn
                ACT(Rbf[:], Rc[:], AF.Copy, [BRc], [BRbf])
                b = PS()
                MM([(psum[b][:, hi * 128:(hi + 1) * 128], [(kbe[:, hi, :], Rbf[:, hi, :])]) for hi in range(4)], [Bkbe, BRbf], [Bps[b]])
                ACT(nkcd[:], v4(b), AF.Copy, [Bps[b]], [Bnkcd], scale=-1.0)
                b = PS()
                MM([(psum[b][:, hi * 128:(hi + 1) * 128],
                     [(Rbf[:, hi, :], vb[:, hi, :]), (nkcd[:, hi, :], Sbf[:, l, h0 + hi, :])]) for hi in range(4)],
                   [BRbf, Bvb, Bnkcd, BSb[l][hh]], [Bps[b]])
                ACT(vnew[:], v4(b), AF.Copy, [Bps[b]], [Bvnew])
                bo = PS()
                MM([(psum[bo][:, hi * 128:(hi + 1) * 128],
                     [(Sbf[:, l, h0 + hi, :], qeT[:, hi, :]), (vnew[:, hi, :], intraT[:, hi, :])]) for hi in range(4)],
                   [BSb[l][hh], BqeT, Bvnew, BintraT], [Bps[bo]])
                bs = PS()
                MM([(psum[bs][:, hi * 128:(hi + 1) * 128], [(ktl[:, hi, :], vnew[:, hi, :])]) for hi in range(4)],
                   [Bktl, Bvnew], [Bps[bs]])
                TTo(Sst[:, l, hs, :], Sst[:, l, hs, :], sc(EGL), ALU.mult, [BS[l][hh], Bsm[EGL]], [BS[l][hh]])
                TTo(Sst[:, l, hs, :], Sst[:, l, hs, :], v4(bs), ALU.add, [BS[l][hh], Bps[bs]], [BS[l][hh]])
                ACT(Sbf[:, l, hs, :], Sst[:, l, hs, :], AF.Copy, [BS[l][hh]], [BSb[l][hh]])
                osb, Bosb = SF.next()
                ACT(osb, psum[bo][:], AF.Copy, [Bps[bo]], [Bosb])
                s, Bs_ = SB.next()
                ACT(s, osb, AF.Square, [Bosb], [Bs_])

                def partB(osb=osb, Bosb=Bosb, s=s, Bs_=Bs_, cs=cs):
                    bb = PS()
                    MM([(psum[bb][:], [(onesb[:], s)])], [Bs_, Bcb], [Bps[bb]])
                    r, Br = SF.next()
                    ACT(r, psum[bb][:], AF.Ln, [Bps[bb], Bcb], [Br], scale=1.0 / 128, bias=cbias[:, 0:1])
                    ACT(r, r, AF.Exp, [Br], [Br], scale=-0.5)
                    TTo(osb, osb, r, ALU.mult, [Bosb, Br], [Bosb])
                    gg = SPO["g_gdn%d" % l]
                    STT(yT[:, 8 + h0:12 + h0, cs], osb.rearrange("p (h c) -> p h c", h=4), spk[:, gg:gg + 1], zsT[:, :, cs],
                        ALU.mult, ALU.mult, [Bosb, Bsp] + BzsT, ByT[8 + h0:12 + h0])
                defer(partB)
            flush()

        def out_proj(l):
            flush()
            for cb in range(4):
                banks = proj_fm("wo%d" % cb, yT, ByT)
                for m in range(4):
                    kc = cb * 4 + m
                    TTo(hT[:, kc, :], hT[:, kc, :], psum[banks[m]][:], ALU.add, [BhT[kc], Bps[banks[m]]], [BhT[kc]])

        def ffn(l):
            rmsnorm_to(xnT, BxnT, SPO["g_ffn%d" % l])
            for q in range(4):
                for j in range(4):
                    banks = proj_fm("w1_%d_%d" % (q, j), xnT, BxnT, pipelined=(q == 0 and j == 0))
                    for m in range(4):
                        kc = j * 4 + m
                        t_, Bt_ = SF.next()
                        ACT(t_, psum[banks[m]][:], AF.Relu, [Bps[banks[m]]], [Bt_])
                        TTo(yT[:, kc, :], t_, psum[banks[m]][:], ALU.mult, [Bt_, Bps[banks[m]]], [ByT[kc]])
                for cb in range(4):
                    banks = proj_fm("w2_%d_%d" % (q, cb), yT, ByT)
                    for m in range(4):
                        kc = cb * 4 + m
                        TTo(hT[:, kc, :], hT[:, kc, :], psum[banks[m]][:], ALU.add, [BhT[kc], Bps[banks[m]]], [BhT[kc]])

        def ple(t, l):
            rmsnorm_to(xnT, BxnT, SPO["g_ple%d" % l])
            r0 = t * TT
            b = PS()
            b2 = PS()
            for half in range(2):
                pg_, Bpg = SF.next()
                pg = pg_.rearrange("p (c f) -> p c f", c=2)
                rr = r0 + half * 256
                P.op("sp", lambda e, pg=pg, rr=rr: e.dma_start(out=pg, in_=p_d[l, rr:rr + 256, :].rearrange("(c p) f -> p c f", p=128)),
                     writes=[Bpg], dma=Dp[half])
                TR([(psum[b][:, (2 * half + c) * 128:(2 * half + c + 1) * 128], pg[:, c, 0:128], identf) for c in range(2)], [Bpg, Bc], [Bps[b]])
                TR([(psum[b2][:, (2 * half + c) * 128:(2 * half + c + 1) * 128], pg[:, c, 128:256], identf) for c in range(2)], [Bpg, Bc], [Bps[b2]])
            ACT(pT[:, 0, :], psum[b][:], AF.Copy, [Bps[b]], [BpT])
            ACT(pT[:, 1, :], psum[b2][:], AF.Copy, [Bps[b2]], [BpT])
            gates = []
            for cb in range(4):
                banks = proj_fm("pg%d" % cb, xnT, BxnT, pipelined=(cb == 0))
                for m in range(4):
                    g_, Bg_ = (yT[:, cb * 4 + m, :], ByT[cb * 4 + m])
                    ACT(g_, psum[banks[m]][:], AF.Sigmoid, [Bps[banks[m]]], [Bg_])
            wv, bw = WNEXT("pp")
            for kc in range(16):
                b = PS()
                MM([(psum[b][:], [(wv[:, k, kc * 128:(kc + 1) * 128], pT[:, k, :]) for k in range(2)])], [bw, BpT], [Bps[b]])
                t_, Bt_ = SF.next()
                TTo(t_, psum[b][:], yT[:, kc, :], ALU.mult, [Bps[b], ByT[kc]], [Bt_])
                TTo(hT[:, kc, :], hT[:, kc, :], t_, ALU.add, [BhT[kc], Bt_], [BhT[kc]])

        for t in range(NT):
            load_x(t)
            DUMP("h0", hT[:], [128, 16, TT], F32, BhT)
            for l in range(NL):
                rmsnorm_to(xnT, BxnT, SPO["g_mix%d" % l])
                DUMP("xn@%d" % l, xnT[:], [128, 16, TT], BF16, BxnT)
                sgu(l)
                sconv(l)
                gdn_scalars(l)
                DUMP("sm@%d" % l, sm[:], [128, 12, 4, 8], F32, Bsm)
                gdn_half(l, 0)
                gdn_half(l, 1)
                DUMP("y@%d" % l, yT[:], [128, 16, TT], BF16, ByT)
                out_proj(l)
                DUMP("h1@%d" % l, hT[:], [128, 16, TT], F32, BhT)
                ffn(l)
                DUMP("h2@%d" % l, hT[:], [128, 16, TT], F32, BhT)
                ple(t, l)
                DUMP("h3@%d" % l, hT[:], [128, 16, TT], F32, BhT)
            store_out(t)
        assert wstate["next"] == len(sched), (wstate, len(sched))
        P.wait_all("sp", ByT)
        print("ops recorded:", P.nops)
        P.emit()
    return nc


def host_pack(inp, NL=2):
    SPO, NSP = sp_layout(NL)
    sp = np.zeros((128, NSP), np.float32)
    fm = lambda v: np.ascontiguousarray(v.reshape(-1, 128).T)
    for l in range(NL):
        sp[:, SPO["g_mix%d" % l]:][:, :16] = fm(inp["norm_mix"][l])
        sp[:, SPO["g_ffn%d" % l]:][:, :16] = fm(inp["norm_ffn"][l])
        sp[:, SPO["g_ple%d" % l]:][:, :16] = fm(inp["norm_ple"][l])
        sp[:, SPO["g_a%d" % l]:][:, :4] = fm(inp["out_norm_a"][l])
        sp[:, SPO["g_b%d" % l]:][:, :4] = fm(inp["out_norm_b"][l])
        sp[:, SPO["g_gdn%d" % l]:][:, :1] = fm(inp["gdn_norm"][l])
        sp[:, SPO["scw%d" % l]:][:, :12] = inp["sc_conv"][l].reshape(3, 4, 128).transpose(2, 1, 0).reshape(128, 12)
        sp[:, SPO["gcw%d" % l]:][:, :96] = inp["gdn_conv"][l].reshape(4, 24, 128).transpose(2, 1, 0).reshape(128, 96)
        sp[:, SPO["alog%d" % l]:][:, :8] = np.broadcast_to(inp["gdn_a_log"][l], (128, 8))
        sp[:, SPO["dtb%d" % l]:][:, :8] = np.broadcast_to(inp["gdn_dt_bias"][l], (128, 8))
        sp[:, SPO["lng%d" % l]:][:, :512] = np.broadcast_to(inp["sg_ln_g"][l], (128, 512))
        sp[:, SPO["lnb%d" % l]:][:, :512] = np.broadcast_to(inp["sg_ln_b"][l], (128, 512))
    sp[:, SPO["g_fin"]:][:, :16] = fm(inp["norm_final"])
    sgwT = np.ascontiguousarray(inp["sg_w"].transpose(3, 0, 1, 2).reshape(128, 8, 128))
    sgb = np.ascontiguousarray(inp["sg_b"].reshape(1, 1024))
    wab = np.ascontiguousarray(inp["w_in"][:, :, 6656:6672])
    return sp, sgwT, sgb, wab


CORES = [0, 2, 4, 6]


def kernel(**inp):
    inp = {k: np.asarray(v) for k, v in inp.items()}
    sp, sgwT, sgb, wab = host_pack(inp)
    nc = build()
    common = {"w_in": inp["w_in"], "w_ab": wab, "w_o": inp["w_o"], "w_ff1": inp["w_ff1"], "w_ff2": inp["w_ff2"],
              "w_ple_gate": inp["w_ple_gate"], "w_ple_proj": inp["w_ple_proj"], "sp": sp, "sgwT": sgwT, "sgb": sgb}
    in_maps = []
    for b in range(4):
        m = dict(common)
        m["x"] = np.ascontiguousarray(inp["x"][b])
        m["p"] = np.ascontiguousarray(inp["p"][:, b])
        in_maps.append(m)
    res = run_bass_kernel_spmd(nc, in_maps, core_ids=CORES)
    return np.stack([np.asarray(res.results[b]["out"]) for b in range(4)], axis=0).astype(np.float32)
```

```python
import numpy as np
from contextlib import ExitStack
import concourse.bass as bass
import concourse.mybir as mybir
from concourse.bass_utils import run_bass_kernel_spmd

F32 = mybir.dt.float32
BF16 = mybir.dt.bfloat16
AF = mybir.ActivationFunctionType
ALU = mybir.AluOpType
AX = mybir.AxisListType

D = 2048
SEQ = 4096
TT = 512
EPS = 1e-6
IN_COLS = 6672


class TL:
    def __init__(self, sem, unit, name):
        self.sem, self.unit, self.count, self.name = sem, unit, 0, name


class Buf:
    def __init__(self, name):
        self.name, self.w, self.r = name, None, {}


class Prog:
    ENGS = ("pe", "act", "dve", "pool", "sp")

    def __init__(self, nc, stack):
        self.nc, self.stack = nc, stack
        self.ops = {e: [] for e in self.ENGS}
        self.tl = {e: TL(stack.enter_context(nc.semaphore("tl_" + e)), 1, e) for e in self.ENGS}
        self.seen = {e: {} for e in self.ENGS}
        self.nops = 0

    def dma_tl(self, name):
        return TL(self.stack.enter_context(self.nc.semaphore("d_" + name)), 16, name)

    def _deps(self, eng, reads, writes):
        deps = {}
        me = self.tl[eng]

        def add(t):
            if t is None:
                return
            tl, c = t
            if tl is me and eng == "pe":
                return
            if deps.get(tl, 0) < c:
                deps[tl] = c

        for b in reads:
            add(b.w)
        for b in writes:
            add(b.w)
            for tl, c in b.r.items():
                add((tl, c))
        waits = []
        seen = self.seen[eng]
        for tl, c in deps.items():
            if seen.get(tl, 0) < c:
                seen[tl] = c
                waits.append((tl.sem, c * tl.unit))
        return waits

    def op(self, eng, fn, reads=(), writes=(), dma=None):
        waits = self._deps(eng, reads, writes)
        tl = dma if dma is not None else self.tl[eng]
        tl.count += 1
        cnt = tl.count
        sem, unit = tl.sem, tl.unit

        def run(e, fn=fn, waits=waits, sem=sem, unit=unit):
            for s, v in waits:
                e.wait_ge(s, v)
            fn(e).then_inc(sem, unit)

        self.ops[eng].append(run)
        self.nops += 1
        for b in reads:
            b.r[tl] = cnt
        for b in writes:
            b.w = (tl, cnt)
            b.r = {}

    def wait_all(self, eng, bufs):
        waits = self._deps(eng, (), bufs)

        def run(e, waits=waits):
            for s, v in waits:
                e.wait_ge(s, v)

        self.ops[eng].append(run)

    def emit(self):
        with self.nc.Block() as block:
            @block.tensor
            def _(e):
                for f in self.ops["pe"]:
                    f(e)

            @block.scalar
            def _(e):
                for f in self.ops["act"]:
                    f(e)

            @block.vector
            def _(e):
                for f in self.ops["dve"]:
                    f(e)

            @block.gpsimd
            def _(e):
                for f in self.ops["pool"]:
                    f(e)

            @block.sync
            def _(e):
                for f in self.ops["sp"]:
                    f(e)


class Ring:
    def __init__(self, aps, name):
        self.aps = aps
        self.bufs = [Buf("%s%d" % (name, i)) for i in range(len(aps))]
        self.i = 0

    def next(self):
        i = self.i
        self.i = (i + 1) % len(self.aps)
        return self.aps[i], self.bufs[i]


def sp_layout(NL):
    off = {}
    o = 0

    def add(name, n):
        nonlocal o
        off[name] = o
        o += n

    for l in range(NL):
        add("g_mix%d" % l, 16)
        add("g_ffn%d" % l, 16)
        add("g_ple%d" % l, 16)
        add("g_a%d" % l, 4)
        add("g_b%d" % l, 4)
        add("g_gdn%d" % l, 1)
        add("scw%d" % l, 12)
        add("gcw%d" % l, 96)
        add("alog%d" % l, 8)
        add("dtb%d" % l, 8)
        add("lng%d" % l, 512)
        add("lnb%d" % l, 512)
    add("g_fin", 16)
    return off, o


def build(NT=8, NL=2, dumps=()):
    nc = bass.Bass("TRN2", target_bir_lowering=False)
    SPO, NSP = sp_layout(NL)
    dt_in = lambda name, shape: nc.dram_tensor(name, shape, F32, kind="ExternalInput").ap()
    x_d = dt_in("x", [SEQ, D])
    p_d = dt_in("p", [2, SEQ, 256])
    win_d = dt_in("w_in", [2, D, IN_COLS])
    wab_d = dt_in("w_ab", [2, D, 16])
    wo_d = dt_in("w_o", [2, D, D])
    w1_d = dt_in("w_ff1", [2, D, 4 * D])
    w2_d = dt_in("w_ff2", [2, 4 * D, D])
    wpg_d = dt_in("w_ple_gate", [2, D, D])
    wpp_d = dt_in("w_ple_proj", [2, 256, D])
    sp_d = dt_in("sp", [128, NSP])
    sgw_d = dt_in("sgwT", [128, 8, 128])
    sgb_d = dt_in("sgb", [1, 1024])
    out_d = nc.dram_tensor("out", [SEQ, D], F32, kind="ExternalOutput").ap()

    with ExitStack() as st:
        P = Prog(nc, st)
        sbt = lambda name, shape, dt: st.enter_context(nc.sbuf_tensor(name, shape, dt))
        pst_ = lambda name, shape, dt: st.enter_context(nc.psum_tensor(name, shape, dt))

        hT = sbt("hT", [128, 16, TT], F32)
        BhT = [Buf("hT%d" % i) for i in range(16)]
        xnT = sbt("xnT", [128, 16, TT], BF16)
        BxnT = [Buf("xnT%d" % i) for i in range(16)]
        yT = sbt("yT", [128, 16, TT], BF16)
        ByT = [Buf("yT%d" % i) for i in range(16)]
        stage = yT[:].rearrange("p a b -> p (a b)").bitcast(F32).rearrange("p (j c) -> p j c", j=2)
        NWB = 3
        wbuf = sbt("wbuf", [128, NWB, 8192], BF16)
        Bw = [Buf("w%d" % i) for i in range(NWB)]
        Dw = [P.dma_tl("w%d" % i) for i in range(NWB)]
        spk = sbt("spk", [128, NSP], F32)
        Bsp = Buf("spk")
        cst = sbt("cst", [128, 6, 128], F32)
        Bc = Buf("cst")
        cbias = sbt("cbias", [128, 4], F32)
        identb = sbt("identb", [128, 128], BF16)
        onesb = sbt("onesb", [128, 128], BF16)
        Bcb = Buf("cstb")
        wcT = sbt("wcT", [128, 8, 128], BF16)
        bsrow = sbt("bsrow", [1, 1024], BF16)
        Bsg = Buf("sg")
        wab = sbt("wab", [128, 2, 16, 16], BF16)
        Bwab = Buf("wab")
        gsc = sbt("gsc", [128, 2, 8], F32)
        Bgsc = Buf("gsc")
        nexpa = sbt("nexpa", [128, 2, 8], F32)
        Sst = sbt("Sst", [128, 2, 8, 128], F32)
        Sbf = sbt("Sbf", [128, 2, 8, 128], BF16)
        BS = [[Buf("S%d_%d" % (l, hh)) for hh in range(2)] for l in range(2)]
        BSb = [[Buf("Sb%d_%d" % (l, hh)) for hh in range(2)] for l in range(2)]
        ghalo = sbt("ghalo", [128, 2, 24, 3], F32)
        Bgh = [[Buf("gh%d_%d" % (l, c)) for c in range(24)] for l in range(2)]
        shalo = sbt("shalo", [128, 2, 4, 2], F32)
        Bsh = [[Buf("sh%d_%d" % (l, c)) for c in range(4)] for l in range(2)]
        scrf_t = sbt("scrf", [128, 6, TT], F32)
        SF = Ring([scrf_t[:, i, :] for i in range(6)], "scrf")
        sgst = scrf_t[:, 0:2, :].rearrange("p a (b c) -> p (a b) c", c=128)
        Bsgst = SF.bufs[0]
        SF.bufs[1] = SF.bufs[0]
        bsst = scrf_t[0:1, 2:4, :].rearrange("p a b -> p (a b)")
        Bbsst = SF.bufs[2]
        SF.bufs[3] = SF.bufs[2]
        scrb_t = sbt("scrb", [128, 4, TT], BF16)
        SB = Ring([scrb_t[:, i, :] for i in range(4)], "scrb")
        pre_t = sbt("pre", [128, 2, TT + 4], F32)
        PRE = Ring([pre_t[:, i, :] for i in range(2)], "pre")
        qT = sbt("qT", [128, 4, TT], BF16); BqT = [Buf("qT%d" % i) for i in range(4)]
        kT = sbt("kT", [128, 4, TT], BF16); BkT = [Buf("kT%d" % i) for i in range(4)]
        uT = qT; BuT = BqT
        vn = kT; Bvn = BkT
        gbT = uT; BgbT = BuT
        gcT = vn; BgcT = Bvn
        vT = sbt("vT", [128, 4, TT], BF16); BvT = [Buf("vT%d" % i) for i in range(4)]
        zsT = sbt("zsT", [128, 4, TT], BF16); BzsT = [Buf("zsT%d" % i) for i in range(4)]
        Dp = [P.dma_tl("p%d" % i) for i in range(2)]
        pT = sbt("pT", [128, 2, TT], BF16); BpT = Buf("pT")
        sm = sbt("sm", [128, 12, 4, 8], F32)
        Bsm = [Buf("sm%d" % i) for i in range(12)]
        G_, BETA, NEGB, EGC, EGL, ETAIL, BEGA, T1, T2, GC, GL, T3 = range(12)
        smps = sbt("smps", [128, 4, 16], F32); Bsmps = Buf("smps")
        lnst = sbt("lnst", [128, 8, 4], F32); Blnst = Buf("lnst")
        mk = lambda name, dt: (sbt(name, [128, 4, 128], dt), Buf(name))
        Em, BEm = mk("Em", F32)
        qeT, BqeT = mk("qeT", BF16)
        kbe, Bkbe = mk("kbe", BF16)
        ktl, Bktl = mk("ktl", BF16)
        vb, Bvb = mk("vb", BF16)
        Wm, BWm = mk("Wm", F32)
        Rbf, BRbf = mk("Rbf", BF16)
        intra, Bintra = mk("intra", BF16)
        intraT, BintraT = mk("intraT", BF16)
        Pm = [mk("Pm%d" % i, F32) for i in range(2)]
        PTm = [mk("PTm%d" % i, F32) for i in range(2)]
        Rm = [mk("Rm%d" % i, F32) for i in range(2)]
        nkcd, Bnkcd = mk("nkcd", BF16)
        vnew, Bvnew = mk("vnew", BF16)
        psum = [pst_("ps%d" % i, [128, 512], F32) for i in range(8)]
        Bps = [Buf("ps%d" % i) for i in range(8)]
        psb = [psum[i][:].bitcast(BF16) for i in range(8)]
        psi = [0]

        def PS():
            i = psi[0]
            psi[0] = (i + 1) % 8
            return i

        Dx = [P.dma_tl("x%d" % j) for j in range(2)]
        Do = [P.dma_tl("o%d" % j) for j in range(2)]
        Dsetup = P.dma_tl("setup")

        dump_d = {}

        def DUMP(name, ap, shape, dt, bufs):
            if name not in dumps or name in dump_d:
                return
            d = nc.dram_tensor("dbg_" + name, shape, dt, kind="ExternalOutput").ap()
            dump_d[name] = d
            P.op("sp", lambda e: e.dma_start(out=d, in_=ap), reads=bufs, dma=P.dma_tl("dbg_" + name))

        def ACT(out, in_, func, reads, writes, **kw):
            P.op("act", lambda e: e.activation(out=out, in_=in_, func=func, **kw), reads, writes)

        def TTo(out, in0, in1, op, reads, writes, eng="dve"):
            P.op(eng, lambda e: e.tensor_tensor(out=out, in0=in0, in1=in1, op=op), reads, writes)

        def TS(out, in0, s1, s2, op0, op1, reads, writes, eng="dve"):
            P.op(eng, lambda e: e.tensor_scalar(out=out, in0=in0, scalar1=s1, scalar2=s2, op0=op0, op1=op1), reads, writes)

        def TS1(out, in_, s, op, reads, writes, eng="dve"):
            P.op(eng, lambda e: e.tensor_single_scalar(out=out, in_=in_, scalar=s, op=op), reads, writes)

        def STT(out, in0, s, in1, op0, op1, reads, writes, eng="dve"):
            P.op(eng, lambda e: e.scalar_tensor_tensor(out=out, in0=in0, scalar=s, in1=in1, op0=op0, op1=op1), reads, writes)

        def CP(out, in_, reads, writes, eng="dve"):
            if eng == "act":
                P.op("act", lambda e: e.copy(out=out, in_=in_), reads, writes)
            else:
                P.op(eng, lambda e: e.tensor_copy(out=out, in_=in_), reads, writes)

        def MM(groups, reads, writes):
            def f(e):
                ins = None
                for out, pairs in groups:
                    n = len(pairs)
                    for i, (l, r) in enumerate(pairs):
                        ins = e.matmul(out, lhsT=l, rhs=r, start=(i == 0), stop=(i == n - 1))
                return ins
            P.op("pe", f, reads, writes)

        def TR(items, reads, writes):
            def f(e):
                ins = None
                for o, i, idn in items:
                    ins = e.transpose(o, i, idn)
                return ins
            P.op("pe", f, reads, writes)

        def bc(ap, shape):
            return ap.to_broadcast(shape)

        identf = cst[:, 0, :]
        onesf = cst[:, 1, :]
        negonesf = cst[:, 2, :]
        Uf = cst[:, 3, :]
        maskneg = cst[:, 4, :]
        SLm = cst[:, 5, :]

        P.op("sp", lambda e: e.dma_start(out=spk[:], in_=sp_d), writes=[Bsp], dma=Dsetup)
        P.op("sp", lambda e: e.dma_start(out=sgst[:], in_=sgw_d), writes=[Bsgst], dma=P.dma_tl("sgst"))
        P.op("sp", lambda e: e.dma_start(out=bsst[:], in_=sgb_d), writes=[Bbsst], dma=P.dma_tl("bsst"))
        P.op("pool", lambda e: e.dma_start(out=wab[:].rearrange("p l k c -> p (l k) c"),
                                           in_=wab_d.rearrange("l (k p) c -> p (l k) c", p=128)),
             writes=[Bwab], dma=P.dma_tl("wab"))
        def cmem(i, v):
            P.op("pool", lambda e: e.memset(cst[:, i, :], v), writes=[Bc])

        def csel(i, pat, op, fill, cm):
            P.op("pool", lambda e: e.affine_select(out=cst[:, i, :], in_=cst[:, i, :], pattern=[[pat, 128]], compare_op=op,
                                                   fill=fill, base=0, channel_multiplier=cm), reads=[Bc], writes=[Bc])
        cmem(0, 1.0); csel(0, -1, ALU.is_equal, 0.0, 1)
        cmem(1, 1.0)
        cmem(2, -1.0)
        cmem(3, 1.0); csel(3, 1, ALU.is_ge, 0.0, -1)
        cmem(4, 0.0); csel(4, -1, ALU.is_ge, -30000.0, 1)
        cmem(5, 1.0); csel(5, -1, ALU.is_gt, 0.0, 1)
        P.op("dve", lambda e: e.memset(cbias[:, 0:1], EPS), writes=[Bcb])
        P.op("dve", lambda e: e.memset(cbias[:, 1:2], 128.0 * EPS), writes=[Bcb])
        P.op("dve", lambda e: e.memset(cbias[:, 2:3], 1.0), writes=[Bcb])
        CP(identb[:], identf, [Bc], [Bcb])
        CP(onesb[:], onesf, [Bc], [Bcb])
        TTo(wcT[:], sgst, bc(cst[:, 3:4, :], [128, 8, 128]), ALU.mult, [Bsgst, Bc], [Bsg])
        CP(bsrow[:], bsst, [Bbsst], [Bsg])
        SF.bufs[1] = Buf("scrf1")
        SF.bufs[3] = Buf("scrf3")
        P.op("dve", lambda e: e.memset(Sst[:], 0.0), writes=[b for r in BS for b in r])
        P.op("dve", lambda e: e.memset(Sbf[:], 0.0), writes=[b for r in BSb for b in r])
        P.op("dve", lambda e: e.memset(ghalo[:], 0.0), writes=[b for r in Bgh for b in r])
        P.op("dve", lambda e: e.memset(shalo[:], 0.0), writes=[b for r in Bsh for b in r])
        for l in range(NL):
            TS1(gsc[:, l, :], spk[:, SPO["g_a%d" % l]:SPO["g_a%d" % l] + 8], float(np.sqrt(128.0)), ALU.mult, [Bsp], [Bgsc])
            ACT(nexpa[:, l, :], spk[:, SPO["alog%d" % l]:SPO["alog%d" % l] + 8], AF.Exp, [Bsp], [Bgsc])
            TS1(nexpa[:, l, :], nexpa[:, l, :], -1.0, ALU.mult, [Bgsc], [Bgsc])

        sched = []

        def blk(W, l, r0, c0):
            return W[l, r0:r0 + D, c0:c0 + 512].rearrange("(k p) c -> p k c", p=128)

        v16 = lambda s: wbuf[:, s, :].rearrange("p (k c) -> p k c", k=16)
        v2 = lambda s: wbuf[:, s, 0:4096].rearrange("p (k c) -> p k c", k=2)
        INB = {"u": 0, "v": 1, "gb": 2, "gc": 3, "xin": 4, "q0": 5, "q1": 6, "k0": 7, "k1": 8, "v0": 9, "v1": 10, "z0": 11, "z1": 12}
        for t in range(NT):
            for l in range(NL):
                for key in ("u", "v", "gb", "gc", "xin", "q0", "k0", "v0", "z0", "q1", "k1", "v1", "z1"):
                    sched.append((key, blk(win_d, l, 0, INB[key] * 512), v16))
                for cb in range(4):
                    sched.append(("wo%d" % cb, blk(wo_d, l, 0, cb * 512), v16))
                for q in range(4):
                    for j in range(4):
                        sched.append(("w1_%d_%d" % (q, j), blk(w1_d, l, 0, q * 2048 + j * 512), v16))
                    for cb in range(4):
                        sched.append(("w2_%d_%d" % (q, cb), blk(w2_d, l, q * 2048, cb * 512), v16))
                for cb in range(4):
                    sched.append(("pg%d" % cb, blk(wpg_d, l, 0, cb * 512), v16))
                sched.append(("pp", wpp_d[l].rearrange("(k p) c -> p k c", p=128), v2))
        wstate = {"issued": 0, "next": 0}
        NBT = len(sched) // NT
        wcache = nc.dram_tensor("wcache", [NBT, 128, 8192], BF16).ap()
        Bcache = [Buf("wc%d" % i) for i in range(NBT)]
        Dst = [P.dma_tl("wst%d" % i) for i in range(NWB)]

        def WNEXT(key):
            i = wstate["next"]
            assert sched[i][0] == key, (sched[i][0], key)
            while wstate["issued"] < min(len(sched), i + NWB):
                j = wstate["issued"]
                s = j % NWB
                cid = j % NBT
                _, src, vf = sched[j]
                if j < NBT or NT == 1:
                    P.op("pool", lambda e, s=s, src=src, vf=vf: e.dma_start(out=vf(s), in_=src), writes=[Bw[s]], dma=Dw[s])
                    if NT > 1:
                        P.op("sp", lambda e, s=s, cid=cid: e.dma_start(out=wcache[cid], in_=wbuf[:, s, :]),
                             reads=[Bw[s]], writes=[Bcache[cid]], dma=Dst[s])
                else:
                    P.op("pool", lambda e, s=s, cid=cid: e.dma_start(out=wbuf[:, s, :], in_=wcache[cid]),
                         reads=[Bcache[cid]], writes=[Bw[s]], dma=Dw[s])
                wstate["issued"] += 1
            wstate["next"] += 1
            s = i % NWB
            return sched[i][2](s), Bw[s]

        grp = {}

        def G(buf):
            if buf not in grp:
                grp[buf] = [Buf(buf.name + "g0"), Buf(buf.name + "g1")]
            return grp[buf]

        def ALLB(buf):
            return [buf] + G(buf)

        pending = []

        def defer(th):
            pending.append(th)

        def tick(n=1):
            for _ in range(n):
                if pending:
                    pending.pop(0)()

        def flush():
            while pending:
                pending.pop(0)()

        def proj_fm(key, src, Bsrc, nk=16, pipelined=False):
            wv, bw = WNEXT(key)
            banks = [PS() for m in range(4)]
            if pipelined:
                for kc in range(nk):
                    def f(e, kc=kc):
                        ins = None
                        for m in range(4):
                            ins = e.matmul(psum[banks[m]][:], lhsT=wv[:, kc, m * 128:(m + 1) * 128], rhs=src[:, kc, :],
                                           start=(kc == 0), stop=(kc == nk - 1))
                        return ins
                    P.op("pe", f, [bw, Bsrc[kc]], [Bps[b] for b in banks])
                flush()
                return banks
            for m in range(4):
                b = banks[m]
                MM([(psum[b][:], [(wv[:, kc, m * 128:(m + 1) * 128], src[:, kc, :]) for kc in range(nk)])],
                   [bw] + Bsrc, [Bps[b]])
                if m == 0:
                    flush()
            return banks

        def rstd_from_ss(b, scale, n=TT):
            r, Br = SF.next()
            ACT(r[:, 0:n], psum[b][:, 0:n], AF.Ln, [Bps[b], Bcb], [Br], scale=scale, bias=cbias[:, 0:1])
            ACT(r[:, 0:n], r[:, 0:n], AF.Exp, [Br], [Br], scale=-0.5)
            return r, Br

        def rmsnorm_to(dst, Bdst, gcol, src_bufs_extra=()):
            b = PS()
            for kc in range(16):
                s, Bs_ = SB.next()
                ACT(s, hT[:, kc, :], AF.Square, [BhT[kc]], [Bs_])
                P.op("pe", lambda e, s=s, kc=kc, b=b: e.matmul(psum[b][:], lhsT=onesb[:], rhs=s, start=(kc == 0), stop=(kc == 15)),
                     [Bs_, Bcb], [Bps[b]])
            r, Br = rstd_from_ss(b, 1.0 / D)
            DUMP("rstd", r, [128, TT], F32, [Br])
            for kc in range(16):
                STT(dst[:, kc, :], hT[:, kc, :], spk[:, gcol + kc:gcol + kc + 1], r, ALU.mult, ALU.mult,
                    [BhT[kc], Bsp, Br], [Bdst[kc]])

        def group_norm_out(raw, Braw, gain_ap, Bgain, dst, Bdst_list, extra_in1=None, Bextra=()):
            s, Bs_ = SB.next()
            ACT(s, raw, AF.Square, [Braw], [Bs_])

            def partB():
                b = PS()
                MM([(psum[b][:], [(onesb[:], s)])], [Bs_, Bcb], [Bps[b]])
                r, Br = SF.next()
                ACT(r, psum[b][:], AF.Ln, [Bps[b], Bcb], [Br], scale=1.0, bias=cbias[:, 1:2])
                ACT(r, r, AF.Exp, [Br], [Br], scale=-0.5)
                STT(dst, raw, gain_ap, r, ALU.mult, ALU.mult, [Braw, Bgain, Br], Bdst_list)
            defer(partB)

        def load_x(t):
            for tc in range(4):
                j = tc % 2
                r0 = t * TT + tc * 128
                P.op("sp", lambda e, j=j, r0=r0: e.dma_start(out=stage[:, j, :], in_=x_d[r0:r0 + 128, :]),
                     writes=ByT[8 * j:8 * j + 8], dma=Dx[j])
                for g in range(4):
                    b = PS()
                    TR([(psum[b][:, k * 128:(k + 1) * 128], stage[:, j, (4 * g + k) * 128:(4 * g + k + 1) * 128], identf)
                        for k in range(4)], ByT[8 * j:8 * j + 8] + [Bc], [Bps[b]])
                    CP(hT[:, 4 * g:4 * g + 4, tc * 128:(tc + 1) * 128], psum[b][:].rearrange("p (k c) -> p k c", k=4),
                       [Bps[b]], BhT[4 * g:4 * g + 4], eng=("act" if g % 2 == 0 else "dve"))

        def store_out(t):
            b = PS()
            for kc in range(16):
                s, Bs_ = SB.next()
                ACT(s, hT[:, kc, :], AF.Square, [BhT[kc]], [Bs_])
                P.op("pe", lambda e, s=s, kc=kc, b=b: e.matmul(psum[b][:], lhsT=onesb[:], rhs=s, start=(kc == 0), stop=(kc == 15)),
                     [Bs_, Bcb], [Bps[b]])
            r, Br = rstd_from_ss(b, 1.0 / D)
            gcol = SPO["g_fin"]
            for kc in range(16):
                STT(hT[:, kc, :], hT[:, kc, :], spk[:, gcol + kc:gcol + kc + 1], r, ALU.mult, ALU.mult,
                    [BhT[kc], Bsp, Br], [BhT[kc]])
            for tc in range(4):
                j = tc % 2
                r0 = t * TT + tc * 128
                for g in range(4):
                    b = PS()
                    TR([(psum[b][:, k * 128:(k + 1) * 128], hT[:, 4 * g + k, tc * 128:(tc + 1) * 128], identf)
                        for k in range(4)], BhT[4 * g:4 * g + 4] + [Bc], [Bps[b]])
                    CP(stage[:, j, g * 512:(g + 1) * 512], psum[b][:], [Bps[b]], ByT[8 * j + 2 * g:8 * j + 2 * g + 2],
                       eng=("act" if g % 2 == 0 else "dve"))
                P.op("sp", lambda e, j=j, r0=r0: e.dma_start(out=out_d[r0:r0 + 128, :], in_=stage[:, j, :]),
                     reads=ByT[8 * j:8 * j + 8], dma=Do[j])

        def sgu(l):
            banks = proj_fm("u", xnT, BxnT, pipelined=True)
            for h in range(4):
                ACT(uT[:, h, :], psum[banks[h]][:], AF.Gelu, [Bps[banks[h]]], [BuT[h]])
            wv, bw = WNEXT("v")
            lng = spk[:, SPO["lng%d" % l]:SPO["lng%d" % l] + 512]
            lnb = spk[:, SPO["lnb%d" % l]:SPO["lnb%d" % l] + 512]
            for tc in range(4):
                b = PS()
                MM([(psum[b][:], [(xnT[:, kc, tc * 128:(tc + 1) * 128], wv[:, kc, :]) for kc in range(16)])],
                   [bw] + BxnT, [Bps[b]])
                vf, Bvf = SF.next()
                ACT(vf, psum[b][:], AF.Gelu, [Bps[b]], [Bvf])
                sq, Bsq = SF.next()
                ACT(sq, vf, AF.Square, [Bvf], [Bsq])
                v3 = vf.rearrange("p (h d) -> p h d", h=4)
                P.op("dve", lambda e, v3=v3: e.tensor_reduce(out=lnst[:, 0, :], in_=v3, axis=AX.X, op=ALU.add), [Bvf], [Blnst])
                P.op("dve", lambda e, sq=sq: e.tensor_reduce(out=lnst[:, 1, :], in_=sq.rearrange("p (h d) -> p h d", h=4), axis=AX.X, op=ALU.add), [Bsq], [Blnst])
                TS1(lnst[:, 2, :], lnst[:, 0, :], 1.0 / 128, ALU.mult, [Blnst], [Blnst])
                TTo(lnst[:, 3, :], lnst[:, 2, :], lnst[:, 2, :], ALU.mult, [Blnst], [Blnst])
                STT(lnst[:, 4, :], lnst[:, 1, :], 1.0 / 128, lnst[:, 3, :], ALU.mult, ALU.subtract, [Blnst], [Blnst])
                ACT(lnst[:, 5, :], lnst[:, 4, :], AF.Ln, [Blnst, Bcb], [Blnst], scale=1.0, bias=cbias[:, 0:1])
                ACT(lnst[:, 5, :], lnst[:, 5, :], AF.Exp, [Blnst], [Blnst], scale=-0.5)
                STT(lnst[:, 6, :], lnst[:, 2, :], -1.0, lnst[:, 5, :], ALU.mult, ALU.mult, [Blnst], [Blnst])
                for h in range(4):
                    TS(v3[:, h, :], v3[:, h, :], lnst[:, 5, h:h + 1], lnst[:, 6, h:h + 1], ALU.mult, ALU.add, [Bvf, Blnst], [Bvf])
                TTo(vf, vf, lng, ALU.mult, [Bvf, Bsp], [Bvf])
                TTo(vn[:, tc, :], vf, lnb, ALU.add, [Bvf, Bsp], [Bvn[tc]])
            for h in range(4):
                b = PS()
                groups = []
                for tc in range(4):
                    groups.append((psum[b][:, tc * 128:(tc + 1) * 128],
                                   [(vn[:, tc, h * 128:(h + 1) * 128], wcT[:, l * 4 + h, :]),
                                    (onesb[0:1, :], bsrow[0:1, (l * 4 + h) * 128:(l * 4 + h + 1) * 128])]))
                MM(groups, Bvn + [Bsg, Bcb], [Bps[b]])
                raw, Braw = SF.next()
                TTo(raw, psum[b][:], uT[:, h, :], ALU.mult, [Bps[b], BuT[h]], [Braw])
                group_norm_out(raw, Braw, gsc[:, l, h:h + 1], Bgsc, yT[:, h, :], [ByT[h]])

        def sconv(l):
            banks = proj_fm("gb", xnT, BxnT)
            for g in range(4):
                ACT(gbT[:, g, :], psum[banks[g]][:], AF.Copy, [Bps[banks[g]]], [BgbT[g]])
            banks = proj_fm("gc", xnT, BxnT)
            for g in range(4):
                ACT(gcT[:, g, :], psum[banks[g]][:], AF.Copy, [Bps[banks[g]]], [BgcT[g]])
            banks = proj_fm("xin", xnT, BxnT)
            cw = SPO["scw%d" % l]
            for g in range(4):
                z, Bz = PRE.next()
                CP(z[:, 0:2], shalo[:, l, g, :], [Bsh[l][g]], [Bz])
                TTo(z[:, 2:2 + TT], psum[banks[g]][:], gcT[:, g, :], ALU.mult, [Bps[banks[g]], BgcT[g]], [Bz])
                CP(shalo[:, l, g, :], z[:, TT:TT + 2], [Bz], [Bsh[l][g]])
                acc, Bacc = SF.next()
                TS1(acc, z[:, 0:TT], spk[:, cw + g * 3:cw + g * 3 + 1], ALU.mult, [Bz, Bsp], [Bacc])
                for j in (1, 2):
                    STT(acc, z[:, j:j + TT], spk[:, cw + g * 3 + j:cw + g * 3 + j + 1], acc, ALU.mult, ALU.add, [Bz, Bsp, Bacc], [Bacc])
                TTo(acc, acc, gbT[:, g, :], ALU.mult, [Bacc, BgbT[g]], [Bacc])
                group_norm_out(acc, Bacc, gsc[:, l, 4 + g:5 + g], Bgsc, yT[:, 4 + g, :], [ByT[4 + g]])

        def conv_silu(l, b, ci, dst, Bdst, norm):
            pr, Bpr = PRE.next()
            CP(pr[:, 0:3], ghalo[:, l, ci, :], [Bgh[l][ci]], [Bpr])
            ACT(pr[:, 3:3 + TT], psum[b][:], AF.Copy, [Bps[b]], [Bpr])
            CP(ghalo[:, l, ci, :], pr[:, TT:TT + 3], [Bpr], [Bgh[l][ci]])
            cw = SPO["gcw%d" % l] + ci * 4
            acc, Bacc = SF.next()
            TS1(acc, pr[:, 0:TT], spk[:, cw:cw + 1], ALU.mult, [Bpr, Bsp], [Bacc])
            for j in (1, 2, 3):
                STT(acc, pr[:, j:j + TT], spk[:, cw + j:cw + j + 1], acc, ALU.mult, ALU.add, [Bpr, Bsp, Bacc], [Bacc])
            if not norm:
                ACT(dst, acc, AF.Silu, [Bacc], [Bdst])
                return
            ACT(acc, acc, AF.Silu, [Bacc], [Bacc])
            s, Bs_ = SB.next()
            ACT(s, acc, AF.Square, [Bacc], [Bs_])

            def partB():
                bb = PS()
                MM([(psum[bb][:], [(onesb[:], s)])], [Bs_, Bcb], [Bps[bb]])
                r, Br = rstd_from_ss(bb, 1.0)
                TTo(dst, acc, r, ALU.mult, [Bacc, Br], [Bdst])
            defer(partB)

        def gdn_scalars(l):
            b = PS()
            groups = []
            for tc in range(4):
                groups.append((psum[b][:, tc * 16:(tc + 1) * 16],
                               [(xnT[:, kc, tc * 128:(tc + 1) * 128], wab[:, l, kc, :]) for kc in range(16)]))
            MM(groups, BxnT + [Bwab], [Bps[b]])
            ab = psum[b][:, 0:64].rearrange("p (c k) -> p c k", c=4)
            dtb = spk[:, SPO["dtb%d" % l]:SPO["dtb%d" % l] + 8]
            TTo(sm[:, T1], ab[:, :, 0:8], bc(dtb.unsqueeze(1), [128, 4, 8]), ALU.add, [Bps[b], Bsp], [Bsm[T1]])
            ACT(sm[:, T1], sm[:, T1], AF.Exp, [Bsm[T1]], [Bsm[T1]])
            ACT(sm[:, T1], sm[:, T1], AF.Ln, [Bsm[T1], Bcb], [Bsm[T1]], scale=1.0, bias=cbias[:, 2:3])
            TTo(sm[:, G_], sm[:, T1], bc(nexpa[:, l, :].unsqueeze(1), [128, 4, 8]), ALU.mult, [Bsm[T1], Bgsc], [Bsm[G_]])
            ACT(sm[:, T2], ab[:, :, 8:16], AF.Exp, [Bps[b], Bsm[T1]], [Bsm[T2]], scale=-1.0)
            TS1(sm[:, T2], sm[:, T2], 1.0, ALU.add, [Bsm[T2]], [Bsm[T2]])
            P.op("dve", lambda e: e.reciprocal(out=sm[:, BETA], in_=sm[:, T2]), [Bsm[T2]], [Bsm[BETA]])
            TS1(sm[:, NEGB], sm[:, BETA], -1.0, ALU.mult, [Bsm[BETA]], [Bsm[NEGB]])
            b2 = PS()
            groups = []
            for tc in range(4):
                groups.append((psum[b2][:, tc * 16:tc * 16 + 8], [(Uf, sm[:, G_, tc, :])]))
                groups.append((psum[b2][:, tc * 16 + 8:tc * 16 + 16], [(onesf, sm[:, G_, tc, :])]))
            MM(groups, [Bsm[G_], Bc], [Bps[b2]])
            CP(smps[:], psum[b2][:, 0:64].rearrange("p (c k) -> p c k", c=4), [Bps[b2]], [Bsmps])
            ACT(sm[:, EGC], smps[:, :, 0:8], AF.Exp, [Bsmps], [Bsm[EGC]])
            ACT(sm[:, EGL], smps[:, :, 8:16], AF.Exp, [Bsmps], [Bsm[EGL]])
            TTo(sm[:, T3], smps[:, :, 8:16], smps[:, :, 0:8], ALU.subtract, [Bsmps], [Bsm[T3]])
            ACT(sm[:, ETAIL], sm[:, T3], AF.Exp, [Bsm[T3]], [Bsm[ETAIL]])
            TTo(sm[:, BEGA], sm[:, BETA], sm[:, EGC], ALU.mult, [Bsm[BETA], Bsm[EGC]], [Bsm[BEGA]])

        def gdn_half(l, hh):
            h0 = 4 * hh
            for key, dst, Bdst, cbase, norm in (("q%d" % hh, qT, BqT, 0, True), ("k%d" % hh, kT, BkT, 8, True),
                                                ("v%d" % hh, vT, BvT, 16, False)):
                banks = proj_fm(key, xnT, BxnT)
                for hi in range(4):
                    conv_silu(l, banks[hi], cbase + h0 + hi, dst[:, hi, :], Bdst[hi], norm)
            banks = proj_fm("z%d" % hh, xnT, BxnT)
            for hi in range(4):
                ACT(zsT[:, hi, :], psum[banks[hi]][:], AF.Silu, [Bps[banks[hi]]], [BzsT[hi]])
            flush()
            hs = slice(h0, h0 + 4)
            v4 = lambda b: psum[b][:].rearrange("p (h c) -> p h c", h=4)
            v4b = lambda b: psb[b][:, 0:512].rearrange("p (h c) -> p h c", h=4)
            for c in range(4):
                cs = slice(c * 128, (c + 1) * 128)
                sc = lambda idx: bc(sm[:, idx, c, hs].unsqueeze(2), [128, 4, 128])
                r4 = lambda ap: ap.rearrange("p (h c) -> p h c", h=4)
                GU_, BGU = SF.next(); GU = r4(GU_)
                EROW_, BEROW = SF.next(); EROW = r4(EROW_)
                tmpf_, Btmpf = SF.next(); tmpf = r4(tmpf_)
                ESL_, BESL = SF.next(); ESL = r4(ESL_)
                TTo(GU, bc(cst[:, 3:4, :], [128, 4, 128]), sc(G_), ALU.mult, [Bc, Bsm[G_]], [BGU])
                b = PS()
                MM([(psum[b][:, hi * 128:(hi + 1) * 128],
                     [(GU[:, hi, :], onesf), (negonesf, GU[:, hi, :]), (identf, maskneg)]) for hi in range(4)],
                   [BGU, Bc], [Bps[b]])
                ACT(Em[:], v4(b), AF.Exp, [Bps[b]], [BEm])
                tick()
                b = PS()
                MM([(psum[b][:, hi * 128:(hi + 1) * 128], [(onesf, GU[:, hi, :])]) for hi in range(4)], [BGU, Bc], [Bps[b]])
                ACT(EROW, v4(b), AF.Exp, [Bps[b]], [BEROW])
                TTo(ESL, Em[:], bc(cst[:, 5:6, :], [128, 4, 128]), ALU.mult, [BEm, Bc], [BESL])
                STT(qeT[:], qT[:, :, cs], float(128.0 ** -0.5), EROW, ALU.mult, ALU.mult, BqT + [BEROW], [BqeT])
                b = PS()
                TR([(psb[b][:, hi * 128:(hi + 1) * 128], kT[:, hi, cs], identb[:]) for hi in range(4)] +
                   [(psb[b][:, 512 + hi * 128:512 + (hi + 1) * 128], vT[:, hi, cs], identb[:]) for hi in range(4)],
                   BkT + BvT + [Bcb], [Bps[b]])
                kk = psb[b][:, 0:512].rearrange("p (h c) -> p h c", h=4)
                vv = psb[b][:, 512:1024].rearrange("p (h c) -> p h c", h=4)
                TTo(kbe[:], kk, sc(BEGA), ALU.mult, [Bps[b], Bsm[BEGA]], [Bkbe])
                TTo(ktl[:], kk, sc(ETAIL), ALU.mult, [Bps[b], Bsm[ETAIL]], [Bktl])
                TTo(vb[:], vv, sc(BETA), ALU.mult, [Bps[b], Bsm[BETA]], [Bvb])
                b = PS()
                MM([(psum[b][:, hi * 128:(hi + 1) * 128], [(kT[:, hi, cs], kT[:, hi, cs])]) for hi in range(4)], BkT, [Bps[b]])
                TTo(tmpf, v4(b), sc(NEGB), ALU.mult, [Bps[b], Bsm[NEGB]], [Btmpf])
                TTo(Wm[:], tmpf, ESL, ALU.mult, [Btmpf, BESL], ALLB(BWm))
                b = PS()
                MM([(psum[b][:, hi * 128:(hi + 1) * 128], [(qT[:, hi, cs], kT[:, hi, cs])]) for hi in range(4)], BqT + BkT, [Bps[b]])
                STT(intra[:], v4(b), float(128.0 ** -0.5), Em[:], ALU.mult, ALU.mult, [Bps[b], BEm], [Bintra])
                b = PS()
                TR([(psum[b][:, hi * 128:(hi + 1) * 128], Wm[:, hi, :], identf) for hi in range(4)], ALLB(BWm) + [Bc], [Bps[b]])
                b2 = PS()
                TR([(psb[b2][:, hi * 128:(hi + 1) * 128], intra[:, hi, :], identb[:]) for hi in range(4)], [Bintra, Bcb], [Bps[b2]])
                P0, BP0 = Pm[0]
                ACT(P0[:], v4(b), AF.Copy, [Bps[b]], ALLB(BP0))
                ACT(intraT[:], psb[b2][:, 0:512].rearrange("p (h c) -> p h c", h=4), AF.Copy, [Bps[b2]], [BintraT])
                R0, BR0 = Rm[0]
                TTo(R0[:], P0[:], bc(cst[:, 0:1, :], [128, 4, 128]), ALU.add, ALLB(BP0) + [Bc], ALLB(BR0))
                Pc, BPc = P0, BP0
                PTc, BPTc = Wm, BWm
                Rc, BRc = R0, BR0
                v2 = lambda b: psum[b][:, 0:256].rearrange("p (h c) -> p h c", h=2)
                for lev in range(1, 7):
                    PTn, BPTn = PTm[lev % 2]
                    Pn, BPn = Pm[lev % 2]
                    Rn, BRn = Rm[lev % 2]
                    for g in range(2):
                        gs = slice(2 * g, 2 * g + 2)
                        b = PS()
                        MM([(psum[b][:, k * 128:(k + 1) * 128], [(Pc[:, 2 * g + k, :], PTc[:, 2 * g + k, :])]) for k in range(2)],
                           [G(BPc)[g], G(BPTc)[g]], [Bps[b]])
                        ACT(PTn[:, gs, :], v2(b), AF.Copy, [Bps[b]], [G(BPTn)[g]])
                        if lev < 6:
                            b2 = PS()
                            MM([(psum[b2][:, k * 128:(k + 1) * 128], [(PTc[:, 2 * g + k, :], Pc[:, 2 * g + k, :])]) for k in range(2)],
                               [G(BPc)[g], G(BPTc)[g]], [Bps[b2]])
                            CP(Pn[:, gs, :], v2(b2), [Bps[b2]], [G(BPn)[g]])
                    for g in range(2):
                        gs = slice(2 * g, 2 * g + 2)
                        b3 = PS()
                        MM([(psum[b3][:, k * 128:(k + 1) * 128], [(PTn[:, 2 * g + k, :], Rc[:, 2 * g + k, :])]) for k in range(2)],
                           [G(BPTn)[g], G(BRc)[g]], [Bps[b3]])
                        TTo(Rn[:, gs, :], v2(b3), Rc[:, gs, :], ALU.add, [Bps[b3], G(BRc)[g]], [G(BRn)[g]])
                    PTc, BPTc = PTn, BPTn
                    if lev < 6:
                        Pc, BPc = Pn, BPn
                    Rc, BRc = Rn, BRn
                ACT(Rbf[:], Rc[:], AF.Copy, ALLB(BRc), [BRbf])
                b = PS()
                MM([(psum[b][:, hi * 128:(hi + 1) * 128], [(kbe[:, hi, :], Rbf[:, hi, :])]) for hi in range(4)], [Bkbe, BRbf], [Bps[b]])
                ACT(nkcd[:], v4(b), AF.Copy, [Bps[b]], [Bnkcd], scale=-1.0)
                b = PS()
                MM([(psum[b][:, hi * 128:(hi + 1) * 128],
                     [(Rbf[:, hi, :], vb[:, hi, :]), (nkcd[:, hi, :], Sbf[:, l, h0 + hi, :])]) for hi in range(4)],
                   [BRbf, Bvb, Bnkcd, BSb[l][hh]], [Bps[b]])
                ACT(vnew[:], v4(b), AF.Copy, [Bps[b]], [Bvnew])
                bo = PS()
                MM([(psum[bo][:, hi * 128:(hi + 1) * 128],
                     [(Sbf[:, l, h0 + hi, :], qeT[:, hi, :]), (vnew[:, hi, :], intraT[:, hi, :])]) for hi in range(4)],
                   [BSb[l][hh], BqeT, Bvnew, BintraT], [Bps[bo]])
                bs = PS()
                MM([(psum[bs][:, hi * 128:(hi + 1) * 128], [(ktl[:, hi, :], vnew[:, hi, :])]) for hi in range(4)],
                   [Bktl, Bvnew], [Bps[bs]])
                TTo(Sst[:, l, hs, :], Sst[:, l, hs, :], sc(EGL), ALU.mult, [BS[l][hh], Bsm[EGL]], [BS[l][hh]])
                TTo(Sst[:, l, hs, :], Sst[:, l, hs, :], v4(bs), ALU.add, [BS[l][hh], Bps[bs]], [BS[l][hh]])
                ACT(Sbf[:, l, hs, :], Sst[:, l, hs, :], AF.Copy, [BS[l][hh]], [BSb[l][hh]])
                osb, Bosb = SF.next()
                ACT(osb, psum[bo][:], AF.Copy, [Bps[bo]], [Bosb])
                s, Bs_ = SB.next()
                ACT(s, osb, AF.Square, [Bosb], [Bs_])

                def partB(osb=osb, Bosb=Bosb, s=s, Bs_=Bs_, cs=cs):
                    bb = PS()
                    MM([(psum[bb][:], [(onesb[:], s)])], [Bs_, Bcb], [Bps[bb]])
                    r, Br = SF.next()
                    ACT(r, psum[bb][:], AF.Ln, [Bps[bb], Bcb], [Br], scale=1.0 / 128, bias=cbias[:, 0:1])
                    ACT(r, r, AF.Exp, [Br], [Br], scale=-0.5)
                    TTo(osb, osb, r, ALU.mult, [Bosb, Br], [Bosb])
                    gg = SPO["g_gdn%d" % l]
                    STT(yT[:, 8 + h0:12 + h0, cs], osb.rearrange("p (h c) -> p h c", h=4), spk[:, gg:gg + 1], zsT[:, :, cs],
                        ALU.mult, ALU.mult, [Bosb, Bsp] + BzsT, ByT[8 + h0:12 + h0])
                defer(partB)
            flush()

        def out_proj(l):
            flush()
            for cb in range(4):
                banks = proj_fm("wo%d" % cb, yT, ByT)
                for m in range(4):
                    kc = cb * 4 + m
                    TTo(hT[:, kc, :], hT[:, kc, :], psum[banks[m]][:], ALU.add, [BhT[kc], Bps[banks[m]]], [BhT[kc]])

        def ffn(l):
            rmsnorm_to(xnT, BxnT, SPO["g_ffn%d" % l])
            for q in range(4):
                for j in range(4):
                    banks = proj_fm("w1_%d_%d" % (q, j), xnT, BxnT, pipelined=(q == 0 and j == 0))
                    for m in range(4):
                        kc = j * 4 + m
                        t_, Bt_ = SF.next()
                        ACT(t_, psum[banks[m]][:], AF.Relu, [Bps[banks[m]]], [Bt_])
                        TTo(yT[:, kc, :], t_, psum[banks[m]][:], ALU.mult, [Bt_, Bps[banks[m]]], [ByT[kc]])
                for cb in range(4):
                    banks = proj_fm("w2_%d_%d" % (q, cb), yT, ByT)
                    for m in range(4):
                        kc = cb * 4 + m
                        TTo(hT[:, kc, :], hT[:, kc, :], psum[banks[m]][:], ALU.add, [BhT[kc], Bps[banks[m]]], [BhT[kc]])

        def ple(t, l):
            rmsnorm_to(xnT, BxnT, SPO["g_ple%d" % l])
            r0 = t * TT
            b = PS()
            b2 = PS()
            for half in range(2):
                pg_, Bpg = SF.next()
                pg = pg_.rearrange("p (c f) -> p c f", c=2)
                rr = r0 + half * 256
                P.op("sp", lambda e, pg=pg, rr=rr: e.dma_start(out=pg, in_=p_d[l, rr:rr + 256, :].rearrange("(c p) f -> p c f", p=128)),
                     writes=[Bpg], dma=Dp[half])
                TR([(psum[b][:, (2 * half + c) * 128:(2 * half + c + 1) * 128], pg[:, c, 0:128], identf) for c in range(2)], [Bpg, Bc], [Bps[b]])
                TR([(psum[b2][:, (2 * half + c) * 128:(2 * half + c + 1) * 128], pg[:, c, 128:256], identf) for c in range(2)], [Bpg, Bc], [Bps[b2]])
            ACT(pT[:, 0, :], psum[b][:], AF.Copy, [Bps[b]], [BpT])
            ACT(pT[:, 1, :], psum[b2][:], AF.Copy, [Bps[b2]], [BpT])
            gates = []
            for cb in range(4):
                banks = proj_fm("pg%d" % cb, xnT, BxnT, pipelined=(cb == 0))
                for m in range(4):
                    g_, Bg_ = (yT[:, cb * 4 + m, :], ByT[cb * 4 + m])
                    ACT(g_, psum[banks[m]][:], AF.Sigmoid, [Bps[banks[m]]], [Bg_])
            wv, bw = WNEXT("pp")
            for kc in range(16):
                b = PS()
                MM([(psum[b][:], [(wv[:, k, kc * 128:(kc + 1) * 128], pT[:, k, :]) for k in range(2)])], [bw, BpT], [Bps[b]])
                t_, Bt_ = SF.next()
                TTo(t_, psum[b][:], yT[:, kc, :], ALU.mult, [Bps[b], ByT[kc]], [Bt_])
                TTo(hT[:, kc, :], hT[:, kc, :], t_, ALU.add, [BhT[kc], Bt_], [BhT[kc]])

        for t in range(NT):
            load_x(t)
            DUMP("h0", hT[:], [128, 16, TT], F32, BhT)
            for l in range(NL):
                rmsnorm_to(xnT, BxnT, SPO["g_mix%d" % l])
                DUMP("xn@%d" % l, xnT[:], [128, 16, TT], BF16, BxnT)
                sgu(l)
                sconv(l)
                gdn_scalars(l)
                DUMP("sm@%d" % l, sm[:], [128, 12, 4, 8], F32, Bsm)
                gdn_half(l, 0)
                gdn_half(l, 1)
                DUMP("y@%d" % l, yT[:], [128, 16, TT], BF16, ByT)
                out_proj(l)
                DUMP("h1@%d" % l, hT[:], [128, 16, TT], F32, BhT)
                ffn(l)
                DUMP("h2@%d" % l, hT[:], [128, 16, TT], F32, BhT)
                ple(t, l)
                DUMP("h3@%d" % l, hT[:], [128, 16, TT], F32, BhT)
            store_out(t)
        assert wstate["next"] == len(sched), (wstate, len(sched))
        P.wait_all("sp", ByT)
        print("ops recorded:", P.nops)
        P.emit()
    return nc


def host_pack(inp, NL=2):
    SPO, NSP = sp_layout(NL)
    sp = np.zeros((128, NSP), np.float32)
    fm = lambda v: np.ascontiguousarray(v.reshape(-1, 128).T)
    for l in range(NL):
        sp[:, SPO["g_mix%d" % l]:][:, :16] = fm(inp["norm_mix"][l])
        sp[:, SPO["g_ffn%d" % l]:][:, :16] = fm(inp["norm_ffn"][l])
        sp[:, SPO["g_ple%d" % l]:][:, :16] = fm(inp["norm_ple"][l])
        sp[:, SPO["g_a%d" % l]:][:, :4] = fm(inp["out_norm_a"][l])
        sp[:, SPO["g_b%d" % l]:][:, :4] = fm(inp["out_norm_b"][l])
        sp[:, SPO["g_gdn%d" % l]:][:, :1] = fm(inp["gdn_norm"][l])
        sp[:, SPO["scw%d" % l]:][:, :12] = inp["sc_conv"][l].reshape(3, 4, 128).transpose(2, 1, 0).reshape(128, 12)
        sp[:, SPO["gcw%d" % l]:][:, :96] = inp["gdn_conv"][l].reshape(4, 24, 128).transpose(2, 1, 0).reshape(128, 96)
        sp[:, SPO["alog%d" % l]:][:, :8] = np.broadcast_to(inp["gdn_a_log"][l], (128, 8))
        sp[:, SPO["dtb%d" % l]:][:, :8] = np.broadcast_to(inp["gdn_dt_bias"][l], (128, 8))
        sp[:, SPO["lng%d" % l]:][:, :512] = np.broadcast_to(inp["sg_ln_g"][l], (128, 512))
        sp[:, SPO["lnb%d" % l]:][:, :512] = np.broadcast_to(inp["sg_ln_b"][l], (128, 512))
    sp[:, SPO["g_fin"]:][:, :16] = fm(inp["norm_final"])
    sgwT = np.ascontiguousarray(inp["sg_w"].transpose(3, 0, 1, 2).reshape(128, 8, 128))
    sgb = np.ascontiguousarray(inp["sg_b"].reshape(1, 1024))
    wab = np.ascontiguousarray(inp["w_in"][:, :, 6656:6672])
    return sp, sgwT, sgb, wab


CORES = [0, 2, 4, 6]


def kernel(**inp):
    inp = {k: np.asarray(v) for k, v in inp.items()}
    sp, sgwT, sgb, wab = host_pack(inp)
    nc = build()
    common = {"w_in": inp["w_in"], "w_ab": wab, "w_o": inp["w_o"], "w_ff1": inp["w_ff1"], "w_ff2": inp["w_ff2"],
              "w_ple_gate": inp["w_ple_gate"], "w_ple_proj": inp["w_ple_proj"], "sp": sp, "sgwT": sgwT, "sgb": sgb}
    in_maps = []
    for b in range(4):
        m = dict(common)
        m["x"] = np.ascontiguousarray(inp["x"][b])
        m["p"] = np.ascontiguousarray(inp["p"][:, b])
        in_maps.append(m)
    res = run_bass_kernel_spmd(nc, in_maps, core_ids=CORES)
    return np.stack([np.asarray(res.results[b]["out"]) for b in range(4)], axis=0).astype(np.float32)
```
